# Optimizing a Trainium2 kernel written in Bass

```python
import jax, jax.numpy as jnp
from jax import lax
import numpy as np

D_MODEL = 1024
BATCH = 8
SEQ = 4096
DEPTH = 4

N_MIXERS = 3
EPS = 1e-6
NEG = -1e30

ATT_HEADS = 8
ATT_HEAD_DIM = D_MODEL // ATT_HEADS
ATT_WIDTH = ATT_HEADS * ATT_HEAD_DIM
MOBA_BLOCK = 256
MOBA_TOPK = 3
MOBA_QCHUNK = 16

CONV_CHANNELS = D_MODEL
CONV_KERNEL = 31

LRU_WIDTH = 1280
LRU_HEADS = 10
LRU_HEAD_DIM = LRU_WIDTH // LRU_HEADS
LRU_CONV = 4
LRU_C = 8.0

kernel_name = 'hybrid_moba_conformer_rglru_trunk'


def rmsnorm(x, g):
    xf = x.astype(jnp.float32)
    y = xf * lax.rsqrt(jnp.mean(xf * xf, axis=-1, keepdims=True) + EPS)
    return (y * g).astype(x.dtype)


def layernorm(x, g, b):
    xf = x.astype(jnp.float32)
    mu = jnp.mean(xf, axis=-1, keepdims=True)
    var = jnp.mean(jnp.square(xf - mu), axis=-1, keepdims=True)
    return ((xf - mu) * lax.rsqrt(var + EPS) * g + b).astype(x.dtype)


def causal_depthwise_conv(x, w, b):
    width = w.shape[0]
    y = lax.conv_general_dilated(
        x, w[:, None, :].astype(x.dtype), window_strides=(1,), padding=[(width - 1, 0)],
        dimension_numbers=('NWC', 'WIO', 'NWC'), feature_group_count=x.shape[-1])
    return y + b


def moba_mixer(u, w_in, w_out):
    bsz, seq, _ = u.shape
    q, k, v, gate = jnp.split(u @ w_in, 4, axis=-1)
    nb = -(-seq // MOBA_BLOCK)
    s_pad = nb * MOBA_BLOCK

    def heads(t):
        t = jnp.pad(t, ((0, 0), (0, s_pad - seq), (0, 0)))
        return t.reshape(bsz, s_pad, ATT_HEADS, ATT_HEAD_DIM).transpose(0, 2, 1, 3)

    q = heads(q) * (ATT_HEAD_DIM ** -0.5)
    k, v = heads(k), heads(v)
    kb = k.reshape(bsz, ATT_HEADS, nb, MOBA_BLOCK, ATT_HEAD_DIM)
    vb = v.reshape(bsz, ATT_HEADS, nb, MOBA_BLOCK, ATT_HEAD_DIM)

    k_mean = jnp.mean(kb.astype(jnp.float32), axis=3)
    scores = jnp.einsum('bhsd,bhnd->bhsn', q.astype(jnp.float32), k_mean)
    q_blk = jnp.arange(s_pad) // MOBA_BLOCK
    past = jnp.arange(nb)[None, :] < q_blk[:, None]
    scores = jnp.where(past, scores, NEG)
    n_sel = min(MOBA_TOPK, nb)
    _, sel = lax.top_k(scores, n_sel)

    n_chunks = s_pad // MOBA_QCHUNK
    q_ch = q.reshape(bsz, ATT_HEADS, n_chunks, MOBA_QCHUNK, ATT_HEAD_DIM).transpose(2, 0, 1, 3, 4)
    sel_ch = sel.reshape(bsz, ATT_HEADS, n_chunks, MOBA_QCHUNK, n_sel).transpose(2, 0, 1, 3, 4)
    gather_blocks = jax.vmap(jax.vmap(lambda blocks, idx: blocks[idx]))

    def attend_chunk(args):
        c, q_c, sel_c = args
        start = c * MOBA_QCHUNK
        blk = start // MOBA_BLOCK
        pos_q = start + jnp.arange(MOBA_QCHUNK)
        k_sel = gather_blocks(kb, sel_c)
        v_sel = gather_blocks(vb, sel_c)
        s_sel = jnp.einsum('bhqd,bhqnkd->bhqnk', q_c, k_sel).astype(jnp.float32)
        valid = (jnp.arange(n_sel) < blk)[:, None]
        s_sel = jnp.where(valid, s_sel, NEG).reshape(bsz, ATT_HEADS, MOBA_QCHUNK, n_sel * MOBA_BLOCK)
        k_own = lax.dynamic_index_in_dim(kb, blk, axis=2, keepdims=False)
        v_own = lax.dynamic_index_in_dim(vb, blk, axis=2, keepdims=False)
        s_own = jnp.einsum('bhqd,bhkd->bhqk', q_c, k_own).astype(jnp.float32)
        pos_k = blk * MOBA_BLOCK + jnp.arange(MOBA_BLOCK)
        s_own = jnp.where(pos_k[None, :] <= pos_q[:, None], s_own, NEG)
        p = jax.nn.softmax(jnp.concatenate([s_sel, s_own], axis=-1), axis=-1)
        p_sel = p[..., :n_sel * MOBA_BLOCK].reshape(bsz, ATT_HEADS, MOBA_QCHUNK, n_sel, MOBA_BLOCK)
        p_own = p[..., n_sel * MOBA_BLOCK:]
        o = (jnp.einsum('bhqnk,bhqnkd->bhqd', p_sel.astype(v.dtype), v_sel)
             + jnp.einsum('bhqk,bhkd->bhqd', p_own.astype(v.dtype), v_own))
        return o

    o = lax.map(attend_chunk, (jnp.arange(n_chunks), q_ch, sel_ch))
    o = o.transpose(1, 0, 3, 2, 4).reshape(bsz, s_pad, ATT_WIDTH)[:, :seq]
    return (o * jax.nn.silu(gate)) @ w_out


def conformer_conv_mixer(u, w_in, conv_w, conv_b, ln_g, ln_b, w_out):
    a, b, gate = jnp.split(u @ w_in, 3, axis=-1)
    y = a * jax.nn.sigmoid(b)
    y = causal_depthwise_conv(y, conv_w, conv_b)
    y = jax.nn.silu(layernorm(y, ln_g, ln_b))
    return (y * jax.nn.silu(gate)) @ w_out


def _linear_recurrence(c1, c2):
    a1, b1 = c1
    a2, b2 = c2
    return a1 * a2, a2 * b1 + b2


def rglru_mixer(u, w_in, conv_w, conv_b, w_rg, b_rg, w_ig, b_ig, lam, w_out):
    bsz, seq, _ = u.shape
    xb, gate = jnp.split(u @ w_in, 2, axis=-1)
    xb = causal_depthwise_conv(xb, conv_w, conv_b)
    xf = xb.astype(jnp.float32)
    xh = xf.reshape(bsz, seq, LRU_HEADS, LRU_HEAD_DIM)
    r = jax.nn.sigmoid(jnp.einsum('bshi,hij->bshj', xh, w_rg.astype(jnp.float32)).reshape(bsz, seq, LRU_WIDTH) + b_rg)
    i = jax.nn.sigmoid(jnp.einsum('bshi,hij->bshj', xh, w_ig.astype(jnp.float32)).reshape(bsz, seq, LRU_WIDTH) + b_ig)
    log_a = -LRU_C * r * jax.nn.softplus(-lam.astype(jnp.float32))
    a = jnp.exp(log_a)
    bterm = jnp.sqrt(-jnp.expm1(2.0 * log_a)) * (i * xf)
    _, h = lax.associative_scan(_linear_recurrence, (a, bterm), axis=1)
    y = h.astype(u.dtype) * jax.nn.silu(gate)
    return y @ w_out


def _normal(key, shape, scale):
    return jax.random.normal(key, shape, jnp.float32) * scale


def _attn_params(key, p):
    k = jax.random.split(key, 3)
    return {
        p + 'norm_g': 1.0 + _normal(k[0], (D_MODEL,), 0.02),
        p + 'w_in': _normal(k[1], (D_MODEL, 4 * ATT_WIDTH), D_MODEL ** -0.5),
        p + 'w_out': _normal(k[2], (ATT_WIDTH, D_MODEL), ATT_WIDTH ** -0.5),
    }


def _conv_params(key, p):
    k = jax.random.split(key, 7)
    return {
        p + 'norm_g': 1.0 + _normal(k[0], (D_MODEL,), 0.02),
        p + 'w_in': _normal(k[1], (D_MODEL, 3 * CONV_CHANNELS), D_MODEL ** -0.5),
        p + 'conv_w': _normal(k[2], (CONV_KERNEL, CONV_CHANNELS), CONV_KERNEL ** -0.5),
        p + 'conv_b': _normal(k[3], (CONV_CHANNELS,), 0.02),
        p + 'ln_g': 1.0 + _normal(k[4], (CONV_CHANNELS,), 0.02),
        p + 'ln_b': _normal(k[5], (CONV_CHANNELS,), 0.02),
        p + 'w_out': _normal(k[6], (CONV_CHANNELS, D_MODEL), CONV_CHANNELS ** -0.5),
    }


def _lru_params(key, p):
    k = jax.random.split(key, 10)
    a_c = jax.random.uniform(k[8], (LRU_WIDTH,), jnp.float32, 0.9, 0.999)
    a_base = a_c ** (1.0 / LRU_C)
    return {
        p + 'norm_g': 1.0 + _normal(k[0], (D_MODEL,), 0.02),
        p + 'w_in': _normal(k[1], (D_MODEL, 2 * LRU_WIDTH), D_MODEL ** -0.5),
        p + 'conv_w': _normal(k[2], (LRU_CONV, LRU_WIDTH), LRU_CONV ** -0.5),
        p + 'conv_b': _normal(k[3], (LRU_WIDTH,), 0.02),
        p + 'w_rg': _normal(k[4], (LRU_HEADS, LRU_HEAD_DIM, LRU_HEAD_DIM), LRU_HEAD_DIM ** -0.5),
        p + 'b_rg': _normal(k[5], (LRU_WIDTH,), 0.1),
        p + 'w_ig': _normal(k[6], (LRU_HEADS, LRU_HEAD_DIM, LRU_HEAD_DIM), LRU_HEAD_DIM ** -0.5),
        p + 'b_ig': _normal(k[7], (LRU_WIDTH,), 0.1),
        p + 'lam': jnp.log(a_base) - jnp.log1p(-a_base),
        p + 'w_out': _normal(k[9], (LRU_WIDTH, D_MODEL), LRU_WIDTH ** -0.5),
    }


def setup_inputs(seed: int = 0) -> dict:
    key = jax.random.key(seed)
    keys = jax.random.split(key, DEPTH + 2)
    builders = (_attn_params, _conv_params, _lru_params)
    params = {'x': jax.random.normal(keys[0], (BATCH, SEQ, D_MODEL), jnp.float32)}
    for i in range(DEPTH):
        params.update(builders[i % N_MIXERS](keys[i + 1], 'l%d_' % i))
    params['final_g'] = 1.0 + _normal(keys[DEPTH + 1], (D_MODEL,), 0.02)
    return params


def reference(x, l0_norm_g, l0_w_in, l0_w_out,
              l1_norm_g, l1_w_in, l1_conv_w, l1_conv_b, l1_ln_g, l1_ln_b, l1_w_out,
              l2_norm_g, l2_w_in, l2_conv_w, l2_conv_b, l2_w_rg, l2_b_rg, l2_w_ig, l2_b_ig, l2_lam, l2_w_out,
              l3_norm_g, l3_w_in, l3_w_out,
              final_g):
    layers = (
        (l0_norm_g, (l0_w_in, l0_w_out)),
        (l1_norm_g, (l1_w_in, l1_conv_w, l1_conv_b, l1_ln_g, l1_ln_b, l1_w_out)),
        (l2_norm_g, (l2_w_in, l2_conv_w, l2_conv_b, l2_w_rg, l2_b_rg, l2_w_ig, l2_b_ig, l2_lam, l2_w_out)),
        (l3_norm_g, (l3_w_in, l3_w_out)),
    )
    mixers = (moba_mixer, conformer_conv_mixer, rglru_mixer)
    h = x
    for i in range(DEPTH):
        g, p = layers[i]
        h = h + mixers[i % N_MIXERS](rmsnorm(h, g), *p)
    return rmsnorm(h, final_g)
```

```python
import contextlib
import numpy as np
import concourse.bass as bass
import concourse.mybir as mybir
from concourse.bass_utils import run_bass_kernel_spmd

F32 = mybir.dt.float32
BF16 = mybir.dt.bfloat16
AF = mybir.ActivationFunctionType
ALU = mybir.AluOpType

S = 4096
D = 1024
EPS = 1e-6
NCORES = 8
NBLK = 16
BLK = 256
BIG = 30000.0
LW = 1280
NCONST = 128 + 2048 + 512 + 512 + 2048

ALL_INPUTS = (
    "x", "l0_norm_g", "l0_w_in", "l0_w_out",
    "l1_norm_g", "l1_w_in", "l1_conv_w", "l1_conv_b", "l1_ln_g", "l1_ln_b", "l1_w_out",
    "l2_norm_g", "l2_w_in", "l2_conv_w", "l2_conv_b", "l2_w_rg", "l2_b_rg", "l2_w_ig", "l2_b_ig", "l2_lam", "l2_w_out",
    "l3_norm_g", "l3_w_in", "l3_w_out", "final_g",
)

HALF_DIAG = False
MODE = "fused"


class SemRec:
    def __init__(self, sem):
        self.sem = sem
        self.cnt = 0


class Buf:
    def __init__(self, name):
        self.name = name
        self.w = {}
        self.r = {}
        self.ds = None


class Eng:
    def __init__(self, nc, name, h, same_sync):
        self.name = name
        self.h = h
        self.sem = nc.alloc_semaphore(name="e_" + name)
        self.cnt = 0
        self.seen = {}
        self.same_sync = same_sync


def _add(d, tok):
    k = id(tok[0])
    if k not in d or d[k][1] < tok[1]:
        d[k] = tok


class Trk:
    def __init__(self, nc):
        self.nc = nc
        self.pe = Eng(nc, "pe", nc.tensor, False)
        self.act = Eng(nc, "act", nc.scalar, True)
        self.dve = Eng(nc, "dve", nc.vector, True)
        self.pool = Eng(nc, "pool", nc.gpsimd, True)
        self.sp = Eng(nc, "sp", nc.sync, False)
        self.engs = [self.pe, self.act, self.dve, self.pool, self.sp]
        self.free_ds = []
        self.all_ds = []
        self.scope_bufs = []
        self.uid = 0

    def buf(self, name):
        b = Buf(name)
        if self.scope_bufs:
            self.scope_bufs[-1].append(b)
        return b

    def _wait(self, eng, deps):
        for (s, v) in deps.values():
            k = id(s)
            if eng.seen.get(k, 0) >= v:
                continue
            if s is eng.sem and v > eng.cnt:
                continue
            eng.h.wait_ge(s, v)
            eng.seen[k] = v

    def op(self, eng, fn, reads=(), writes=(), inc=True):
        deps = {}
        for b in reads:
            for t in b.w.values():
                if t[0] is eng.sem and not eng.same_sync:
                    continue
                _add(deps, t)
        for b in writes:
            for t in b.w.values():
                if t[0] is not eng.sem or eng.same_sync:
                    _add(deps, t)
            for t in b.r.values():
                if t[0] is not eng.sem or eng.same_sync:
                    _add(deps, t)
        self._wait(eng, deps)
        ins = fn(eng.h)
        if inc:
            eng.cnt += 1
            ins.then_inc(eng.sem, 1)
            tok = (eng.sem, eng.cnt)
        else:
            tok = (eng.sem, eng.cnt + 1)
        for b in reads:
            _add(b.r, tok)
        for b in writes:
            _add(b.w, tok)
        return ins

    def dma(self, q, out_ap, in_ap, src, dst, sembuf, **kw):
        srcs = list(src) if isinstance(src, (list, tuple)) else [src]
        deps = {}
        for sb_ in srcs:
            for t in sb_.w.values():
                _add(deps, t)
        for t in dst.w.values():
            _add(deps, t)
        for t in dst.r.values():
            _add(deps, t)
        self._wait(q, deps)
        if sembuf.ds is None:
            if self.free_ds:
                sembuf.ds = self.free_ds.pop()
            else:
                sembuf.ds = SemRec(self.nc.alloc_semaphore(name="d%d" % len(self.all_ds)))
                self.all_ds.append(sembuf.ds)
        ds = sembuf.ds
        ins = q.h.dma_start(out=out_ap, in_=in_ap, **kw)
        ds.cnt += 16
        ins.then_inc(ds.sem, 16)
        tok = (ds.sem, ds.cnt)
        for sb_ in srcs:
            _add(sb_.r, tok)
        _add(dst.w, tok)
        return ins

    def barrier(self):
        for e in self.engs:
            deps = {}
            for f in self.engs:
                if f is not e and f.cnt > 0:
                    _add(deps, (f.sem, f.cnt))
            for ds in self.all_ds:
                if ds.cnt > 0:
                    _add(deps, (ds.sem, ds.cnt))
            self._wait(e, deps)

    @contextlib.contextmanager
    def scope(self):
        es = contextlib.ExitStack()
        self.scope_bufs.append([])
        nc = self.nc

        def alloc(name, shape, dt):
            self.uid += 1
            return es.enter_context(nc.sbuf_tensor("%s_u%d" % (name, self.uid), shape, dt))

        try:
            yield alloc
            self.barrier()
            for b in self.scope_bufs[-1]:
                if b.ds is not None:
                    self.free_ds.append(b.ds)
                    b.ds = None
        finally:
            self.scope_bufs.pop()
            es.close()


class Prog:
    def __init__(self, layers, final_norm, first, name_in="xT", name_out="oT"):
        nc = bass.Bass("TRN2", target_bir_lowering=False)
        self.nc = nc
        self.K = Trk(nc)
        K = self.K
        self.inputs = {}
        self.pending = []
        self.bg_pieces = []
        self.bg_n = 0

        def ext(name, shape, dt=F32):
            self.inputs[name] = shape
            return nc.dram_tensor(name, list(shape), dt, kind="ExternalInput").ap()

        self.ext = ext
        self.h_in = ext(name_in, [D, S])
        self.h_out = nc.dram_tensor(name_out, [D, S], F32, kind="ExternalOutput").ap()
        self.b_hin = Buf("h_in")
        self.b_hout = Buf("h_out")
        self.h_scr = None
        consts = ext("consts", [128, NCONST])

        self.ps = [nc.alloc_psum_tensor("ps%d" % i, [128, 512], F32) for i in range(7)]
        self.b_ps = [Buf("ps%d" % i) for i in range(7)]
        self.pg = nc.alloc_psum_tensor("pg", [128, 512], F32)
        self.b_pg = Buf("pg")

        self.ones = nc.alloc_sbuf_tensor("ones", [128, 128], BF16)
        self.ident = nc.alloc_sbuf_tensor("ident", [128, 128], BF16)
        self.identf = nc.alloc_sbuf_tensor("identf", [128, 128], F32)
        self.b_const = Buf("const")
        self.consts_d = consts
        with K.scope() as alloc:
            cst = alloc("cst", [128, 128], F32)
            b_cst = K.buf("cst")
            b_cd = Buf("consts_d")
            K.dma(K.sp, cst[:], consts[:, 0:128], b_cd, b_cst, b_cst)
            K.op(K.dve, lambda e: e.memset(self.ones[:], 1.0), writes=[self.b_const])
            K.op(K.dve, lambda e: e.tensor_copy(out=self.ident[:], in_=cst[:]), reads=[b_cst], writes=[self.b_const])
            K.op(K.dve, lambda e: e.tensor_copy(out=self.identf[:], in_=cst[:]), reads=[b_cst], writes=[self.b_const])

        cur_in, b_in = self.h_in, self.b_hin
        fuse_fn = final_norm and len(layers) > 0 and layers[-1] in (0, 3)
        if fuse_fn:
            final_norm = False
        carry = None
        for li, l in enumerate(layers):
            last = (li == len(layers) - 1) and not final_norm
            if last:
                cur_out, b_out = self.h_out, self.b_hout
            else:
                if self.h_scr is None:
                    self.h_scr = nc.dram_tensor("h_scr", [D, S], F32, kind="Internal").ap()
                    self.b_hscr = Buf("h_scr")
                cur_out, b_out = self.h_scr, self.b_hscr
            if l in (0, 3):
                nxt = layers[li + 1] if li + 1 < len(layers) else None
                carry = self.attn_layer(l, cur_in, b_in, cur_out, b_out, fuse_final=(fuse_fn and li == len(layers) - 1),
                                        prefetch_conv=(nxt == 1))
            elif l == 1:
                self.conv_layer(l, cur_in, b_in, cur_out, b_out, pre=carry)
                if carry is not None:
                    carry["cm"].__exit__(None, None, None)
                carry = None
            else:
                self.lru_layer(l, cur_in, b_in, cur_out, b_out)
            cur_in, b_in = cur_out, b_out
        if final_norm:
            self.final_norm(cur_in, b_in, self.h_out, self.b_hout)
        deps = {}
        for t in self.b_hout.w.values():
            _add(deps, t)
        K._wait(K.sp, deps)

    def load_w_bf16(self, alloc, name, shape_in, kchunks, ncols):
        K = self.K
        w_d = self.ext(name, shape_in)
        wb = alloc(name + "_sb", [128, kchunks, ncols], BF16)
        b = K.buf(name)
        for c in range(kchunks):
            for n0 in range(0, ncols, 2048):
                n1 = min(ncols, n0 + 2048)
                self.pending.append((wb[:, c, n0:n1], w_d[c * 128:(c + 1) * 128, n0:n1], b, 128, n1 - n0))
        return wb, b

    def stage_all(self):
        K = self.K
        if not self.pending:
            return
        with K.scope() as alloc:
            NS = 8
            stg = [alloc("stg%d" % k, [128, 2048], F32) for k in range(NS)]
            b_stg = [K.buf("stg%d" % k) for k in range(NS)]
            bd = Buf("wdram")
            engs = [K.dve, K.act]
            for n, (dst, src, b, npart, ncol) in enumerate(self.pending):
                k = n % NS
                sv = stg[k][0:npart, 0:ncol]
                if len(dst.shape) == 3:
                    sv = sv.rearrange("p (a b) -> p a b", b=dst.shape[2])
                K.dma(K.sp if n % 2 == 0 else K.act, sv, src, bd, b_stg[k], b_stg[k])
                eng = engs[n % 2]
                if eng is K.act:
                    K.op(eng, lambda e: e.activation(out=dst, in_=sv, func=AF.Copy), reads=[b_stg[k]], writes=[b])
                else:
                    K.op(eng, lambda e: e.tensor_copy(out=dst, in_=sv), reads=[b_stg[k]], writes=[b])
        self.pending = []

    def bg_register(self, name, shape_in, kchunks, ncols, alloc):
        K = self.K
        w_d = self.ext(name, shape_in)
        wb = alloc(name + "_sb", [128, kchunks, ncols], BF16)
        b = K.buf(name)
        for c in range(kchunks):
            for n0 in range(0, ncols, 1024):
                n1 = min(ncols, n0 + 1024)
                self.bg_pieces.append((wb[:, c, n0:n1], w_d[c * 128:(c + 1) * 128, n0:n1], b, 128, n1 - n0))
        return wb, b

    def bg_step(self):
        K = self.K
        if not self.bg_pieces:
            return
        dst, src, b, npart, ncol = self.bg_pieces.pop(0)
        k = self.bg_n % 2
        self.bg_n += 1
        stg, b_stg = self.bg_stg[k], self.b_bg_stg[k]
        K.dma(K.sp, stg[0:npart, 0:ncol], src, Buf("wdram"), b_stg, b_stg)
        K.op(K.pool, lambda e: e.tensor_copy(out=dst, in_=stg[0:npart, 0:ncol]), reads=[b_stg], writes=[b])

    def load_f32(self, alloc, name, shape):
        K = self.K
        d = self.ext(name, shape)
        t = alloc(name + "_sb", list(shape), F32)
        b = K.buf(name)
        K.dma(K.sp, t[:], d, Buf(name + "_d"), b, b)
        return t, b

    def norm_stage(self, TT, ht, b_ht, sq, b_sq, pstat, b_pstat, std, b_std, rstd, b_rstd, ub, b_ub, g_sb, b_g):
        K = self.K
        K.op(K.act, lambda e: e.activation(out=sq[:], in_=ht[:], func=AF.Square), reads=[b_ht], writes=[b_sq])
        for c in range(8):
            K.op(K.pe, lambda e: e.matmul(pstat[:, 0:TT], lhsT=self.ones[:], rhs=sq[:, c, :], start=(c == 0), stop=(c == 7)),
                 reads=[self.b_const, b_sq], writes=[b_pstat], inc=(c == 7))
        K.op(K.act, lambda e: e.activation(out=std[:], in_=pstat[:, 0:TT], func=AF.Sqrt, bias=EPS, scale=1.0 / D),
             reads=[b_pstat], writes=[b_std])
        K.op(K.dve, lambda e: e.reciprocal(out=rstd[:], in_=std[:]), reads=[b_std], writes=[b_rstd])
        for c in range(8):
            K.op(K.dve, lambda e: e.scalar_tensor_tensor(out=ub[:, c, :], in0=ht[:, c, :], scalar=g_sb[:, c:c + 1],
                                                          in1=rstd[:], op0=ALU.mult, op1=ALU.mult),
                 reads=[b_ht, b_g, b_rstd], writes=[b_ub])

    def final_norm(self, h_in, b_in, h_out, b_out):
        K = self.K
        TT = 512
        NT = S // TT
        hin_v = h_in.rearrange("(c p) t -> p c t", p=128)
        hout_v = h_out.rearrange("(c p) t -> p c t", p=128)
        with K.scope() as alloc:
            g_sb, b_g = self.load_f32(alloc, "final_gT", [128, 8])
            ht = [alloc("fn_ht%d" % i, [128, 8, TT], F32) for i in range(2)]
            b_ht = [K.buf("fn_ht%d" % i) for i in range(2)]
            sq = alloc("fn_sq", [128, 8, TT], BF16)
            b_sq = K.buf("fn_sq")
            std = alloc("fn_std", [128, TT], F32)
            b_std = K.buf("fn_std")
            rstd = alloc("fn_rstd", [128, TT], F32)
            b_rstd = K.buf("fn_rstd")
            K.dma(K.sp, ht[0][:], hin_v[:, :, 0:TT], b_in, b_ht[0], b_ht[0])
            for T in range(NT):
                i = T % 2
                sl = slice(T * TT, (T + 1) * TT)
                if T + 1 < NT:
                    K.dma(K.sp, ht[1 - i][:], hin_v[:, :, (T + 1) * TT:(T + 2) * TT], b_in, b_ht[1 - i], b_ht[1 - i])
                self.norm_stage(TT, ht[i], b_ht[i], sq, b_sq, self.ps[6], self.b_ps[6], std, b_std, rstd, b_rstd,
                                ht[i], b_ht[i], g_sb, b_g)
                K.dma(K.sp, hout_v[:, :, sl], ht[i][:], b_ht[i], b_out, b_ht[i])

    def lru_layer(self, l, h_in, b_in, h_out, b_out):
        K = self.K
        nc = self.nc
        TT = 256
        NT = S // TT
        NC = LW // 128
        p = "l%d_" % l
        hin_v = h_in.rearrange("(c p) t -> p c t", p=128)
        hout_v = h_out.rearrange("(c p) t -> p c t", p=128)
        with K.scope() as alloc:
            win, b_win = self.load_w_bf16(alloc, p + "w_in", [D, 2 * LW], 8, 2 * LW)
            wout, b_wout = self.load_w_bf16(alloc, p + "w_out", [LW, D], NC, D)
            g_sb, b_g = self.load_f32(alloc, p + "norm_gT", [128, 8])
            vec, b_vec = self.load_f32(alloc, p + "vec", [128, 8, NC])
            wrg_d = self.ext(p + "w_rg", [NC, 128, 128])
            wig_d = self.ext(p + "w_ig", [NC, 128, 128])
            wrg = alloc("wrg", [128, NC, 128], BF16)
            wig = alloc("wig", [128, NC, 128], BF16)
            b_wg = K.buf("wg")
            self.pending.append((wrg[:], wrg_d.rearrange("h i j -> i h j"), b_wg, 128, NC * 128))
            self.pending.append((wig[:], wig_d.rearrange("h i j -> i h j"), b_wg, 128, NC * 128))
            self.stage_all()
            cl = alloc("cl", [128, 2, NC], F32)
            tmpv = alloc("tmpv", [128, 2, NC], F32)
            b_cl = K.buf("cl")
            K.op(K.act, lambda e: e.activation(out=tmpv[:, 0, :], in_=vec[:, 7, :], func=AF.Exp, scale=-1.0), reads=[b_vec], writes=[b_cl])
            K.op(K.act, lambda e: e.activation(out=tmpv[:, 1, :], in_=tmpv[:, 0, :], func=AF.Ln, bias=1.0, scale=1.0), reads=[b_cl], writes=[b_cl])
            K.op(K.dve, lambda e: e.tensor_scalar(out=cl[:, 0, :], in0=tmpv[:, 1, :], scalar1=-8.0, scalar2=None, op0=ALU.mult),
                 reads=[b_cl], writes=[b_cl])
            K.op(K.dve, lambda e: e.tensor_scalar(out=cl[:, 1, :], in0=tmpv[:, 1, :], scalar1=-16.0, scalar2=None, op0=ALU.mult),
                 reads=[b_cl], writes=[b_cl])
            dg = alloc("dg", [128, 4, NC, 128], BF16)
            b_dg = K.buf("dg")
            for j in range(4):
                for c in range(NC):
                    K.op(K.dve, lambda e: e.tensor_scalar(out=dg[:, j, c, :], in0=self.identf[:], scalar1=vec[:, j, c:c + 1],
                                                           scalar2=None, op0=ALU.mult),
                         reads=[b_vec, self.b_const], writes=[b_dg])
            ht = [alloc("ht%d" % i, [128, 8, TT], F32) for i in range(3)]
            b_ht = [K.buf("ht%d" % i) for i in range(3)]
            sq = alloc("sq", [128, 8, TT], BF16)
            b_sq = K.buf("sq")
            std = alloc("std", [128, TT], F32)
            b_std = K.buf("std")
            rstd = alloc("rstd", [128, TT], F32)
            b_rstd = K.buf("rstd")
            ub = [alloc("ub%d" % i, [128, 8, TT], BF16) for i in range(2)]
            b_ub = [K.buf("ub%d" % i) for i in range(2)]
            xbuf = [alloc("xbuf%d" % i, [128, NC, TT + 3], BF16) for i in range(2)]
            b_xbuf = [K.buf("xbuf%d" % i) for i in range(2)]
            sg = [alloc("sg%d" % i, [128, NC, TT], BF16) for i in range(2)]
            b_sg = [K.buf("sg%d" % i) for i in range(2)]
            xc = alloc("xc", [128, NC, TT], F32)
            xcb = alloc("xcb", [128, NC, TT], BF16)
            b_xc = [K.buf("xc%d" % c) for c in range(NC)]
            b_xcb = [K.buf("xcb%d" % c) for c in range(NC)]
            hs = [alloc("hs%d" % i, [128, NC, TT], F32) for i in range(2)]
            b_hs = [[K.buf("hs%d_%d" % (i, c)) for c in range(NC)] for i in range(2)]
            zt = alloc("zt", [128, NC, TT], BF16)
            b_zt = K.buf("zt")
            HC = 5
            NR = 3
            r5 = alloc("r5", [128, HC, TT], F32)
            i5 = alloc("i5", [128, HC, TT], F32)
            a5 = alloc("a5", [128, HC, TT], F32)
            s5 = alloc("s5", [128, HC, TT], F32)
            b_r5 = [K.buf("r5_%d" % k) for k in range(HC)]
            b_i5 = [K.buf("i5_%d" % k) for k in range(HC)]
            b_a5 = [K.buf("a5_%d" % k) for k in range(HC)]
            b_s5 = [K.buf("s5_%d" % k) for k in range(HC)]
            gx5 = alloc("gx5", [128, HC, TT], F32)
            b_gx5 = K.buf("gx5")
            bt5 = alloc("bt5", [128, HC, TT], F32)
            b_bt5 = K.buf("bt5")
            K.op(K.pool, lambda e: e.memset(xbuf[0][:, :, 0:3], 0.0), writes=[b_xbuf[0]])
            psrot = [0]

            def nextps():
                k = psrot[0] % 6
                psrot[0] += 1
                return self.ps[k], self.b_ps[k]

            def loadA(T):
                K.dma(K.sp, ht[T % 3][:], hin_v[:, :, T * TT:(T + 1) * TT], b_in, b_ht[T % 3], b_ht[T % 3])

            def stageA(T):
                i = T % 2
                self.norm_stage(TT, ht[T % 3], b_ht[T % 3], sq, b_sq, self.ps[6], self.b_ps[6], std, b_std, rstd, b_rstd,
                                ub[i], b_ub[i], g_sb, b_g)

            def stageB(T, part):
                i = T % 2
                for f in range(part * NC, (part + 1) * NC):
                    pst, b_pst = nextps()
                    for c in range(8):
                        K.op(K.pe, lambda e: e.matmul(pst[:, 0:TT], lhsT=win[:, c, f * 128:(f + 1) * 128], rhs=ub[i][:, c, :],
                                                      start=(c == 0), stop=(c == 7)),
                             reads=[b_win, b_ub[i]], writes=[b_pst], inc=(c == 7))
                    if f < NC:
                        K.op(K.act, lambda e: e.activation(out=xbuf[i][:, f, 3:3 + TT], in_=pst[:, 0:TT], func=AF.Copy),
                             reads=[b_pst], writes=[b_xbuf[i]])
                    else:
                        K.op(K.act, lambda e: e.activation(out=sg[i][:, f - NC, :], in_=pst[:, 0:TT], func=AF.Silu),
                             reads=[b_pst], writes=[b_sg[i]])
                if part == 0 and T + 1 < NT:
                    K.op(K.pool, lambda e: e.tensor_copy(out=xbuf[1 - i][:, :, 0:3], in_=xbuf[i][:, :, TT:TT + 3]),
                         reads=[b_xbuf[i]], writes=[b_xbuf[1 - i]])

            def stageC(T):
                i = T % 2
                for c in range(NC):
                    pst, b_pst = nextps()
                    for j in range(4):
                        K.op(K.pe, lambda e: e.matmul(pst[:, 0:TT], lhsT=dg[:, j, c, :], rhs=xbuf[i][:, c, j:j + TT],
                                                      start=(j == 0), stop=(j == 3)),
                             reads=[b_dg, b_xbuf[i]], writes=[b_pst], inc=(j == 3))
                    K.op(K.act, lambda e: e.activation(out=xc[:, c, :], in_=pst[:, 0:TT], func=AF.Identity, bias=vec[:, 4, c:c + 1], scale=1.0),
                         reads=[b_pst, b_vec], writes=[b_xc[c]])
                    K.op(K.dve, lambda e: e.tensor_copy(out=xcb[:, c, :], in_=xc[:, c, :]), reads=[b_xc[c]], writes=[b_xcb[c]])

            def stageD(T, h):
                i = T % 2
                if True:
                    cs = list(range(h * HC, (h + 1) * HC))
                    for k, c in enumerate(cs):
                        psr, b_psr = nextps()
                        K.op(K.pe, lambda e: e.matmul(psr[:, 0:TT], lhsT=wrg[:, c, :], rhs=xcb[:, c, :], start=True, stop=True),
                             reads=[b_wg, b_xcb[c]], writes=[b_psr])
                        psi, b_psi = nextps()
                        K.op(K.pe, lambda e: e.matmul(psi[:, 0:TT], lhsT=wig[:, c, :], rhs=xcb[:, c, :], start=True, stop=True),
                             reads=[b_wg, b_xcb[c]], writes=[b_psi])
                        K.op(K.act, lambda e: e.activation(out=r5[:, k, :], in_=psr[:, 0:TT], func=AF.Sigmoid, bias=vec[:, 5, c:c + 1], scale=1.0),
                             reads=[b_psr, b_vec], writes=[b_r5[k]])
                        K.op(K.act, lambda e: e.activation(out=i5[:, k, :], in_=psi[:, 0:TT], func=AF.Sigmoid, bias=vec[:, 6, c:c + 1], scale=1.0),
                             reads=[b_psi, b_vec], writes=[b_i5[k]])
                    for k, c in enumerate(cs):
                        K.op(K.act, lambda e: e.activation(out=a5[:, k, :], in_=r5[:, k, :], func=AF.Exp, scale=cl[:, 0, c:c + 1]),
                             reads=[b_r5[k], b_cl], writes=[b_a5[k]])
                    cs0, cs1 = cs[0], cs[-1] + 1
                    K.op(K.pool, lambda e: e.tensor_tensor(out=s5[:], in0=a5[:], in1=a5[:], op=ALU.mult),
                         reads=b_a5, writes=b_s5)
                    for k, c in enumerate(cs):
                        K.op(K.act, lambda e: e.activation(out=s5[:, k, :], in_=s5[:, k, :], func=AF.Sqrt, bias=1.0, scale=-1.0),
                             reads=[b_s5[k]], writes=[b_s5[k]])
                    K.op(K.pool, lambda e: e.tensor_tensor(out=gx5[:], in0=i5[:], in1=xc[:, cs0:cs1, :], op=ALU.mult),
                         reads=b_i5 + [b_xc[c] for c in cs], writes=[b_gx5])
                    K.op(K.dve, lambda e: e.tensor_tensor(out=bt5[:], in0=s5[:], in1=gx5[:], op=ALU.mult),
                         reads=b_s5 + [b_gx5], writes=[b_bt5])
                    for k, c in enumerate(cs):
                        init = 0.0 if T == 0 else hs[1 - i][:, c, TT - 1:TT]
                        rd = [b_a5[k], b_bt5] + ([] if T == 0 else [b_hs[1 - i][c]])
                        K.op(K.dve, lambda e: e.tensor_tensor_scan(out=hs[i][:, c, :], data0=a5[:, k, :], data1=bt5[:, k, :], initial=init,
                                                                    op0=ALU.mult, op1=ALU.add),
                             reads=rd, writes=[b_hs[i][c]])
                    K.op(K.pool, lambda e: e.tensor_tensor(out=zt[:, cs0:cs1, :], in0=hs[i][:, cs0:cs1, :], in1=sg[i][:, cs0:cs1, :], op=ALU.mult),
                         reads=[b_hs[i][c] for c in cs] + [b_sg[i]], writes=[b_zt])

            def stageE(T):
                i = T % 2
                for f in range(8):
                    pst, b_pst = nextps()
                    for c in range(NC):
                        K.op(K.pe, lambda e: e.matmul(pst[:, 0:TT], lhsT=wout[:, c, f * 128:(f + 1) * 128], rhs=zt[:, c, :],
                                                      start=(c == 0), stop=(c == NC - 1)),
                             reads=[b_wout, b_zt], writes=[b_pst], inc=(c == NC - 1))
                    K.op(K.dve, lambda e: e.tensor_tensor(out=ht[T % 3][:, f, :], in0=pst[:, 0:TT], in1=ht[T % 3][:, f, :], op=ALU.add),
                         reads=[b_pst, b_ht[T % 3]], writes=[b_ht[T % 3]])
                K.dma(K.sp, hout_v[:, :, T * TT:(T + 1) * TT], ht[T % 3][:], b_ht[T % 3], b_out, b_ht[T % 3])

            loadA(0)
            loadA(1)
            stageA(0)
            stageB(0, 0)
            stageB(0, 1)
            stageC(0)
            for T in range(NT):
                if T + 2 < NT:
                    loadA(T + 2)
                if T + 1 < NT:
                    stageA(T + 1)
                for h in range(2):
                    stageD(T, h)
                    if T + 1 < NT:
                        stageB(T + 1, h)
                if T + 1 < NT:
                    stageC(T + 1)
                stageE(T)

    def conv_layer(self, l, h_in, b_in, h_out, b_out, pre=None):
        K = self.K
        TT = 256
        NT = S // TT
        CW = 31
        p = "l%d_" % l
        hin_v = h_in.rearrange("(c p) t -> p c t", p=128)
        hout_v = h_out.rearrange("(c p) t -> p c t", p=128)
        with K.scope() as alloc:
            if pre is not None:
                win, b_win, wout, b_wout = pre["win"], pre["b_win"], pre["wout"], pre["b_wout"]
            else:
                win, b_win = self.load_w_bf16(alloc, p + "w_in", [D, 3 * D], 8, 3 * D)
                wout, b_wout = self.load_w_bf16(alloc, p + "w_out", [D, D], 8, D)
                self.stage_all()
            g_sb, b_g = self.load_f32(alloc, p + "norm_gT", [128, 8])
            vec, b_vec = self.load_f32(alloc, p + "vec", [128, CW + 3, 8])
            dg = alloc("dg", [128, CW, 8, 128], BF16)
            b_dgd = K.buf("dgd")
            b_dga = K.buf("dga")
            for j in range(CW):
                for c in range(8):
                    if (j * 8 + c) % 3 != 0:
                        K.op(K.dve, lambda e: e.tensor_scalar(out=dg[:, j, c, :], in0=self.ident[:], scalar1=vec[:, j, c:c + 1],
                                                               scalar2=None, op0=ALU.mult),
                             reads=[b_vec, self.b_const], writes=[b_dgd])
                    else:
                        K.op(K.act, lambda e: e.activation(out=dg[:, j, c, :], in_=self.identf[:], func=AF.Copy, scale=vec[:, j, c:c + 1]),
                             reads=[b_vec, self.b_const], writes=[b_dga])
            ht = [alloc("ht%d" % i, [128, 8, TT], F32) for i in range(2)]
            b_ht = [K.buf("ht%d" % i) for i in range(2)]
            sq = alloc("sq", [128, 8, TT], BF16)
            b_sq = K.buf("sq")
            std = alloc("std", [128, TT], F32)
            b_std = K.buf("std")
            rstd = alloc("rstd", [128, TT], F32)
            b_rstd = K.buf("rstd")
            ub = [alloc("ub%d" % i, [128, 8, TT], BF16) for i in range(2)]
            b_ub = [K.buf("ub%d" % i) for i in range(2)]
            H = CW - 1
            ybuf = [alloc("ybuf%d" % i, [128, 8, TT + H], BF16) for i in range(2)]
            b_ybuf = [K.buf("ybuf%d" % i) for i in range(2)]
            sgt = [alloc("sgt%d" % k, [128, TT], F32) for k in range(2)]
            b_sgt = [K.buf("sgt%d" % k) for k in range(2)]
            sg = [alloc("sg%d" % i, [128, 8, TT], BF16) for i in range(2)]
            b_sg = [K.buf("sg%d" % i) for i in range(2)]
            y2 = alloc("y2", [128, 8, TT], F32)
            y2b = alloc("y2b", [128, 8, TT], BF16)
            sq2 = alloc("sq2", [128, 8, TT], BF16)
            b_y2 = [K.buf("y2_%d" % c) for c in range(8)]
            st = {n: alloc("st_" + n, [128, TT], F32) for n in ["mean", "var", "rstd2"]}
            st["msq"] = st["var"]
            st["std2"] = st["var"]
            st["mr"] = st["mean"]
            b_st = {n: K.buf("st_" + n) for n in ["mean", "var", "rstd2"]}
            b_st["msq"] = b_st["var"]
            b_st["std2"] = b_st["var"]
            b_st["mr"] = b_st["mean"]
            ost = [alloc("ost%d" % k, [128, TT], F32) for k in range(2)]
            b_ost = [K.buf("ost%d" % k) for k in range(2)]
            tn = [alloc("tn%d" % k, [128, TT], F32) for k in range(2)]
            b_tn = [K.buf("tn%d" % k) for k in range(2)]
            sn = [alloc("sn%d" % k, [128, TT], F32) for k in range(2)]
            b_sn = [K.buf("sn%d" % k) for k in range(2)]
            zt = alloc("zt", [128, 8, TT], BF16)
            b_zt = K.buf("zt")
            K.op(K.pool, lambda e: e.memset(ybuf[0][:, :, 0:H], 0.0), writes=[b_ybuf[0]])
            psrot = [0]

            def nextps():
                k = psrot[0] % 5
                psrot[0] += 1
                return self.ps[k], self.b_ps[k]

            def loadA(T):
                i = T % 2
                K.dma(K.sp, ht[i][:], hin_v[:, :, T * TT:(T + 1) * TT], b_in, b_ht[i], b_ht[i])

            def stageA(T):
                i = T % 2
                self.norm_stage(TT, ht[i], b_ht[i], sq, b_sq, self.ps[6], self.b_ps[6], std, b_std, rstd, b_rstd,
                                ub[i], b_ub[i], g_sb, b_g)

            def proj(i, f):
                pst, b_pst = nextps()
                for c in range(8):
                    K.op(K.pe, lambda e: e.matmul(pst[:, 0:TT], lhsT=win[:, c, f * 128:(f + 1) * 128], rhs=ub[i][:, c, :],
                                                  start=(c == 0), stop=(c == 7)),
                         reads=[b_win, b_ub[i]], writes=[b_pst], inc=(c == 7))
                return pst, b_pst

            def stageB(T, f):
                i = T % 2
                if True:
                    k = f % 2
                    psb_, b_psb_ = proj(i, 8 + f)
                    K.op(K.act, lambda e: e.activation(out=sgt[k][:], in_=psb_[:, 0:TT], func=AF.Sigmoid),
                         reads=[b_psb_], writes=[b_sgt[k]])
                    psa, b_psa = proj(i, f)
                    K.op(K.dve, lambda e: e.tensor_tensor(out=ybuf[i][:, f, H:H + TT], in0=psa[:, 0:TT], in1=sgt[k][:], op=ALU.mult),
                         reads=[b_psa, b_sgt[k]], writes=[b_ybuf[i]])
                    psg, b_psg = proj(i, 16 + f)
                    K.op(K.act, lambda e: e.activation(out=sgt[1 - k][:], in_=psg[:, 0:TT], func=AF.Sigmoid),
                         reads=[b_psg], writes=[b_sgt[1 - k]])
                    K.op(K.dve, lambda e: e.tensor_tensor(out=sg[i][:, f, :], in0=psg[:, 0:TT], in1=sgt[1 - k][:], op=ALU.mult),
                         reads=[b_psg, b_sgt[1 - k]], writes=[b_sg[i]])
                if f == 7 and T + 1 < NT:
                    K.op(K.pool, lambda e: e.tensor_copy(out=ybuf[1 - i][:, :, 0:H], in_=ybuf[i][:, :, TT:TT + H]),
                         reads=[b_ybuf[i]], writes=[b_ybuf[1 - i]])

            def stageC(T, chunks):
                i = T % 2
                for c in chunks:
                    pst, b_pst = nextps()
                    for j in range(CW):
                        K.op(K.pe, lambda e: e.matmul(pst[:, 0:TT], lhsT=dg[:, j, c, :], rhs=ybuf[i][:, c, j:j + TT],
                                                      start=(j == 0), stop=(j == CW - 1)),
                             reads=[b_dgd, b_dga, b_ybuf[i]], writes=[b_pst], inc=(j == CW - 1))
                    K.op(K.act, lambda e: e.activation(out=y2[:, c, :], in_=pst[:, 0:TT], func=AF.Identity, bias=vec[:, CW, c:c + 1], scale=1.0),
                         reads=[b_pst, b_vec], writes=[b_y2[c]])
                    K.op(K.act, lambda e: e.activation(out=sq2[:, c, :], in_=pst[:, 0:TT], func=AF.Square, bias=vec[:, CW, c:c + 1], scale=1.0),
                         reads=[b_pst, b_vec], writes=[b_y2[c]])
                    K.op(K.pool, lambda e: e.tensor_copy(out=y2b[:, c, :], in_=y2[:, c, :]), reads=[b_y2[c]], writes=[b_y2[c]])

            def stageD(T):
                i = T % 2
                ps1, b_ps1 = self.ps[5], self.b_ps[5]
                ps2, b_ps2 = self.ps[6], self.b_ps[6]
                for c in range(8):
                    K.op(K.pe, lambda e: e.matmul(ps1[:, 0:TT], lhsT=self.ones[:], rhs=y2b[:, c, :], start=(c == 0), stop=(c == 7)),
                         reads=[self.b_const, b_y2[c]], writes=[b_ps1], inc=(c == 7))
                for c in range(8):
                    K.op(K.pe, lambda e: e.matmul(ps2[:, 0:TT], lhsT=self.ones[:], rhs=sq2[:, c, :], start=(c == 0), stop=(c == 7)),
                         reads=[self.b_const, b_y2[c]], writes=[b_ps2], inc=(c == 7))
                K.op(K.dve, lambda e: e.tensor_scalar(out=st["mean"][:], in0=ps1[:, 0:TT], scalar1=1.0 / D, scalar2=None, op0=ALU.mult),
                     reads=[b_ps1], writes=[b_st["mean"]])
                K.op(K.dve, lambda e: e.tensor_tensor(out=st["msq"][:], in0=st["mean"][:], in1=st["mean"][:], op=ALU.mult),
                     reads=[b_st["mean"]], writes=[b_st["msq"]])
                K.op(K.dve, lambda e: e.scalar_tensor_tensor(out=st["var"][:], in0=ps2[:, 0:TT], scalar=1.0 / D, in1=st["msq"][:],
                                                              op0=ALU.mult, op1=ALU.subtract),
                     reads=[b_ps2, b_st["msq"]], writes=[b_st["var"]])
                K.op(K.dve, lambda e: e.tensor_scalar(out=st["var"][:], in0=st["var"][:], scalar1=0.0, scalar2=None, op0=ALU.max),
                     reads=[b_st["var"]], writes=[b_st["var"]])
                K.op(K.act, lambda e: e.activation(out=st["std2"][:], in_=st["var"][:], func=AF.Sqrt, bias=EPS, scale=1.0),
                     reads=[b_st["var"]], writes=[b_st["std2"]])
                K.op(K.dve, lambda e: e.reciprocal(out=st["rstd2"][:], in_=st["std2"][:]), reads=[b_st["std2"]], writes=[b_st["rstd2"]])
                K.op(K.dve, lambda e: e.tensor_tensor(out=st["mr"][:], in0=st["mean"][:], in1=st["rstd2"][:], op=ALU.mult),
                     reads=[b_st["mean"], b_st["rstd2"]], writes=[b_st["mr"]])

            def stageDn(T, c):
                i = T % 2
                if True:
                    k = c % 2
                    K.op(K.dve, lambda e: e.tensor_tensor(out=tn[k][:], in0=y2[:, c, :], in1=st["rstd2"][:], op=ALU.mult),
                         reads=[b_y2[c], b_st["rstd2"]], writes=[b_tn[k]])
                    K.op(K.dve, lambda e: e.tensor_tensor(out=tn[k][:], in0=tn[k][:], in1=st["mr"][:], op=ALU.subtract),
                         reads=[b_tn[k], b_st["mr"]], writes=[b_tn[k]])
                    K.op(K.act, lambda e: e.activation(out=sn[k][:], in_=tn[k][:], func=AF.Sigmoid, bias=vec[:, CW + 2, c:c + 1],
                                                       scale=vec[:, CW + 1, c:c + 1]),
                         reads=[b_tn[k], b_vec], writes=[b_sn[k]])
                    K.op(K.dve, lambda e: e.tensor_scalar(out=tn[k][:], in0=tn[k][:], scalar1=vec[:, CW + 1, c:c + 1],
                                                           scalar2=vec[:, CW + 2, c:c + 1], op0=ALU.mult, op1=ALU.add),
                         reads=[b_tn[k], b_vec], writes=[b_tn[k]])
                    K.op(K.pool, lambda e: e.tensor_tensor(out=sn[k][:], in0=sn[k][:], in1=tn[k][:], op=ALU.mult),
                         reads=[b_sn[k], b_tn[k]], writes=[b_sn[k]])
                    K.op(K.pool, lambda e: e.tensor_tensor(out=zt[:, c, :], in0=sn[k][:], in1=sg[i][:, c, :], op=ALU.mult),
                         reads=[b_sn[k], b_sg[i]], writes=[b_zt])

            def stageE(T):
                i = T % 2
                for f in range(8):
                    pst, b_pst = nextps()
                    for c in range(8):
                        K.op(K.pe, lambda e: e.matmul(pst[:, 0:TT], lhsT=wout[:, c, f * 128:(f + 1) * 128], rhs=zt[:, c, :],
                                                      start=(c == 0), stop=(c == 7)),
                             reads=[b_wout, b_zt], writes=[b_pst], inc=(c == 7))
                    k2 = f % 2
                    K.op(K.dve, lambda e: e.tensor_tensor(out=ost[k2][:], in0=pst[:, 0:TT], in1=ht[i][:, f, :], op=ALU.add),
                         reads=[b_pst, b_ht[i]], writes=[b_ost[k2]])
                    K.dma(K.sp, hout_v[:, f, T * TT:(T + 1) * TT], ost[k2][:], b_ost[k2], b_out, b_ost[k2])

            loadA(0)
            stageA(0)
            for f in range(8):
                stageB(0, f)
            for T in range(NT):
                if T + 1 < NT:
                    loadA(T + 1)
                stageC(T, range(0, 4) if T == 0 else range(2, 4))
                if T + 1 < NT:
                    stageA(T + 1)
                stageC(T, range(4, 8))
                stageD(T)
                for f in range(8):
                    if T + 1 < NT:
                        stageB(T + 1, f)
                    stageDn(T, f)
                if T + 1 < NT:
                    stageC(T + 1, range(0, 2))
                stageE(T)

    def attn_layer(self, l, h_in, b_in, h_out, b_out, fuse_final=False, prefetch_conv=False):
        K = self.K
        nc = self.nc
        TT = 512
        NT = S // TT
        p = "l%d_" % l
        hin_v = h_in.rearrange("(c p) t -> p c t", p=128)
        hout_v = h_out.rearrange("(c p) t -> p c t", p=128)
        if not hasattr(self, "qT_d"):
            self.qT_d = nc.dram_tensor("qT_scr", [D, S], BF16, kind="Internal").ap()
            self.kT_d = nc.dram_tensor("kT_scr", [D, S], BF16, kind="Internal").ap()
            self.sg_d = nc.dram_tensor("sg_scr", [D, S], BF16, kind="Internal").ap()
            self.v_d = nc.dram_tensor("v_scr", [S, D], BF16, kind="Internal").ap()
            self.z_d = nc.dram_tensor("z_scr", [D, S], BF16, kind="Internal").ap()
            self.b_scr = Buf("qkv_scr")
            self.b_zd = Buf("z_scr")
        qT_d, kT_d, sg_d, v_d, z_d = self.qT_d, self.kT_d, self.sg_d, self.v_d, self.z_d
        b_scr, b_zd = self.b_scr, self.b_zd
        ksum = nc.alloc_sbuf_tensor(p + "ksum", [128, 8, NBLK], F32)
        b_ksum = Buf(p + "ksum")

        b_mc = Buf("maskc")

        def alloc_masks(malloc):
            self.cm = malloc("cm", [128, 4, 512], BF16)
            self.cneg = malloc("cneg", [128, 512], F32)
            self.ownb = malloc("ownb", [128, 512], F32)
            self.ind = malloc("ind", [128, 16, 128], BF16)
            K.op(K.pool, lambda e: e.memset(self.ind[:].rearrange("p a b -> p (a b)"), 0.0), writes=[b_mc])
            b_cd = Buf("consts_d")
            b_ms = K.buf("masksem")
            self.pending.append((self.cm[:].rearrange("p a b -> p (a b)"), self.consts_d[:, 128:2176], b_mc, 128, 2048))
            self.pending.append((self.ind[0:16].rearrange("p a b -> p (a b)"), self.consts_d[0:16, 3200:5248], b_mc, 16, 2048))
            K.dma(K.sp, self.cneg[:], self.consts_d[:, 2176:2688], b_cd, b_mc, b_ms)
            K.dma(K.sp, self.ownb[:], self.consts_d[:, 2688:3200], b_cd, b_mc, b_ms)

        mcm = None
        if not prefetch_conv:
            mcm = K.scope()
            alloc_masks(mcm.__enter__())

        with K.scope() as alloc:
            win, b_win = self.load_w_bf16(alloc, p + "w_in", [D, 4 * D], 8, 4 * D)
            self.stage_all()
            g_sb, b_g = self.load_f32(alloc, p + "norm_gT", [128, 8])
            ht = [alloc("ht%d" % i, [128, 8, TT], F32) for i in range(2)]
            b_ht = [K.buf("ht%d" % i) for i in range(2)]
            sq = alloc("sq", [128, 8, TT], BF16)
            b_sq = K.buf("sq")
            std = alloc("std", [128, TT], F32)
            b_std = K.buf("std")
            rstd = alloc("rstd", [128, TT], F32)
            b_rstd = K.buf("rstd")
            ub = [alloc("ub%d" % i, [128, 8, TT], BF16) for i in range(2)]
            b_ub = [K.buf("ub%d" % i) for i in range(2)]
            qo = alloc("qo", [128, 8, TT], BF16)
            ko = alloc("ko", [128, 8, TT], BF16)
            go = alloc("go", [128, 8, TT], BF16)
            vo = alloc("vo", [128, 4, D], BF16)
            b_qo = [K.buf("qo%d" % k) for k in range(8)]
            b_ko = [K.buf("ko%d" % k) for k in range(8)]
            b_go = [K.buf("go%d" % k) for k in range(8)]
            b_vo = [K.buf("vo%d" % k) for k in range(8)]
            b_qs, b_ks, b_gs_, b_vs = K.buf("qs"), K.buf("ks"), K.buf("gs"), K.buf("vs")
            psrot = [0]

            def nextps():
                k = psrot[0] % 6
                psrot[0] += 1
                return self.ps[k], self.b_ps[k]

            def loadA(T):
                i = T % 2
                K.dma(K.sp, ht[i][:], hin_v[:, :, T * TT:(T + 1) * TT], b_in, b_ht[i], b_ht[i])

            def stageA(T):
                i = T % 2
                self.norm_stage(TT, ht[i], b_ht[i], sq, b_sq, self.ps[6], self.b_ps[6], std, b_std, rstd, b_rstd,
                                ub[i], b_ub[i], g_sb, b_g)

            def proj(i, col0):
                pst, b_pst = nextps()
                for c in range(8):
                    K.op(K.pe, lambda e: e.matmul(pst[:], lhsT=win[:, c, col0:col0 + 128], rhs=ub[i][:, c, :],
                                                  start=(c == 0), stop=(c == 7)),
                         reads=[b_win, b_ub[i]], writes=[b_pst], inc=(c == 7))
                return pst, b_pst

            def stageB(T):
                i = T % 2
                tsl = slice(T * TT, (T + 1) * TT)
                if T + 1 < NT:
                    loadA(T + 1)
                for hd in range(8):
                    pst, b_pst = proj(i, hd * 128)
                    K.op(K.act, lambda e: e.activation(out=qo[:, hd, :], in_=pst[:], func=AF.Copy, scale=128.0 ** -0.5),
                         reads=[b_pst], writes=[b_qo[hd]])
                K.dma(K.sp, qT_d.rearrange("(c p) t -> p c t", p=128)[:, :, tsl], qo[:], b_qo, b_scr, b_qs)
                for hd in range(8):
                    pst, b_pst = proj(i, D + hd * 128)
                    for hf in range(2):
                        K.op(K.act, lambda e: e.activation(out=ko[:, hd, hf * BLK:(hf + 1) * BLK], in_=pst[:, hf * BLK:(hf + 1) * BLK],
                                                           func=AF.Identity, accum_out=ksum[:, hd, 2 * T + hf:2 * T + hf + 1]),
                             reads=[b_pst], writes=[b_ko[hd], b_ksum])
                K.dma(K.sp, kT_d.rearrange("(c p) t -> p c t", p=128)[:, :, tsl], ko[:], b_ko, b_scr, b_ks)
                if T + 1 < NT:
                    stageA(T + 1)
                for s4 in range(4):
                    for hf in range(2):
                        pst, b_pst = nextps()
                        for c in range(8):
                            K.op(K.pe, lambda e: e.matmul(pst[:], lhsT=ub[i][:, c, s4 * 128:(s4 + 1) * 128],
                                                          rhs=win[:, c, 2 * D + hf * 512:2 * D + (hf + 1) * 512],
                                                          start=(c == 0), stop=(c == 7)),
                                 reads=[b_win, b_ub[i]], writes=[b_pst], inc=(c == 7))
                        K.op(K.dve, lambda e: e.tensor_copy(out=vo[:, s4, hf * 512:(hf + 1) * 512], in_=pst[:]),
                             reads=[b_pst], writes=[b_vo[s4 * 2 + hf]])
                K.dma(K.sp, v_d[tsl, :].rearrange("(s p) n -> p s n", p=128), vo[:], b_vo, b_scr, b_vs)
                for hd in range(8):
                    pst, b_pst = proj(i, 3 * D + hd * 128)
                    K.op(K.act, lambda e: e.activation(out=go[:, hd, :], in_=pst[:], func=AF.Silu),
                         reads=[b_pst], writes=[b_go[hd]])
                K.dma(K.sp, sg_d.rearrange("(c p) t -> p c t", p=128)[:, :, tsl], go[:], b_go, b_scr, b_gs_)

            loadA(0)
            stageA(0)
            for T in range(NT):
                stageB(T)

        carry = None
        if prefetch_conv:
            ccm = K.scope()
            calloc = ccm.__enter__()
            cwin, cb_win = self.bg_register("l1_w_in", [D, 3 * D], 8, 3 * D, calloc)
            cwout, cb_wout = self.bg_register("l1_w_out", [D, D], 8, D, calloc)
            carry = {"cm": ccm, "win": cwin, "b_win": cb_win, "wout": cwout, "b_wout": cb_wout}
        pcm = K.scope()
        palloc = pcm.__enter__()
        if not prefetch_conv:
            wout, b_wout = self.bg_register(p + "w_out", [D, D], 8, D, palloc)
        if prefetch_conv:
            alloc_masks(palloc)
            self.stage_all()
        self.bg_stg = [palloc("bgstg%d" % k, [128, 1024], F32) for k in range(2)]
        self.b_bg_stg = [K.buf("bgstg%d" % k) for k in range(2)]
        n_bg = len(self.bg_pieces)
        bg_every = max(1, 1000 // max(1, n_bg))

        with K.scope() as alloc:
            qh = [alloc("qh%d" % i, [128, S], BF16) for i in range(2)]
            kh = [alloc("kh%d" % i, [128, S], BF16) for i in range(2)]
            sgh = [alloc("sgh%d" % i, [128, S], BF16) for i in range(2)]
            vh = [alloc("vh%d" % i, [128, 32, 128], BF16) for i in range(2)]
            b_hd = [K.buf("hd%d" % i) for i in range(2)]
            b_hq = [K.buf("hq%d" % i) for i in range(2)]
            kmT = alloc("kmT", [128, 8, NBLK], BF16)
            b_kmT = K.buf("kmT")
            K.op(K.dve, lambda e: e.tensor_scalar(out=kmT[:], in0=ksum[:], scalar1=1.0 / BLK, scalar2=None, op0=ALU.mult),
                 reads=[b_ksum], writes=[b_kmT])
            ownbm = alloc("ownbm", [128, 512], F32)
            b_ownbm = K.buf("ownbm")
            K.op(K.dve, lambda e: e.tensor_scalar(out=ownbm[:], in0=self.ownb[:], scalar1=-BIG, scalar2=None, op0=ALU.add),
                 reads=[b_mc], writes=[b_ownbm])
            gsm = alloc("gsm", [128, 32, NBLK], F32)
            top8 = alloc("top8", [128, 32, 8], F32)
            thr = alloc("thr", [128, 32], F32)
            tmpm = alloc("tmpm", [128, 32, NBLK], F32)
            negsel = alloc("negsel", [128, 32, 128], F32)
            b_gate = K.buf("gate")
            b_negsel = K.buf("negsel")
            K.op(K.pool, lambda e: e.memset(negsel[:].rearrange("p a b -> p (a b)"), 0.0), writes=[b_negsel])
            nselT = [alloc("nselT%d" % i, [128, S], BF16) for i in range(2)]
            b_nselT = [K.buf("nselT%d" % i) for i in range(2)]
            NP = 4
            pT = [alloc("pT%d" % k, [128, TT], BF16) for k in range(NP)]
            b_pT = [K.buf("pT%d" % k) for k in range(NP)]
            rden = alloc("rden", [128, TT], F32)
            b_rden = K.buf("rden")
            acc = [alloc("acc%d" % a, [128, TT], F32) for a in range(2)]
            b_acc = [K.buf("acc%d" % a) for a in range(2)]
            onesf = alloc("onesf", [128, 128], F32)
            b_onesf = K.buf("onesf")
            K.op(K.pool, lambda e: e.memset(onesf[:], 1.0), writes=[b_onesf])
            ot = alloc("ot", [128, TT], F32)
            b_ot = K.buf("ot")
            zo = [alloc("zo%d" % k, [128, TT], BF16) for k in range(2)]
            b_zo = [K.buf("zo%d" % k) for k in range(2)]
            z_v = z_d.rearrange("(c p) t -> c p t", p=128)
            pg, b_pg = self.pg, self.b_pg

            def load_head(hd):
                i = hd % 2
                hs_ = slice(hd * 128, (hd + 1) * 128)
                K.dma(K.sp, qh[i][:], qT_d[hs_, :], b_scr, b_hq[i], b_hq[i])
                K.dma(K.sp, kh[i][:], kT_d[hs_, :], b_scr, b_hd[i], b_hd[i])
                K.dma(K.sp, sgh[i][:], sg_d[hs_, :], b_scr, b_hd[i], b_hd[i])
                vv = v_d[:, hs_].rearrange("(s p) n -> p s n", p=128)
                for s8 in range(4):
                    K.dma(K.sp, vh[i][:, s8 * 8:(s8 + 1) * 8, :], vv[:, s8 * 8:(s8 + 1) * 8, :], b_scr, b_hd[i], b_hd[i])

            def gating1(hd):
                i = hd % 2
                for j in range(32):
                    K.op(K.pe, lambda e: e.matmul(pg[:, j * NBLK:(j + 1) * NBLK], lhsT=qh[i][:, j * 128:(j + 1) * 128], rhs=kmT[:, hd, :],
                                                  start=True, stop=True),
                         reads=[b_hq[i], b_kmT], writes=[b_pg], inc=(j == 31))
                gsm2 = gsm[:].rearrange("p a b -> p (a b)")
                K.op(K.dve, lambda e: e.tensor_tensor(out=gsm2, in0=pg[:], in1=self.cneg[:], op=ALU.add),
                     reads=[b_pg, b_mc], writes=[b_gate])
                for j in range(32):
                    K.op(K.dve, lambda e: e.max(out=top8[:, j, :], in_=gsm[:, j, :]), reads=[b_gate], writes=[b_gate], inc=(j == 31))
                K.op(K.dve, lambda e: e.tensor_scalar(out=thr[:], in0=top8[:, :, 2], scalar1=-1e29, scalar2=None, op0=ALU.max),
                     reads=[b_gate], writes=[b_gate])
                K.op(K.dve, lambda e: e.tensor_tensor(out=tmpm[:], in0=gsm[:], in1=thr[:, :, None].broadcast_to([128, 32, NBLK]),
                                                      op=ALU.is_ge),
                     reads=[b_gate], writes=[b_gate])
                K.op(K.dve, lambda e: e.scalar_tensor_tensor(out=negsel[:, :, 0:NBLK], in0=tmpm[:], scalar=BIG,
                                                              in1=ownbm[:].rearrange("p (a b) -> p a b", b=NBLK),
                                                              op0=ALU.mult, op1=ALU.add),
                     reads=[b_gate, b_ownbm], writes=[b_negsel])

            def gating2(hd, groups=range(8)):
                i = hd % 2
                for g8 in groups:
                    for jj in range(4):
                        j = g8 * 4 + jj
                        K.op(K.pe, lambda e: e.transpose(out=pg[:, jj * 128:(jj + 1) * 128], in_=negsel[:, j, :], identity=self.identf[:]),
                             reads=[b_negsel, self.b_const], writes=[b_pg], inc=(jj == 3))
                    K.op(K.dve, lambda e: e.tensor_copy(out=nselT[i][:, g8 * 512:(g8 + 1) * 512], in_=pg[:]),
                         reads=[b_pg], writes=[b_nselT[i]])

            steps = [(hd, T, kt) for hd in range(8) for T in range(NT) for kt in range(4 * (T + 1))]
            NS = len(steps)

            def obanks(T):
                return (self.ps[3 + (T % 2) * 2], self.b_ps[3 + (T % 2) * 2], self.ps[4 + (T % 2) * 2], self.b_ps[4 + (T % 2) * 2])

            def cols(T, kt):
                return 256 if (HALF_DIAG and kt >= 4 * T + 2) else 0

            def s_mm(g):
                hd, T, kt = steps[g]
                i = hd % 2
                c0 = cols(T, kt)
                qsl = slice(T * TT + c0, (T + 1) * TT)
                k3 = g % 3
                sp_, b_sp = self.ps[k3], self.b_ps[k3]
                diag = kt >= 4 * T
                K.op(K.pe, lambda e: e.matmul(sp_[:, c0:TT], lhsT=kh[i][:, kt * 128:(kt + 1) * 128], rhs=qh[i][:, qsl], start=True, stop=False),
                     reads=[b_hd[i], b_hq[i]], writes=[b_sp], inc=False)
                K.op(K.pe, lambda e: e.matmul(sp_[:, c0:TT], lhsT=self.ind[:, kt // 2, :], rhs=nselT[i][:, qsl], start=False, stop=not diag),
                     reads=[b_mc, b_nselT[i]], writes=[b_sp], inc=not diag)
                if diag:
                    K.op(K.pe, lambda e: e.matmul(sp_[:, c0:TT], lhsT=self.ident[:], rhs=self.cm[:, kt - 4 * T, c0:TT], start=False, stop=True),
                         reads=[self.b_const, b_mc], writes=[b_sp], inc=True)

            def pv_mm(g):
                hd, T, kt = steps[g]
                i = hd % 2
                nk = 4 * (T + 1)
                c0 = cols(T, kt)
                k3 = g % 3
                kp = g % NP
                sp_, b_sp = self.ps[k3], self.b_ps[k3]
                ops_, b_ops, dps, b_dps = obanks(T)
                K.op(K.act, lambda e: e.activation(out=pT[kp][:, c0:TT], in_=sp_[:, c0:TT], func=AF.Exp), reads=[b_sp], writes=[b_pT[kp]])
                K.op(K.pe, lambda e: e.matmul(ops_[:, c0:TT], lhsT=vh[i][:, kt, :], rhs=pT[kp][:, c0:TT], start=(kt == 0), stop=(kt == nk - 1)),
                     reads=[b_hd[i], b_pT[kp]], writes=[b_ops], inc=True)
                if kt % 2 == 1:
                    K.op(K.pe, lambda e: e.matmul(dps[:, c0:TT], lhsT=self.ones[:], rhs=pT[kp][:, c0:TT], start=(kt == 1), stop=False),
                         reads=[self.b_const, b_pT[kp]], writes=[b_dps], inc=True)
                else:
                    ac, b_ac = acc[T % 2], b_acc[T % 2]
                    if kt == 0:
                        K.op(K.dve, lambda e: e.tensor_copy(out=ac[:], in_=pT[kp][:]), reads=[b_pT[kp]], writes=[b_ac])
                    else:
                        K.op(K.dve, lambda e: e.tensor_tensor(out=ac[:, c0:TT], in0=ac[:, c0:TT], in1=pT[kp][:, c0:TT], op=ALU.add),
                             reads=[b_pT[kp], b_ac], writes=[b_ac])

            deferred = []

            def finalize(hd, T, g):
                i = hd % 2
                qsl = slice(T * TT, (T + 1) * TT)
                ops_, b_ops, dps, b_dps = obanks(T)
                zi = T % 2
                K.op(K.pe, lambda e: e.matmul(dps[:], lhsT=onesf[:], rhs=acc[T % 2][:], start=False, stop=True),
                     reads=[b_onesf, b_acc[T % 2]], writes=[b_dps], inc=True)

                def quarter(q):
                    cs = slice(q * 128, (q + 1) * 128)
                    K.op(K.dve, lambda e: e.reciprocal(out=rden[:, cs], in_=dps[:, cs]), reads=[b_dps], writes=[b_rden])
                    K.op(K.dve, lambda e: e.tensor_tensor(out=ot[:, cs], in0=ops_[:, cs], in1=rden[:, cs], op=ALU.mult),
                         reads=[b_ops, b_rden], writes=[b_ot])
                    if q == 3:
                        K.op(K.pool, lambda e: e.tensor_tensor(out=zo[zi][:], in0=ot[:], in1=sgh[i][:, qsl], op=ALU.mult),
                             reads=[b_ot, b_hd[i]], writes=[b_zo[zi]])
                        K.dma(K.sp, z_v[hd, :, qsl], zo[zi][:], b_zo[zi], b_zd, b_zo[zi])

                for q in range(4):
                    deferred.append((g + 1 + q, lambda q=q: quarter(q)))
                deferred.sort(key=lambda d: d[0])

            def run_deferred(g):
                while deferred and deferred[0][0] <= g:
                    deferred.pop(0)[1]()

            load_head(0)
            gating1(0)
            gating2(0)
            s_mm(0)
            s_mm(1)
            for g in range(NS):
                hd, T, kt = steps[g]
                if T == 1 and kt == 0 and hd + 1 < 8:
                    load_head(hd + 1)
                if g + 2 < NS:
                    s_mm(g + 2)
                pv_mm(g)
                run_deferred(g)
                if g >= 10 and (g - 10) % bg_every == 0:
                    self.bg_step()
                if kt == 4 * (T + 1) - 1:
                    finalize(hd, T, g)
                    if hd + 1 < 8 and T == 3:
                        gating1(hd + 1)
                    if hd + 1 < 8 and T == 5:
                        for g8 in range(8):
                            deferred.append((g + 1 + 2 * g8, lambda g8=g8, hd=hd: gating2(hd + 1, [g8])))
                        deferred.sort(key=lambda d: d[0])
            run_deferred(NS + 10)
            while self.bg_pieces:
                self.bg_step()

        with K.scope() as alloc:
            if prefetch_conv:
                wout, b_wout = self.load_w_bf16(alloc, p + "w_out", [D, D], 8, D)
                self.stage_all()
            ht = [alloc("ht%d" % i, [128, 8, TT], F32) for i in range(2)]
            b_ht = [K.buf("ht%d" % i) for i in range(2)]
            zt = [alloc("zt%d" % i, [128, 8, TT], BF16) for i in range(2)]
            b_zt = [K.buf("zt%d" % i) for i in range(2)]
            zd_v = z_d.rearrange("(c p) t -> p c t", p=128)
            psrot = [0]
            if fuse_final:
                fg_sb, b_fg = self.load_f32(alloc, "final_gT", [128, 8])
                fsq = alloc("fsq", [128, 8, TT], BF16)
                b_fsq = K.buf("fsq")
                fstd = alloc("fstd", [128, TT], F32)
                b_fstd = K.buf("fstd")
                frstd = alloc("frstd", [128, TT], F32)
                b_frstd = K.buf("frstd")

            def load3(T):
                i = T % 2
                K.dma(K.sp, ht[i][:], hin_v[:, :, T * TT:(T + 1) * TT], b_in, b_ht[i], b_ht[i])
                K.dma(K.sp, zt[i][:], zd_v[:, :, T * TT:(T + 1) * TT], b_zd, b_zt[i], b_zt[i])

            load3(0)
            for T in range(NT):
                i = T % 2
                if T + 1 < NT:
                    load3(T + 1)
                for f in range(8):
                    k = psrot[0] % 6
                    psrot[0] += 1
                    pst, b_pst = self.ps[k], self.b_ps[k]
                    for c in range(8):
                        K.op(K.pe, lambda e: e.matmul(pst[:], lhsT=wout[:, c, f * 128:(f + 1) * 128], rhs=zt[i][:, c, :],
                                                      start=(c == 0), stop=(c == 7)),
                             reads=[b_wout, b_zt[i]], writes=[b_pst], inc=(c == 7))
                    K.op(K.dve, lambda e: e.tensor_tensor(out=ht[i][:, f, :], in0=pst[:], in1=ht[i][:, f, :], op=ALU.add),
                         reads=[b_pst, b_ht[i]], writes=[b_ht[i]])
                if fuse_final:
                    self.norm_stage(TT, ht[i], b_ht[i], fsq, b_fsq, self.ps[6], self.b_ps[6], fstd, b_fstd, frstd, b_frstd,
                                    ht[i], b_ht[i], fg_sb, b_fg)
                K.dma(K.sp, hout_v[:, :, T * TT:(T + 1) * TT], ht[i][:], b_ht[i], b_out, b_ht[i])

        pcm.__exit__(None, None, None)
        if mcm is not None:
            mcm.__exit__(None, None, None)
        return carry


def _consts():
    c = np.zeros((128, NCONST), np.float32)
    c[:, 0:128] = np.eye(128, dtype=np.float32)
    r = np.arange(128)[:, None]
    q = np.arange(512)[None, :]
    cm = np.zeros((128, 4, 512), np.float32)
    for a in range(4):
        kpos = 128 * a + r
        blk_k = kpos // BLK
        blk_q = q // BLK
        same = blk_k == blk_q
        cm[:, a, :] = np.where(same & (kpos > q), -BIG, 0.0)
    c[:, 128:2176] = cm.reshape(128, 2048)
    j = np.arange(32)[:, None]
    n = np.arange(NBLK)[None, :]
    cneg = np.where(n < (j // 2), 0.0, -1e30).astype(np.float32)
    ownb = np.where(n == (j // 2), BIG, 0.0).astype(np.float32)
    c[:, 2176:2688] = np.broadcast_to(cneg.reshape(1, 512), (128, 512))
    c[:, 2688:3200] = np.broadcast_to(ownb.reshape(1, 512), (128, 512))
    ind = np.zeros((16, 16, 128), np.float32)
    for nn in range(16):
        ind[nn, nn, :] = 1.0
    c[0:16, 3200:5248] = ind.reshape(16, 2048)
    return c


def _col8(v):
    return np.ascontiguousarray(np.asarray(v, np.float32).reshape(-1, 128).T)


def _layer_inputs(l, inp):
    p = "l%d_" % l
    d = {}
    d[p + "norm_gT"] = _col8(inp[p + "norm_g"])
    d[p + "w_in"] = np.ascontiguousarray(inp[p + "w_in"], dtype=np.float32)
    d[p + "w_out"] = np.ascontiguousarray(inp[p + "w_out"], dtype=np.float32)
    if l == 1:
        cw = np.asarray(inp[p + "conv_w"], np.float32).reshape(31, 8, 128).transpose(2, 0, 1)
        rest = np.stack([_col8(inp[p + k]) for k in ("conv_b", "ln_g", "ln_b")], axis=1)
        d[p + "vec"] = np.ascontiguousarray(np.concatenate([cw, rest], axis=1))
    if l == 2:
        cw = np.asarray(inp[p + "conv_w"], np.float32).reshape(4, 10, 128).transpose(2, 0, 1)
        rest = np.stack([_col8(inp[p + k]) for k in ("conv_b", "b_rg", "b_ig", "lam")], axis=1)
        d[p + "vec"] = np.ascontiguousarray(np.concatenate([cw, rest], axis=1))
        d[p + "w_rg"] = np.ascontiguousarray(inp[p + "w_rg"], dtype=np.float32)
        d[p + "w_ig"] = np.ascontiguousarray(inp[p + "w_ig"], dtype=np.float32)
    return d


def run_layers(layers, final_norm, hT_list, inp):
    prog = Prog(layers, final_norm, first=True)
    shared = {"consts": _consts()}
    for l in layers:
        shared.update(_layer_inputs(l, inp))
    if final_norm:
        shared["final_gT"] = _col8(inp["final_g"])
    in_maps = []
    for b in range(len(hT_list)):
        m = dict(shared)
        m["xT"] = hT_list[b]
        in_maps.append(m)
    res = run_bass_kernel_spmd(prog.nc, in_maps, core_ids=list(range(len(hT_list))))
    return [r["oT"] for r in res.results]


def kernel(**inputs):
    inp = {k: np.asarray(inputs[k]) for k in ALL_INPUTS}
    x = inp["x"]
    hT = [np.ascontiguousarray(x[b].T, dtype=np.float32) for b in range(NCORES)]
    if MODE == "fused":
        hT = run_layers([0, 1, 2, 3], True, hT, inp)
    else:
        for l in range(4):
            hT = run_layers([l], l == 3, hT, inp)
    return np.ascontiguousarray(np.stack([h.T for h in hT], axis=0)).astype(np.float32)
```

```python
import contextlib
import numpy as np
import concourse.bass as bass
import concourse.mybir as mybir
from concourse.bass_utils import run_bass_kernel_spmd

F32 = mybir.dt.float32
BF16 = mybir.dt.bfloat16
AF = mybir.ActivationFunctionType
ALU = mybir.AluOpType

S = 4096
D = 1024
EPS = 1e-6
NCORES = 8
NBLK = 16
BLK = 256
BIG = 30000.0
LW = 1280
NCONST = 128 + 2048 + 512 + 512 + 2048

ALL_INPUTS = (
    "x", "l0_norm_g", "l0_w_in", "l0_w_out",
    "l1_norm_g", "l1_w_in", "l1_conv_w", "l1_conv_b", "l1_ln_g", "l1_ln_b", "l1_w_out",
    "l2_norm_g", "l2_w_in", "l2_conv_w", "l2_conv_b", "l2_w_rg", "l2_b_rg", "l2_w_ig", "l2_b_ig", "l2_lam", "l2_w_out",
    "l3_norm_g", "l3_w_in", "l3_w_out", "final_g",
)

HALF_DIAG = False
MODE = "fused"


class SemRec:
    def __init__(self, sem):
        self.sem = sem
        self.cnt = 0


class Buf:
    def __init__(self, name):
        self.name = name
        self.w = {}
        self.r = {}
        self.ds = None


class Eng:
    def __init__(self, nc, name, h, same_sync):
        self.name = name
        self.h = h
        self.sem = nc.alloc_semaphore(name="e_" + name)
        self.cnt = 0
        self.seen = {}
        self.same_sync = same_sync


def _add(d, tok):
    k = id(tok[0])
    if k not in d or d[k][1] < tok[1]:
        d[k] = tok


class Trk:
    def __init__(self, nc):
        self.nc = nc
        self.pe = Eng(nc, "pe", nc.tensor, False)
        self.act = Eng(nc, "act", nc.scalar, True)
        self.dve = Eng(nc, "dve", nc.vector, True)
        self.pool = Eng(nc, "pool", nc.gpsimd, True)
        self.sp = Eng(nc, "sp", nc.sync, False)
        self.engs = [self.pe, self.act, self.dve, self.pool, self.sp]
        self.free_ds = []
        self.all_ds = []
        self.scope_bufs = []
        self.uid = 0

    def buf(self, name):
        b = Buf(name)
        if self.scope_bufs:
            self.scope_bufs[-1].append(b)
        return b

    def _wait(self, eng, deps):
        for (s, v) in deps.values():
            k = id(s)
            if eng.seen.get(k, 0) >= v:
                continue
            if s is eng.sem and v > eng.cnt:
                continue
            eng.h.wait_ge(s, v)
            eng.seen[k] = v

    def op(self, eng, fn, reads=(), writes=(), inc=True):
        deps = {}
        for b in reads:
            for t in b.w.values():
                if t[0] is eng.sem and not eng.same_sync:
                    continue
                _add(deps, t)
        for b in writes:
            for t in b.w.values():
                if t[0] is not eng.sem or eng.same_sync:
                    _add(deps, t)
            for t in b.r.values():
                if t[0] is not eng.sem or eng.same_sync:
                    _add(deps, t)
        self._wait(eng, deps)
        ins = fn(eng.h)
        if inc:
            eng.cnt += 1
            ins.then_inc(eng.sem, 1)
            tok = (eng.sem, eng.cnt)
        else:
            tok = (eng.sem, eng.cnt + 1)
        for b in reads:
            _add(b.r, tok)
        for b in writes:
            _add(b.w, tok)
        return ins

    def dma(self, q, out_ap, in_ap, src, dst, sembuf, **kw):
        srcs = list(src) if isinstance(src, (list, tuple)) else [src]
        deps = {}
        for sb_ in srcs:
            for t in sb_.w.values():
                _add(deps, t)
        for t in dst.w.values():
            _add(deps, t)
        for t in dst.r.values():
            _add(deps, t)
        self._wait(q, deps)
        if sembuf.ds is None:
            if self.free_ds:
                sembuf.ds = self.free_ds.pop()
            else:
                sembuf.ds = SemRec(self.nc.alloc_semaphore(name="d%d" % len(self.all_ds)))
                self.all_ds.append(sembuf.ds)
        ds = sembuf.ds
        ins = q.h.dma_start(out=out_ap, in_=in_ap, **kw)
        ds.cnt += 16
        ins.then_inc(ds.sem, 16)
        tok = (ds.sem, ds.cnt)
        for sb_ in srcs:
            _add(sb_.r, tok)
        _add(dst.w, tok)
        return ins

    def barrier(self):
        for e in self.engs:
            deps = {}
            for f in self.engs:
                if f is not e and f.cnt > 0:
                    _add(deps, (f.sem, f.cnt))
            for ds in self.all_ds:
                if ds.cnt > 0:
                    _add(deps, (ds.sem, ds.cnt))
            self._wait(e, deps)

    @contextlib.contextmanager
    def scope(self):
        es = contextlib.ExitStack()
        self.scope_bufs.append([])
        nc = self.nc

        def alloc(name, shape, dt):
            self.uid += 1
            return es.enter_context(nc.sbuf_tensor("%s_u%d" % (name, self.uid), shape, dt))

        try:
            yield alloc
            self.barrier()
            for b in self.scope_bufs[-1]:
                if b.ds is not None:
                    self.free_ds.append(b.ds)
                    b.ds = None
        finally:
            self.scope_bufs.pop()
            es.close()


class Prog:
    def __init__(self, layers, final_norm, first, name_in="xT", name_out="oT"):
        nc = bass.Bass("TRN2", target_bir_lowering=False)
        self.nc = nc
        self.K = Trk(nc)
        K = self.K
        self.inputs = {}
        self.pending = []
        self.bg_pieces = []
        self.bg_n = 0

        def ext(name, shape, dt=F32):
            self.inputs[name] = shape
            return nc.dram_tensor(name, list(shape), dt, kind="ExternalInput").ap()

        self.ext = ext
        self.h_in = ext(name_in, [D, S])
        self.h_out = nc.dram_tensor(name_out, [D, S], F32, kind="ExternalOutput").ap()
        self.b_hin = Buf("h_in")
        self.b_hout = Buf("h_out")
        self.h_scr = None
        consts = ext("consts", [128, NCONST])

        self.ps = [nc.alloc_psum_tensor("ps%d" % i, [128, 512], F32) for i in range(7)]
        self.b_ps = [Buf("ps%d" % i) for i in range(7)]
        self.pg = nc.alloc_psum_tensor("pg", [128, 512], F32)
        self.b_pg = Buf("pg")

        self.ones = nc.alloc_sbuf_tensor("ones", [128, 128], BF16)
        self.ident = nc.alloc_sbuf_tensor("ident", [128, 128], BF16)
        self.identf = nc.alloc_sbuf_tensor("identf", [128, 128], F32)
        self.b_const = Buf("const")
        self.consts_d = consts
        with K.scope() as alloc:
            cst = alloc("cst", [128, 128], F32)
            b_cst = K.buf("cst")
            b_cd = Buf("consts_d")
            K.dma(K.sp, cst[:], consts[:, 0:128], b_cd, b_cst, b_cst)
            K.op(K.dve, lambda e: e.memset(self.ones[:], 1.0), writes=[self.b_const])
            K.op(K.dve, lambda e: e.tensor_copy(out=self.ident[:], in_=cst[:]), reads=[b_cst], writes=[self.b_const])
            K.op(K.dve, lambda e: e.tensor_copy(out=self.identf[:], in_=cst[:]), reads=[b_cst], writes=[self.b_const])

        cur_in, b_in = self.h_in, self.b_hin
        fuse_fn = final_norm and len(layers) > 0 and layers[-1] in (0, 3)
        if fuse_fn:
            final_norm = False
        carry = None
        for li, l in enumerate(layers):
            last = (li == len(layers) - 1) and not final_norm
            if last:
                cur_out, b_out = self.h_out, self.b_hout
            else:
                if self.h_scr is None:
                    self.h_scr = nc.dram_tensor("h_scr", [D, S], F32, kind="Internal").ap()
                    self.b_hscr = Buf("h_scr")
                cur_out, b_out = self.h_scr, self.b_hscr
            if l in (0, 3):
                nxt = layers[li + 1] if li + 1 < len(layers) else None
                carry = self.attn_layer(l, cur_in, b_in, cur_out, b_out, fuse_final=(fuse_fn and li == len(layers) - 1),
                                        prefetch_conv=(nxt == 1))
            elif l == 1:
                self.conv_layer(l, cur_in, b_in, cur_out, b_out, pre=carry)
                if carry is not None:
                    carry["cm"].__exit__(None, None, None)
                carry = None
            else:
                self.lru_layer(l, cur_in, b_in, cur_out, b_out)
            cur_in, b_in = cur_out, b_out
        if final_norm:
            self.final_norm(cur_in, b_in, self.h_out, self.b_hout)
        deps = {}
        for t in self.b_hout.w.values():
            _add(deps, t)
        K._wait(K.sp, deps)

    def load_w_bf16(self, alloc, name, shape_in, kchunks, ncols):
        K = self.K
        w_d = self.ext(name, shape_in)
        wb = alloc(name + "_sb", [128, kchunks, ncols], BF16)
        b = K.buf(name)
        for c in range(kchunks):
            for n0 in range(0, ncols, 2048):
                n1 = min(ncols, n0 + 2048)
                self.pending.append((wb[:, c, n0:n1], w_d[c * 128:(c + 1) * 128, n0:n1], b, 128, n1 - n0))
        return wb, b

    def stage_all(self):
        K = self.K
        if not self.pending:
            return
        with K.scope() as alloc:
            NS = 8
            stg = [alloc("stg%d" % k, [128, 2048], F32) for k in range(NS)]
            b_stg = [K.buf("stg%d" % k) for k in range(NS)]
            bd = Buf("wdram")
            engs = [K.dve, K.act]
            for n, (dst, src, b, npart, ncol) in enumerate(self.pending):
                k = n % NS
                sv = stg[k][0:npart, 0:ncol]
                if len(dst.shape) == 3:
                    sv = sv.rearrange("p (a b) -> p a b", b=dst.shape[2])
                K.dma(K.sp, sv, src, bd, b_stg[k], b_stg[k])
                eng = engs[n % 2]
                if eng is K.act:
                    K.op(eng, lambda e: e.activation(out=dst, in_=sv, func=AF.Copy), reads=[b_stg[k]], writes=[b])
                else:
                    K.op(eng, lambda e: e.tensor_copy(out=dst, in_=sv), reads=[b_stg[k]], writes=[b])
        self.pending = []

    def bg_register(self, name, shape_in, kchunks, ncols, alloc):
        K = self.K
        w_d = self.ext(name, shape_in)
        wb = alloc(name + "_sb", [128, kchunks, ncols], BF16)
        b = K.buf(name)
        for c in range(kchunks):
            for n0 in range(0, ncols, 1024):
                n1 = min(ncols, n0 + 1024)
                self.bg_pieces.append((wb[:, c, n0:n1], w_d[c * 128:(c + 1) * 128, n0:n1], b, 128, n1 - n0))
        return wb, b

    def bg_step(self):
        K = self.K
        if not self.bg_pieces:
            return
        dst, src, b, npart, ncol = self.bg_pieces.pop(0)
        k = self.bg_n % 2
        self.bg_n += 1
        stg, b_stg = self.bg_stg[k], self.b_bg_stg[k]
        K.dma(K.sp, stg[0:npart, 0:ncol], src, Buf("wdram"), b_stg, b_stg)
        K.op(K.pool, lambda e: e.tensor_copy(out=dst, in_=stg[0:npart, 0:ncol]), reads=[b_stg], writes=[b])

    def load_f32(self, alloc, name, shape):
        K = self.K
        d = self.ext(name, shape)
        t = alloc(name + "_sb", list(shape), F32)
        b = K.buf(name)
        K.dma(K.sp, t[:], d, Buf(name + "_d"), b, b)
        return t, b

    def norm_stage(self, TT, ht, b_ht, sq, b_sq, pstat, b_pstat, std, b_std, rstd, b_rstd, ub, b_ub, g_sb, b_g):
        K = self.K
        K.op(K.act, lambda e: e.activation(out=sq[:], in_=ht[:], func=AF.Square), reads=[b_ht], writes=[b_sq])
        for c in range(8):
            K.op(K.pe, lambda e: e.matmul(pstat[:, 0:TT], lhsT=self.ones[:], rhs=sq[:, c, :], start=(c == 0), stop=(c == 7)),
                 reads=[self.b_const, b_sq], writes=[b_pstat], inc=(c == 7))
        K.op(K.act, lambda e: e.activation(out=std[:], in_=pstat[:, 0:TT], func=AF.Sqrt, bias=EPS, scale=1.0 / D),
             reads=[b_pstat], writes=[b_std])
        K.op(K.dve, lambda e: e.reciprocal(out=rstd[:], in_=std[:]), reads=[b_std], writes=[b_rstd])
        for c in range(8):
            K.op(K.dve, lambda e: e.scalar_tensor_tensor(out=ub[:, c, :], in0=ht[:, c, :], scalar=g_sb[:, c:c + 1],
                                                          in1=rstd[:], op0=ALU.mult, op1=ALU.mult),
                 reads=[b_ht, b_g, b_rstd], writes=[b_ub])

    def final_norm(self, h_in, b_in, h_out, b_out):
        K = self.K
        TT = 512
        NT = S // TT
        hin_v = h_in.rearrange("(c p) t -> p c t", p=128)
        hout_v = h_out.rearrange("(c p) t -> p c t", p=128)
        with K.scope() as alloc:
            g_sb, b_g = self.load_f32(alloc, "final_gT", [128, 8])
            ht = [alloc("fn_ht%d" % i, [128, 8, TT], F32) for i in range(2)]
            b_ht = [K.buf("fn_ht%d" % i) for i in range(2)]
            sq = alloc("fn_sq", [128, 8, TT], BF16)
            b_sq = K.buf("fn_sq")
            std = alloc("fn_std", [128, TT], F32)
            b_std = K.buf("fn_std")
            rstd = alloc("fn_rstd", [128, TT], F32)
            b_rstd = K.buf("fn_rstd")
            K.dma(K.sp, ht[0][:], hin_v[:, :, 0:TT], b_in, b_ht[0], b_ht[0])
            for T in range(NT):
                i = T % 2
                sl = slice(T * TT, (T + 1) * TT)
                if T + 1 < NT:
                    K.dma(K.sp, ht[1 - i][:], hin_v[:, :, (T + 1) * TT:(T + 2) * TT], b_in, b_ht[1 - i], b_ht[1 - i])
                self.norm_stage(TT, ht[i], b_ht[i], sq, b_sq, self.ps[6], self.b_ps[6], std, b_std, rstd, b_rstd,
                                ht[i], b_ht[i], g_sb, b_g)
                K.dma(K.sp, hout_v[:, :, sl], ht[i][:], b_ht[i], b_out, b_ht[i])

    def lru_layer(self, l, h_in, b_in, h_out, b_out):
        K = self.K
        nc = self.nc
        TT = 256
        NT = S // TT
        NC = LW // 128
        p = "l%d_" % l
        hin_v = h_in.rearrange("(c p) t -> p c t", p=128)
        hout_v = h_out.rearrange("(c p) t -> p c t", p=128)
        with K.scope() as alloc:
            win, b_win = self.load_w_bf16(alloc, p + "w_in", [D, 2 * LW], 8, 2 * LW)
            wout, b_wout = self.load_w_bf16(alloc, p + "w_out", [LW, D], NC, D)
            g_sb, b_g = self.load_f32(alloc, p + "norm_gT", [128, 8])
            vec, b_vec = self.load_f32(alloc, p + "vec", [128, 8, NC])
            wrg_d = self.ext(p + "w_rg", [NC, 128, 128])
            wig_d = self.ext(p + "w_ig", [NC, 128, 128])
            wrg = alloc("wrg", [128, NC, 128], BF16)
            wig = alloc("wig", [128, NC, 128], BF16)
            b_wg = K.buf("wg")
            self.pending.append((wrg[:], wrg_d.rearrange("h i j -> i h j"), b_wg, 128, NC * 128))
            self.pending.append((wig[:], wig_d.rearrange("h i j -> i h j"), b_wg, 128, NC * 128))
            self.stage_all()
            cl = alloc("cl", [128, 2, NC], F32)
            tmpv = alloc("tmpv", [128, 2, NC], F32)
            b_cl = K.buf("cl")
            K.op(K.act, lambda e: e.activation(out=tmpv[:, 0, :], in_=vec[:, 7, :], func=AF.Exp, scale=-1.0), reads=[b_vec], writes=[b_cl])
            K.op(K.act, lambda e: e.activation(out=tmpv[:, 1, :], in_=tmpv[:, 0, :], func=AF.Ln, bias=1.0, scale=1.0), reads=[b_cl], writes=[b_cl])
            K.op(K.dve, lambda e: e.tensor_scalar(out=cl[:, 0, :], in0=tmpv[:, 1, :], scalar1=-8.0, scalar2=None, op0=ALU.mult),
                 reads=[b_cl], writes=[b_cl])
            K.op(K.dve, lambda e: e.tensor_scalar(out=cl[:, 1, :], in0=tmpv[:, 1, :], scalar1=-16.0, scalar2=None, op0=ALU.mult),
                 reads=[b_cl], writes=[b_cl])
            dg = alloc("dg", [128, 4, NC, 128], BF16)
            b_dg = K.buf("dg")
            for j in range(4):
                for c in range(NC):
                    K.op(K.dve, lambda e: e.tensor_scalar(out=dg[:, j, c, :], in0=self.identf[:], scalar1=vec[:, j, c:c + 1],
                                                           scalar2=None, op0=ALU.mult),
                         reads=[b_vec, self.b_const], writes=[b_dg])
            ht = [alloc("ht%d" % i, [128, 8, TT], F32) for i in range(3)]
            b_ht = [K.buf("ht%d" % i) for i in range(3)]
            sq = alloc("sq", [128, 8, TT], BF16)
            b_sq = K.buf("sq")
            std = alloc("std", [128, TT], F32)
            b_std = K.buf("std")
            rstd = alloc("rstd", [128, TT], F32)
            b_rstd = K.buf("rstd")
            ub = [alloc("ub%d" % i, [128, 8, TT], BF16) for i in range(2)]
            b_ub = [K.buf("ub%d" % i) for i in range(2)]
            xbuf = [alloc("xbuf%d" % i, [128, NC, TT + 3], BF16) for i in range(2)]
            b_xbuf = [K.buf("xbuf%d" % i) for i in range(2)]
            sg = [alloc("sg%d" % i, [128, NC, TT], BF16) for i in range(2)]
            b_sg = [K.buf("sg%d" % i) for i in range(2)]
            xc = alloc("xc", [128, NC, TT], F32)
            xcb = alloc("xcb", [128, NC, TT], BF16)
            b_xc = [K.buf("xc%d" % c) for c in range(NC)]
            b_xcb = [K.buf("xcb%d" % c) for c in range(NC)]
            hs = [alloc("hs%d" % i, [128, NC, TT], F32) for i in range(2)]
            b_hs = [[K.buf("hs%d_%d" % (i, c)) for c in range(NC)] for i in range(2)]
            zt = alloc("zt", [128, NC, TT], BF16)
            b_zt = K.buf("zt")
            HC = 5
            NR = 3
            r5 = alloc("r5", [128, HC, TT], F32)
            i5 = alloc("i5", [128, HC, TT], F32)
            a5 = alloc("a5", [128, HC, TT], F32)
            s5 = alloc("s5", [128, HC, TT], F32)
            b_r5 = [K.buf("r5_%d" % k) for k in range(HC)]
            b_i5 = [K.buf("i5_%d" % k) for k in range(HC)]
            b_a5 = [K.buf("a5_%d" % k) for k in range(HC)]
            b_s5 = [K.buf("s5_%d" % k) for k in range(HC)]
            gx5 = alloc("gx5", [128, HC, TT], F32)
            b_gx5 = K.buf("gx5")
            bt5 = alloc("bt5", [128, HC, TT], F32)
            b_bt5 = K.buf("bt5")
            K.op(K.pool, lambda e: e.memset(xbuf[0][:, :, 0:3], 0.0), writes=[b_xbuf[0]])
            psrot = [0]

            def nextps():
                k = psrot[0] % 6
                psrot[0] += 1
                return self.ps[k], self.b_ps[k]

            def loadA(T):
                K.dma(K.sp, ht[T % 3][:], hin_v[:, :, T * TT:(T + 1) * TT], b_in, b_ht[T % 3], b_ht[T % 3])

            def stageA(T):
                i = T % 2
                self.norm_stage(TT, ht[T % 3], b_ht[T % 3], sq, b_sq, self.ps[6], self.b_ps[6], std, b_std, rstd, b_rstd,
                                ub[i], b_ub[i], g_sb, b_g)

            def stageB(T, part):
                i = T % 2
                for f in range(part * NC, (part + 1) * NC):
                    pst, b_pst = nextps()
                    for c in range(8):
                        K.op(K.pe, lambda e: e.matmul(pst[:, 0:TT], lhsT=win[:, c, f * 128:(f + 1) * 128], rhs=ub[i][:, c, :],
                                                      start=(c == 0), stop=(c == 7)),
                             reads=[b_win, b_ub[i]], writes=[b_pst], inc=(c == 7))
                    if f < NC:
                        K.op(K.act, lambda e: e.activation(out=xbuf[i][:, f, 3:3 + TT], in_=pst[:, 0:TT], func=AF.Copy),
                             reads=[b_pst], writes=[b_xbuf[i]])
                    else:
                        K.op(K.act, lambda e: e.activation(out=sg[i][:, f - NC, :], in_=pst[:, 0:TT], func=AF.Silu),
                             reads=[b_pst], writes=[b_sg[i]])
                if part == 0 and T + 1 < NT:
                    K.op(K.pool, lambda e: e.tensor_copy(out=xbuf[1 - i][:, :, 0:3], in_=xbuf[i][:, :, TT:TT + 3]),
                         reads=[b_xbuf[i]], writes=[b_xbuf[1 - i]])

            def stageC(T):
                i = T % 2
                for c in range(NC):
                    pst, b_pst = nextps()
                    for j in range(4):
                        K.op(K.pe, lambda e: e.matmul(pst[:, 0:TT], lhsT=dg[:, j, c, :], rhs=xbuf[i][:, c, j:j + TT],
                                                      start=(j == 0), stop=(j == 3)),
                             reads=[b_dg, b_xbuf[i]], writes=[b_pst], inc=(j == 3))
                    K.op(K.act, lambda e: e.activation(out=xc[:, c, :], in_=pst[:, 0:TT], func=AF.Identity, bias=vec[:, 4, c:c + 1], scale=1.0),
                         reads=[b_pst, b_vec], writes=[b_xc[c]])
                    K.op(K.dve, lambda e: e.tensor_copy(out=xcb[:, c, :], in_=xc[:, c, :]), reads=[b_xc[c]], writes=[b_xcb[c]])

            def stageD(T, h):
                i = T % 2
                if True:
                    cs = list(range(h * HC, (h + 1) * HC))
                    for k, c in enumerate(cs):
                        psr, b_psr = nextps()
                        K.op(K.pe, lambda e: e.matmul(psr[:, 0:TT], lhsT=wrg[:, c, :], rhs=xcb[:, c, :], start=True, stop=True),
                             reads=[b_wg, b_xcb[c]], writes=[b_psr])
                        psi, b_psi = nextps()
                        K.op(K.pe, lambda e: e.matmul(psi[:, 0:TT], lhsT=wig[:, c, :], rhs=xcb[:, c, :], start=True, stop=True),
                             reads=[b_wg, b_xcb[c]], writes=[b_psi])
                        K.op(K.act, lambda e: e.activation(out=r5[:, k, :], in_=psr[:, 0:TT], func=AF.Sigmoid, bias=vec[:, 5, c:c + 1], scale=1.0),
                             reads=[b_psr, b_vec], writes=[b_r5[k]])
                        K.op(K.act, lambda e: e.activation(out=i5[:, k, :], in_=psi[:, 0:TT], func=AF.Sigmoid, bias=vec[:, 6, c:c + 1], scale=1.0),
                             reads=[b_psi, b_vec], writes=[b_i5[k]])
                    for k, c in enumerate(cs):
                        K.op(K.act, lambda e: e.activation(out=a5[:, k, :], in_=r5[:, k, :], func=AF.Exp, scale=cl[:, 0, c:c + 1]),
                             reads=[b_r5[k], b_cl], writes=[b_a5[k]])
                    cs0, cs1 = cs[0], cs[-1] + 1
                    K.op(K.pool, lambda e: e.tensor_tensor(out=s5[:], in0=a5[:], in1=a5[:], op=ALU.mult),
                         reads=b_a5, writes=b_s5)
                    for k, c in enumerate(cs):
                        K.op(K.act, lambda e: e.activation(out=s5[:, k, :], in_=s5[:, k, :], func=AF.Sqrt, bias=1.0, scale=-1.0),
                             reads=[b_s5[k]], writes=[b_s5[k]])
                    K.op(K.pool, lambda e: e.tensor_tensor(out=gx5[:], in0=i5[:], in1=xc[:, cs0:cs1, :], op=ALU.mult),
                         reads=b_i5 + [b_xc[c] for c in cs], writes=[b_gx5])
                    K.op(K.dve, lambda e: e.tensor_tensor(out=bt5[:], in0=s5[:], in1=gx5[:], op=ALU.mult),
                         reads=b_s5 + [b_gx5], writes=[b_bt5])
                    for k, c in enumerate(cs):
                        init = 0.0 if T == 0 else hs[1 - i][:, c, TT - 1:TT]
                        rd = [b_a5[k], b_bt5] + ([] if T == 0 else [b_hs[1 - i][c]])
                        K.op(K.dve, lambda e: e.tensor_tensor_scan(out=hs[i][:, c, :], data0=a5[:, k, :], data1=bt5[:, k, :], initial=init,
                                                                    op0=ALU.mult, op1=ALU.add),
                             reads=rd, writes=[b_hs[i][c]])
                    K.op(K.pool, lambda e: e.tensor_tensor(out=zt[:, cs0:cs1, :], in0=hs[i][:, cs0:cs1, :], in1=sg[i][:, cs0:cs1, :], op=ALU.mult),
                         reads=[b_hs[i][c] for c in cs] + [b_sg[i]], writes=[b_zt])

            def stageE(T):
                i = T % 2
                for f in range(8):
                    pst, b_pst = nextps()
                    for c in range(NC):
                        K.op(K.pe, lambda e: e.matmul(pst[:, 0:TT], lhsT=wout[:, c, f * 128:(f + 1) * 128], rhs=zt[:, c, :],
                                                      start=(c == 0), stop=(c == NC - 1)),
                             reads=[b_wout, b_zt], writes=[b_pst], inc=(c == NC - 1))
                    K.op(K.dve, lambda e: e.tensor_tensor(out=ht[T % 3][:, f, :], in0=pst[:, 0:TT], in1=ht[T % 3][:, f, :], op=ALU.add),
                         reads=[b_pst, b_ht[T % 3]], writes=[b_ht[T % 3]])
                K.dma(K.sp, hout_v[:, :, T * TT:(T + 1) * TT], ht[T % 3][:], b_ht[T % 3], b_out, b_ht[T % 3])

            loadA(0)
            loadA(1)
            stageA(0)
            stageB(0, 0)
            stageB(0, 1)
            stageC(0)
            for T in range(NT):
                if T + 2 < NT:
                    loadA(T + 2)
                if T + 1 < NT:
                    stageA(T + 1)
                for h in range(2):
                    stageD(T, h)
                    if T + 1 < NT:
                        stageB(T + 1, h)
                if T + 1 < NT:
                    stageC(T + 1)
                stageE(T)

    def conv_layer(self, l, h_in, b_in, h_out, b_out, pre=None):
        K = self.K
        TT = 256
        NT = S // TT
        CW = 31
        p = "l%d_" % l
        hin_v = h_in.rearrange("(c p) t -> p c t", p=128)
        hout_v = h_out.rearrange("(c p) t -> p c t", p=128)
        with K.scope() as alloc:
            if pre is not None:
                win, b_win, wout, b_wout = pre["win"], pre["b_win"], pre["wout"], pre["b_wout"]
            else:
                win, b_win = self.load_w_bf16(alloc, p + "w_in", [D, 3 * D], 8, 3 * D)
                wout, b_wout = self.load_w_bf16(alloc, p + "w_out", [D, D], 8, D)
                self.stage_all()
            g_sb, b_g = self.load_f32(alloc, p + "norm_gT", [128, 8])
            vec, b_vec = self.load_f32(alloc, p + "vec", [128, CW + 3, 8])
            dg = alloc("dg", [128, CW, 8, 128], BF16)
            b_dgd = K.buf("dgd")
            b_dga = K.buf("dga")
            for j in range(CW):
                for c in range(8):
                    if (j * 8 + c) % 3 != 0:
                        K.op(K.dve, lambda e: e.tensor_scalar(out=dg[:, j, c, :], in0=self.ident[:], scalar1=vec[:, j, c:c + 1],
                                                               scalar2=None, op0=ALU.mult),
                             reads=[b_vec, self.b_const], writes=[b_dgd])
                    else:
                        K.op(K.act, lambda e: e.activation(out=dg[:, j, c, :], in_=self.identf[:], func=AF.Copy, scale=vec[:, j, c:c + 1]),
                             reads=[b_vec, self.b_const], writes=[b_dga])
            ht = [alloc("ht%d" % i, [128, 8, TT], F32) for i in range(2)]
            b_ht = [K.buf("ht%d" % i) for i in range(2)]
            sq = alloc("sq", [128, 8, TT], BF16)
            b_sq = K.buf("sq")
            std = alloc("std", [128, TT], F32)
            b_std = K.buf("std")
            rstd = alloc("rstd", [128, TT], F32)
            b_rstd = K.buf("rstd")
            ub = [alloc("ub%d" % i, [128, 8, TT], BF16) for i in range(2)]
            b_ub = [K.buf("ub%d" % i) for i in range(2)]
            H = CW - 1
            ybuf = [alloc("ybuf%d" % i, [128, 8, TT + H], BF16) for i in range(2)]
            b_ybuf = [K.buf("ybuf%d" % i) for i in range(2)]
            sgt = [alloc("sgt%d" % k, [128, TT], F32) for k in range(2)]
            b_sgt = [K.buf("sgt%d" % k) for k in range(2)]
            sg = [alloc("sg%d" % i, [128, 8, TT], BF16) for i in range(2)]
            b_sg = [K.buf("sg%d" % i) for i in range(2)]
            y2 = alloc("y2", [128, 8, TT], F32)
            y2b = alloc("y2b", [128, 8, TT], BF16)
            sq2 = alloc("sq2", [128, 8, TT], BF16)
            b_y2 = [K.buf("y2_%d" % c) for c in range(8)]
            st = {n: alloc("st_" + n, [128, TT], F32) for n in ["mean", "var", "rstd2"]}
            st["msq"] = st["var"]
            st["std2"] = st["var"]
            st["mr"] = st["mean"]
            b_st = {n: K.buf("st_" + n) for n in ["mean", "var", "rstd2"]}
            b_st["msq"] = b_st["var"]
            b_st["std2"] = b_st["var"]
            b_st["mr"] = b_st["mean"]
            ost = [alloc("ost%d" % k, [128, TT], F32) for k in range(2)]
            b_ost = [K.buf("ost%d" % k) for k in range(2)]
            tn = [alloc("tn%d" % k, [128, TT], F32) for k in range(2)]
            b_tn = [K.buf("tn%d" % k) for k in range(2)]
            sn = [alloc("sn%d" % k, [128, TT], F32) for k in range(2)]
            b_sn = [K.buf("sn%d" % k) for k in range(2)]
            zt = alloc("zt", [128, 8, TT], BF16)
            b_zt = K.buf("zt")
            K.op(K.pool, lambda e: e.memset(ybuf[0][:, :, 0:H], 0.0), writes=[b_ybuf[0]])
            psrot = [0]

            def nextps():
                k = psrot[0] % 5
                psrot[0] += 1
                return self.ps[k], self.b_ps[k]

            def loadA(T):
                i = T % 2
                K.dma(K.sp, ht[i][:], hin_v[:, :, T * TT:(T + 1) * TT], b_in, b_ht[i], b_ht[i])

            def stageA(T):
                i = T % 2
                self.norm_stage(TT, ht[i], b_ht[i], sq, b_sq, self.ps[6], self.b_ps[6], std, b_std, rstd, b_rstd,
                                ub[i], b_ub[i], g_sb, b_g)

            def proj(i, f):
                pst, b_pst = nextps()
                for c in range(8):
                    K.op(K.pe, lambda e: e.matmul(pst[:, 0:TT], lhsT=win[:, c, f * 128:(f + 1) * 128], rhs=ub[i][:, c, :],
                                                  start=(c == 0), stop=(c == 7)),
                         reads=[b_win, b_ub[i]], writes=[b_pst], inc=(c == 7))
                return pst, b_pst

            def stageB(T, f):
                i = T % 2
                if True:
                    k = f % 2
                    psb_, b_psb_ = proj(i, 8 + f)
                    K.op(K.act, lambda e: e.activation(out=sgt[k][:], in_=psb_[:, 0:TT], func=AF.Sigmoid),
                         reads=[b_psb_], writes=[b_sgt[k]])
                    psa, b_psa = proj(i, f)
                    K.op(K.dve, lambda e: e.tensor_tensor(out=ybuf[i][:, f, H:H + TT], in0=psa[:, 0:TT], in1=sgt[k][:], op=ALU.mult),
                         reads=[b_psa, b_sgt[k]], writes=[b_ybuf[i]])
                    psg, b_psg = proj(i, 16 + f)
                    K.op(K.act, lambda e: e.activation(out=sgt[1 - k][:], in_=psg[:, 0:TT], func=AF.Sigmoid),
                         reads=[b_psg], writes=[b_sgt[1 - k]])
                    K.op(K.dve, lambda e: e.tensor_tensor(out=sg[i][:, f, :], in0=psg[:, 0:TT], in1=sgt[1 - k][:], op=ALU.mult),
                         reads=[b_psg, b_sgt[1 - k]], writes=[b_sg[i]])
                if f == 7 and T + 1 < NT:
                    K.op(K.pool, lambda e: e.tensor_copy(out=ybuf[1 - i][:, :, 0:H], in_=ybuf[i][:, :, TT:TT + H]),
                         reads=[b_ybuf[i]], writes=[b_ybuf[1 - i]])

            def stageC(T, chunks):
                i = T % 2
                for c in chunks:
                    pst, b_pst = nextps()
                    for j in range(CW):
                        K.op(K.pe, lambda e: e.matmul(pst[:, 0:TT], lhsT=dg[:, j, c, :], rhs=ybuf[i][:, c, j:j + TT],
                                                      start=(j == 0), stop=(j == CW - 1)),
                             reads=[b_dgd, b_dga, b_ybuf[i]], writes=[b_pst], inc=(j == CW - 1))
                    K.op(K.act, lambda e: e.activation(out=y2[:, c, :], in_=pst[:, 0:TT], func=AF.Identity, bias=vec[:, CW, c:c + 1], scale=1.0),
                         reads=[b_pst, b_vec], writes=[b_y2[c]])
                    K.op(K.act, lambda e: e.activation(out=sq2[:, c, :], in_=pst[:, 0:TT], func=AF.Square, bias=vec[:, CW, c:c + 1], scale=1.0),
                         reads=[b_pst, b_vec], writes=[b_y2[c]])
                    K.op(K.pool, lambda e: e.tensor_copy(out=y2b[:, c, :], in_=y2[:, c, :]), reads=[b_y2[c]], writes=[b_y2[c]])

            def stageD(T):
                i = T % 2
                ps1, b_ps1 = self.ps[5], self.b_ps[5]
                ps2, b_ps2 = self.ps[6], self.b_ps[6]
                for c in range(8):
                    K.op(K.pe, lambda e: e.matmul(ps1[:, 0:TT], lhsT=self.ones[:], rhs=y2b[:, c, :], start=(c == 0), stop=(c == 7)),
                         reads=[self.b_const, b_y2[c]], writes=[b_ps1], inc=(c == 7))
                for c in range(8):
                    K.op(K.pe, lambda e: e.matmul(ps2[:, 0:TT], lhsT=self.ones[:], rhs=sq2[:, c, :], start=(c == 0), stop=(c == 7)),
                         reads=[self.b_const, b_y2[c]], writes=[b_ps2], inc=(c == 7))
                K.op(K.dve, lambda e: e.tensor_scalar(out=st["mean"][:], in0=ps1[:, 0:TT], scalar1=1.0 / D, scalar2=None, op0=ALU.mult),
                     reads=[b_ps1], writes=[b_st["mean"]])
                K.op(K.dve, lambda e: e.tensor_tensor(out=st["msq"][:], in0=st["mean"][:], in1=st["mean"][:], op=ALU.mult),
                     reads=[b_st["mean"]], writes=[b_st["msq"]])
                K.op(K.dve, lambda e: e.scalar_tensor_tensor(out=st["var"][:], in0=ps2[:, 0:TT], scalar=1.0 / D, in1=st["msq"][:],
                                                              op0=ALU.mult, op1=ALU.subtract),
                     reads=[b_ps2, b_st["msq"]], writes=[b_st["var"]])
                K.op(K.dve, lambda e: e.tensor_scalar(out=st["var"][:], in0=st["var"][:], scalar1=0.0, scalar2=None, op0=ALU.max),
                     reads=[b_st["var"]], writes=[b_st["var"]])
                K.op(K.act, lambda e: e.activation(out=st["std2"][:], in_=st["var"][:], func=AF.Sqrt, bias=EPS, scale=1.0),
                     reads=[b_st["var"]], writes=[b_st["std2"]])
                K.op(K.dve, lambda e: e.reciprocal(out=st["rstd2"][:], in_=st["std2"][:]), reads=[b_st["std2"]], writes=[b_st["rstd2"]])
                K.op(K.dve, lambda e: e.tensor_tensor(out=st["mr"][:], in0=st["mean"][:], in1=st["rstd2"][:], op=ALU.mult),
                     reads=[b_st["mean"], b_st["rstd2"]], writes=[b_st["mr"]])

            def stageDn(T, c):
                i = T % 2
                if True:
                    k = c % 2
                    K.op(K.dve, lambda e: e.tensor_tensor(out=tn[k][:], in0=y2[:, c, :], in1=st["rstd2"][:], op=ALU.mult),
                         reads=[b_y2[c], b_st["rstd2"]], writes=[b_tn[k]])
                    K.op(K.dve, lambda e: e.tensor_tensor(out=tn[k][:], in0=tn[k][:], in1=st["mr"][:], op=ALU.subtract),
                         reads=[b_tn[k], b_st["mr"]], writes=[b_tn[k]])
                    K.op(K.act, lambda e: e.activation(out=sn[k][:], in_=tn[k][:], func=AF.Sigmoid, bias=vec[:, CW + 2, c:c + 1],
                                                       scale=vec[:, CW + 1, c:c + 1]),
                         reads=[b_tn[k], b_vec], writes=[b_sn[k]])
                    K.op(K.dve, lambda e: e.tensor_scalar(out=tn[k][:], in0=tn[k][:], scalar1=vec[:, CW + 1, c:c + 1],
                                                           scalar2=vec[:, CW + 2, c:c + 1], op0=ALU.mult, op1=ALU.add),
                         reads=[b_tn[k], b_vec], writes=[b_tn[k]])
                    K.op(K.pool, lambda e: e.tensor_tensor(out=sn[k][:], in0=sn[k][:], in1=tn[k][:], op=ALU.mult),
                         reads=[b_sn[k], b_tn[k]], writes=[b_sn[k]])
                    K.op(K.pool, lambda e: e.tensor_tensor(out=zt[:, c, :], in0=sn[k][:], in1=sg[i][:, c, :], op=ALU.mult),
                         reads=[b_sn[k], b_sg[i]], writes=[b_zt])

            def stageE(T):
                i = T % 2
                for f in range(8):
                    pst, b_pst = nextps()
                    for c in range(8):
                        K.op(K.pe, lambda e: e.matmul(pst[:, 0:TT], lhsT=wout[:, c, f * 128:(f + 1) * 128], rhs=zt[:, c, :],
                                                      start=(c == 0), stop=(c == 7)),
                             reads=[b_wout, b_zt], writes=[b_pst], inc=(c == 7))
                    k2 = f % 2
                    K.op(K.dve, lambda e: e.tensor_tensor(out=ost[k2][:], in0=pst[:, 0:TT], in1=ht[i][:, f, :], op=ALU.add),
                         reads=[b_pst, b_ht[i]], writes=[b_ost[k2]])
                    K.dma(K.sp, hout_v[:, f, T * TT:(T + 1) * TT], ost[k2][:], b_ost[k2], b_out, b_ost[k2])

            loadA(0)
            stageA(0)
            for f in range(8):
                stageB(0, f)
            for T in range(NT):
                if T + 1 < NT:
                    loadA(T + 1)
                stageC(T, range(0, 4) if T == 0 else range(2, 4))
                if T + 1 < NT:
                    stageA(T + 1)
                stageC(T, range(4, 8))
                stageD(T)
                for f in range(8):
                    if T + 1 < NT:
                        stageB(T + 1, f)
                    stageDn(T, f)
                if T + 1 < NT:
                    stageC(T + 1, range(0, 2))
                stageE(T)

    def attn_layer(self, l, h_in, b_in, h_out, b_out, fuse_final=False, prefetch_conv=False):
        K = self.K
        nc = self.nc
        TT = 512
        NT = S // TT
        p = "l%d_" % l
        hin_v = h_in.rearrange("(c p) t -> p c t", p=128)
        hout_v = h_out.rearrange("(c p) t -> p c t", p=128)
        if not hasattr(self, "qT_d"):
            self.qT_d = nc.dram_tensor("qT_scr", [D, S], BF16, kind="Internal").ap()
            self.kT_d = nc.dram_tensor("kT_scr", [D, S], BF16, kind="Internal").ap()
            self.sg_d = nc.dram_tensor("sg_scr", [D, S], BF16, kind="Internal").ap()
            self.v_d = nc.dram_tensor("v_scr", [S, D], BF16, kind="Internal").ap()
            self.z_d = nc.dram_tensor("z_scr", [D, S], BF16, kind="Internal").ap()
            self.b_scr = Buf("qkv_scr")
            self.b_zd = Buf("z_scr")
        qT_d, kT_d, sg_d, v_d, z_d = self.qT_d, self.kT_d, self.sg_d, self.v_d, self.z_d
        b_scr, b_zd = self.b_scr, self.b_zd
        ksum = nc.alloc_sbuf_tensor(p + "ksum", [128, 8, NBLK], F32)
        b_ksum = Buf(p + "ksum")

        b_mc = Buf("maskc")

        def alloc_masks(malloc):
            self.cm = malloc("cm", [128, 4, 512], BF16)
            self.cneg = malloc("cneg", [128, 512], F32)
            self.ownb = malloc("ownb", [128, 512], F32)
            self.ind = malloc("ind", [128, 16, 128], BF16)
            K.op(K.pool, lambda e: e.memset(self.ind[:].rearrange("p a b -> p (a b)"), 0.0), writes=[b_mc])
            b_cd = Buf("consts_d")
            b_ms = K.buf("masksem")
            self.pending.append((self.cm[:].rearrange("p a b -> p (a b)"), self.consts_d[:, 128:2176], b_mc, 128, 2048))
            self.pending.append((self.ind[0:16].rearrange("p a b -> p (a b)"), self.consts_d[0:16, 3200:5248], b_mc, 16, 2048))
            K.dma(K.sp, self.cneg[:], self.consts_d[:, 2176:2688], b_cd, b_mc, b_ms)
            K.dma(K.sp, self.ownb[:], self.consts_d[:, 2688:3200], b_cd, b_mc, b_ms)

        mcm = None
        if not prefetch_conv:
            mcm = K.scope()
            alloc_masks(mcm.__enter__())

        with K.scope() as alloc:
            win, b_win = self.load_w_bf16(alloc, p + "w_in", [D, 4 * D], 8, 4 * D)
            self.stage_all()
            g_sb, b_g = self.load_f32(alloc, p + "norm_gT", [128, 8])
            ht = [alloc("ht%d" % i, [128, 8, TT], F32) for i in range(2)]
            b_ht = [K.buf("ht%d" % i) for i in range(2)]
            sq = alloc("sq", [128, 8, TT], BF16)
            b_sq = K.buf("sq")
            std = alloc("std", [128, TT], F32)
            b_std = K.buf("std")
            rstd = alloc("rstd", [128, TT], F32)
            b_rstd = K.buf("rstd")
            ub = [alloc("ub%d" % i, [128, 8, TT], BF16) for i in range(2)]
            b_ub = [K.buf("ub%d" % i) for i in range(2)]
            qo = alloc("qo", [128, 8, TT], BF16)
            ko = alloc("ko", [128, 8, TT], BF16)
            go = alloc("go", [128, 8, TT], BF16)
            vo = alloc("vo", [128, 4, D], BF16)
            b_qo = [K.buf("qo%d" % k) for k in range(8)]
            b_ko = [K.buf("ko%d" % k) for k in range(8)]
            b_go = [K.buf("go%d" % k) for k in range(8)]
            b_vo = [K.buf("vo%d" % k) for k in range(8)]
            b_qs, b_ks, b_gs_, b_vs = K.buf("qs"), K.buf("ks"), K.buf("gs"), K.buf("vs")
            psrot = [0]

            def nextps():
                k = psrot[0] % 6
                psrot[0] += 1
                return self.ps[k], self.b_ps[k]

            def loadA(T):
                i = T % 2
                K.dma(K.sp, ht[i][:], hin_v[:, :, T * TT:(T + 1) * TT], b_in, b_ht[i], b_ht[i])

            def stageA(T):
                i = T % 2
                self.norm_stage(TT, ht[i], b_ht[i], sq, b_sq, self.ps[6], self.b_ps[6], std, b_std, rstd, b_rstd,
                                ub[i], b_ub[i], g_sb, b_g)

            def proj(i, col0):
                pst, b_pst = nextps()
                for c in range(8):
                    K.op(K.pe, lambda e: e.matmul(pst[:], lhsT=win[:, c, col0:col0 + 128], rhs=ub[i][:, c, :],
                                                  start=(c == 0), stop=(c == 7)),
                         reads=[b_win, b_ub[i]], writes=[b_pst], inc=(c == 7))
                return pst, b_pst

            def stageB(T):
                i = T % 2
                tsl = slice(T * TT, (T + 1) * TT)
                if T + 1 < NT:
                    loadA(T + 1)
                for hd in range(8):
                    pst, b_pst = proj(i, hd * 128)
                    K.op(K.act, lambda e: e.activation(out=qo[:, hd, :], in_=pst[:], func=AF.Copy, scale=128.0 ** -0.5),
                         reads=[b_pst], writes=[b_qo[hd]])
                K.dma(K.sp, qT_d.rearrange("(c p) t -> p c t", p=128)[:, :, tsl], qo[:], b_qo, b_scr, b_qs)
                for hd in range(8):
                    pst, b_pst = proj(i, D + hd * 128)
                    for hf in range(2):
                        K.op(K.act, lambda e: e.activation(out=ko[:, hd, hf * BLK:(hf + 1) * BLK], in_=pst[:, hf * BLK:(hf + 1) * BLK],
                                                           func=AF.Identity, accum_out=ksum[:, hd, 2 * T + hf:2 * T + hf + 1]),
                             reads=[b_pst], writes=[b_ko[hd], b_ksum])
                K.dma(K.sp, kT_d.rearrange("(c p) t -> p c t", p=128)[:, :, tsl], ko[:], b_ko, b_scr, b_ks)
                if T + 1 < NT:
                    stageA(T + 1)
                for s4 in range(4):
                    for hf in range(2):
                        pst, b_pst = nextps()
                        for c in range(8):
                            K.op(K.pe, lambda e: e.matmul(pst[:], lhsT=ub[i][:, c, s4 * 128:(s4 + 1) * 128],
                                                          rhs=win[:, c, 2 * D + hf * 512:2 * D + (hf + 1) * 512],
                                                          start=(c == 0), stop=(c == 7)),
                                 reads=[b_win, b_ub[i]], writes=[b_pst], inc=(c == 7))
                        K.op(K.dve, lambda e: e.tensor_copy(out=vo[:, s4, hf * 512:(hf + 1) * 512], in_=pst[:]),
                             reads=[b_pst], writes=[b_vo[s4 * 2 + hf]])
                K.dma(K.sp, v_d[tsl, :].rearrange("(s p) n -> p s n", p=128), vo[:], b_vo, b_scr, b_vs)
                for hd in range(8):
                    pst, b_pst = proj(i, 3 * D + hd * 128)
                    K.op(K.act, lambda e: e.activation(out=go[:, hd, :], in_=pst[:], func=AF.Silu),
                         reads=[b_pst], writes=[b_go[hd]])
                K.dma(K.sp, sg_d.rearrange("(c p) t -> p c t", p=128)[:, :, tsl], go[:], b_go, b_scr, b_gs_)

            loadA(0)
            stageA(0)
            for T in range(NT):
                stageB(T)

        carry = None
        if prefetch_conv:
            ccm = K.scope()
            calloc = ccm.__enter__()
            cwin, cb_win = self.bg_register("l1_w_in", [D, 3 * D], 8, 3 * D, calloc)
            cwout, cb_wout = self.bg_register("l1_w_out", [D, D], 8, D, calloc)
            carry = {"cm": ccm, "win": cwin, "b_win": cb_win, "wout": cwout, "b_wout": cb_wout}
        pcm = K.scope()
        palloc = pcm.__enter__()
        if not prefetch_conv:
            wout, b_wout = self.bg_register(p + "w_out", [D, D], 8, D, palloc)
        if prefetch_conv:
            alloc_masks(palloc)
            self.stage_all()
        self.bg_stg = [palloc("bgstg%d" % k, [128, 1024], F32) for k in range(2)]
        self.b_bg_stg = [K.buf("bgstg%d" % k) for k in range(2)]
        n_bg = len(self.bg_pieces)
        bg_every = max(1, 1000 // max(1, n_bg))

        with K.scope() as alloc:
            qh = [alloc("qh%d" % i, [128, S], BF16) for i in range(2)]
            kh = [alloc("kh%d" % i, [128, S], BF16) for i in range(2)]
            sgh = [alloc("sgh%d" % i, [128, S], BF16) for i in range(2)]
            vh = [alloc("vh%d" % i, [128, 32, 128], BF16) for i in range(2)]
            b_hd = [K.buf("hd%d" % i) for i in range(2)]
            b_hq = [K.buf("hq%d" % i) for i in range(2)]
            kmT = alloc("kmT", [128, 8, NBLK], BF16)
            b_kmT = K.buf("kmT")
            K.op(K.dve, lambda e: e.tensor_scalar(out=kmT[:], in0=ksum[:], scalar1=1.0 / BLK, scalar2=None, op0=ALU.mult),
                 reads=[b_ksum], writes=[b_kmT])
            ownbm = alloc("ownbm", [128, 512], F32)
            b_ownbm = K.buf("ownbm")
            K.op(K.dve, lambda e: e.tensor_scalar(out=ownbm[:], in0=self.ownb[:], scalar1=-BIG, scalar2=None, op0=ALU.add),
                 reads=[b_mc], writes=[b_ownbm])
            gsm = alloc("gsm", [128, 32, NBLK], F32)
            top8 = alloc("top8", [128, 32, 8], F32)
            thr = alloc("thr", [128, 32], F32)
            tmpm = alloc("tmpm", [128, 32, NBLK], F32)
            negsel = alloc("negsel", [128, 32, 128], F32)
            b_gate = K.buf("gate")
            b_negsel = K.buf("negsel")
            K.op(K.pool, lambda e: e.memset(negsel[:].rearrange("p a b -> p (a b)"), 0.0), writes=[b_negsel])
            nselT = [alloc("nselT%d" % i, [128, S], BF16) for i in range(2)]
            b_nselT = [K.buf("nselT%d" % i) for i in range(2)]
            NP = 6
            pT = [alloc("pT%d" % k, [128, TT], BF16) for k in range(NP)]
            b_pT = [K.buf("pT%d" % k) for k in range(NP)]
            rden = alloc("rden", [128, TT], F32)
            b_rden = K.buf("rden")
            acc = [alloc("acc%d" % a, [128, TT], F32) for a in range(2)]
            b_acc = [K.buf("acc%d" % a) for a in range(2)]
            onesf = alloc("onesf", [128, 128], F32)
            b_onesf = K.buf("onesf")
            K.op(K.pool, lambda e: e.memset(onesf[:], 1.0), writes=[b_onesf])
            ot = alloc("ot", [128, TT], F32)
            b_ot = K.buf("ot")
            zo = [alloc("zo%d" % k, [128, TT], BF16) for k in range(2)]
            b_zo = [K.buf("zo%d" % k) for k in range(2)]
            z_v = z_d.rearrange("(c p) t -> c p t", p=128)
            pg, b_pg = self.pg, self.b_pg

            def load_head(hd):
                i = hd % 2
                hs_ = slice(hd * 128, (hd + 1) * 128)
                K.dma(K.sp, qh[i][:], qT_d[hs_, :], b_scr, b_hq[i], b_hq[i])
                K.dma(K.sp, kh[i][:], kT_d[hs_, :], b_scr, b_hd[i], b_hd[i])
                K.dma(K.sp, sgh[i][:], sg_d[hs_, :], b_scr, b_hd[i], b_hd[i])
                vv = v_d[:, hs_].rearrange("(s p) n -> p s n", p=128)
                for s8 in range(4):
                    K.dma(K.sp, vh[i][:, s8 * 8:(s8 + 1) * 8, :], vv[:, s8 * 8:(s8 + 1) * 8, :], b_scr, b_hd[i], b_hd[i])

            def gating1(hd):
                i = hd % 2
                for j in range(32):
                    K.op(K.pe, lambda e: e.matmul(pg[:, j * NBLK:(j + 1) * NBLK], lhsT=qh[i][:, j * 128:(j + 1) * 128], rhs=kmT[:, hd, :],
                                                  start=True, stop=True),
                         reads=[b_hq[i], b_kmT], writes=[b_pg], inc=(j == 31))
                gsm2 = gsm[:].rearrange("p a b -> p (a b)")
                K.op(K.dve, lambda e: e.tensor_tensor(out=gsm2, in0=pg[:], in1=self.cneg[:], op=ALU.add),
                     reads=[b_pg, b_mc], writes=[b_gate])
                for j in range(32):
                    K.op(K.dve, lambda e: e.max(out=top8[:, j, :], in_=gsm[:, j, :]), reads=[b_gate], writes=[b_gate], inc=(j == 31))
                K.op(K.dve, lambda e: e.tensor_scalar(out=thr[:], in0=top8[:, :, 2], scalar1=-1e29, scalar2=None, op0=ALU.max),
                     reads=[b_gate], writes=[b_gate])
                K.op(K.dve, lambda e: e.tensor_tensor(out=tmpm[:], in0=gsm[:], in1=thr[:, :, None].broadcast_to([128, 32, NBLK]),
                                                      op=ALU.is_ge),
                     reads=[b_gate], writes=[b_gate])
                K.op(K.dve, lambda e: e.scalar_tensor_tensor(out=negsel[:, :, 0:NBLK], in0=tmpm[:], scalar=BIG,
                                                              in1=ownbm[:].rearrange("p (a b) -> p a b", b=NBLK),
                                                              op0=ALU.mult, op1=ALU.add),
                     reads=[b_gate, b_ownbm], writes=[b_negsel])

            def gating2(hd, groups=range(8)):
                i = hd % 2
                for g8 in groups:
                    for jj in range(4):
                        j = g8 * 4 + jj
                        K.op(K.pe, lambda e: e.transpose(out=pg[:, jj * 128:(jj + 1) * 128], in_=negsel[:, j, :], identity=self.identf[:]),
                             reads=[b_negsel, self.b_const], writes=[b_pg], inc=(jj == 3))
                    K.op(K.dve, lambda e: e.tensor_copy(out=nselT[i][:, g8 * 512:(g8 + 1) * 512], in_=pg[:]),
                         reads=[b_pg], writes=[b_nselT[i]])

            steps = [(hd, T, kt) for hd in range(8) for T in range(NT) for kt in range(4 * (T + 1))]
            NS = len(steps)

            def obanks(T):
                return (self.ps[3 + (T % 2) * 2], self.b_ps[3 + (T % 2) * 2], self.ps[4 + (T % 2) * 2], self.b_ps[4 + (T % 2) * 2])

            def cols(T, kt):
                return 256 if (HALF_DIAG and kt >= 4 * T + 2) else 0

            def s_mm(g):
                hd, T, kt = steps[g]
                i = hd % 2
                c0 = cols(T, kt)
                qsl = slice(T * TT + c0, (T + 1) * TT)
                k3 = g % 3
                sp_, b_sp = self.ps[k3], self.b_ps[k3]
                diag = kt >= 4 * T
                K.op(K.pe, lambda e: e.matmul(sp_[:, c0:TT], lhsT=kh[i][:, kt * 128:(kt + 1) * 128], rhs=qh[i][:, qsl], start=True, stop=False),
                     reads=[b_hd[i], b_hq[i]], writes=[b_sp], inc=False)
                K.op(K.pe, lambda e: e.matmul(sp_[:, c0:TT], lhsT=self.ind[:, kt // 2, :], rhs=nselT[i][:, qsl], start=False, stop=not diag),
                     reads=[b_mc, b_nselT[i]], writes=[b_sp], inc=not diag)
                if diag:
                    K.op(K.pe, lambda e: e.matmul(sp_[:, c0:TT], lhsT=self.ident[:], rhs=self.cm[:, kt - 4 * T, c0:TT], start=False, stop=True),
                         reads=[self.b_const, b_mc], writes=[b_sp], inc=True)

            def pv_mm(g):
                hd, T, kt = steps[g]
                i = hd % 2
                nk = 4 * (T + 1)
                c0 = cols(T, kt)
                k3 = g % 3
                kp = g % NP
                sp_, b_sp = self.ps[k3], self.b_ps[k3]
                ops_, b_ops, dps, b_dps = obanks(T)
                K.op(K.act, lambda e: e.activation(out=pT[kp][:, c0:TT], in_=sp_[:, c0:TT], func=AF.Exp), reads=[b_sp], writes=[b_pT[kp]])
                K.op(K.pe, lambda e: e.matmul(ops_[:, c0:TT], lhsT=vh[i][:, kt, :], rhs=pT[kp][:, c0:TT], start=(kt == 0), stop=(kt == nk - 1)),
                     reads=[b_hd[i], b_pT[kp]], writes=[b_ops], inc=True)
                if kt % 2 == 1:
                    K.op(K.pe, lambda e: e.matmul(dps[:, c0:TT], lhsT=self.ones[:], rhs=pT[kp][:, c0:TT], start=(kt == 1), stop=False),
                         reads=[self.b_const, b_pT[kp]], writes=[b_dps], inc=True)
                else:
                    ac, b_ac = acc[T % 2], b_acc[T % 2]
                    if kt == 0:
                        K.op(K.dve, lambda e: e.tensor_copy(out=ac[:], in_=pT[kp][:]), reads=[b_pT[kp]], writes=[b_ac])
                    else:
                        K.op(K.dve, lambda e: e.tensor_tensor(out=ac[:, c0:TT], in0=ac[:, c0:TT], in1=pT[kp][:, c0:TT], op=ALU.add),
                             reads=[b_pT[kp], b_ac], writes=[b_ac])

            deferred = []

            def finalize(hd, T, g):
                i = hd % 2
                qsl = slice(T * TT, (T + 1) * TT)
                ops_, b_ops, dps, b_dps = obanks(T)
                zi = T % 2
                K.op(K.pe, lambda e: e.matmul(dps[:], lhsT=onesf[:], rhs=acc[T % 2][:], start=False, stop=True),
                     reads=[b_onesf, b_acc[T % 2]], writes=[b_dps], inc=True)

                def quarter(q):
                    cs = slice(q * 128, (q + 1) * 128)
                    K.op(K.dve, lambda e: e.reciprocal(out=rden[:, cs], in_=dps[:, cs]), reads=[b_dps], writes=[b_rden])
                    K.op(K.dve, lambda e: e.tensor_tensor(out=ot[:, cs], in0=ops_[:, cs], in1=rden[:, cs], op=ALU.mult),
                         reads=[b_ops, b_rden], writes=[b_ot])
                    if q == 3:
                        K.op(K.pool, lambda e: e.tensor_tensor(out=zo[zi][:], in0=ot[:], in1=sgh[i][:, qsl], op=ALU.mult),
                             reads=[b_ot, b_hd[i]], writes=[b_zo[zi]])
                        K.dma(K.sp, z_v[hd, :, qsl], zo[zi][:], b_zo[zi], b_zd, b_zo[zi])

                for q in range(4):
                    deferred.append((g + 1 + q, lambda q=q: quarter(q)))
                deferred.sort(key=lambda d: d[0])

            def run_deferred(g):
                while deferred and deferred[0][0] <= g:
                    deferred.pop(0)[1]()

            load_head(0)
            gating1(0)
            gating2(0)
            s_mm(0)
            s_mm(1)
            for g in range(NS):
                hd, T, kt = steps[g]
                if T == 1 and kt == 0 and hd + 1 < 8:
                    load_head(hd + 1)
                if g + 2 < NS:
                    s_mm(g + 2)
                pv_mm(g)
                run_deferred(g)
                if g >= 10 and (g - 10) % bg_every == 0:
                    self.bg_step()
                if kt == 4 * (T + 1) - 1:
                    finalize(hd, T, g)
                    if hd + 1 < 8 and T == 3:
                        gating1(hd + 1)
                    if hd + 1 < 8 and T == 5:
                        for g8 in range(8):
                            deferred.append((g + 1 + 2 * g8, lambda g8=g8, hd=hd: gating2(hd + 1, [g8])))
                        deferred.sort(key=lambda d: d[0])
            run_deferred(NS + 10)
            while self.bg_pieces:
                self.bg_step()

        with K.scope() as alloc:
            if prefetch_conv:
                wout, b_wout = self.load_w_bf16(alloc, p + "w_out", [D, D], 8, D)
                self.stage_all()
            ht = [alloc("ht%d" % i, [128, 8, TT], F32) for i in range(2)]
            b_ht = [K.buf("ht%d" % i) for i in range(2)]
            zt = [alloc("zt%d" % i, [128, 8, TT], BF16) for i in range(2)]
            b_zt = [K.buf("zt%d" % i) for i in range(2)]
            zd_v = z_d.rearrange("(c p) t -> p c t", p=128)
            psrot = [0]
            if fuse_final:
                fg_sb, b_fg = self.load_f32(alloc, "final_gT", [128, 8])
                fsq = alloc("fsq", [128, 8, TT], BF16)
                b_fsq = K.buf("fsq")
                fstd = alloc("fstd", [128, TT], F32)
                b_fstd = K.buf("fstd")
                frstd = alloc("frstd", [128, TT], F32)
                b_frstd = K.buf("frstd")

            def load3(T):
                i = T % 2
                K.dma(K.sp, ht[i][:], hin_v[:, :, T * TT:(T + 1) * TT], b_in, b_ht[i], b_ht[i])
                K.dma(K.sp, zt[i][:], zd_v[:, :, T * TT:(T + 1) * TT], b_zd, b_zt[i], b_zt[i])

            load3(0)
            for T in range(NT):
                i = T % 2
                if T + 1 < NT:
                    load3(T + 1)
                for f in range(8):
                    k = psrot[0] % 6
                    psrot[0] += 1
                    pst, b_pst = self.ps[k], self.b_ps[k]
                    for c in range(8):
                        K.op(K.pe, lambda e: e.matmul(pst[:], lhsT=wout[:, c, f * 128:(f + 1) * 128], rhs=zt[i][:, c, :],
                                                      start=(c == 0), stop=(c == 7)),
                             reads=[b_wout, b_zt[i]], writes=[b_pst], inc=(c == 7))
                    K.op(K.dve, lambda e: e.tensor_tensor(out=ht[i][:, f, :], in0=pst[:], in1=ht[i][:, f, :], op=ALU.add),
                         reads=[b_pst, b_ht[i]], writes=[b_ht[i]])
                if fuse_final:
                    self.norm_stage(TT, ht[i], b_ht[i], fsq, b_fsq, self.ps[6], self.b_ps[6], fstd, b_fstd, frstd, b_frstd,
                                    ht[i], b_ht[i], fg_sb, b_fg)
                K.dma(K.sp, hout_v[:, :, T * TT:(T + 1) * TT], ht[i][:], b_ht[i], b_out, b_ht[i])

        pcm.__exit__(None, None, None)
        if mcm is not None:
            mcm.__exit__(None, None, None)
        return carry


def _consts():
    c = np.zeros((128, NCONST), np.float32)
    c[:, 0:128] = np.eye(128, dtype=np.float32)
    r = np.arange(128)[:, None]
    q = np.arange(512)[None, :]
    cm = np.zeros((128, 4, 512), np.float32)
    for a in range(4):
        kpos = 128 * a + r
        blk_k = kpos // BLK
        blk_q = q // BLK
        same = blk_k == blk_q
        cm[:, a, :] = np.where(same & (kpos > q), -BIG, 0.0)
    c[:, 128:2176] = cm.reshape(128, 2048)
    j = np.arange(32)[:, None]
    n = np.arange(NBLK)[None, :]
    cneg = np.where(n < (j // 2), 0.0, -1e30).astype(np.float32)
    ownb = np.where(n == (j // 2), BIG, 0.0).astype(np.float32)
    c[:, 2176:2688] = np.broadcast_to(cneg.reshape(1, 512), (128, 512))
    c[:, 2688:3200] = np.broadcast_to(ownb.reshape(1, 512), (128, 512))
    ind = np.zeros((16, 16, 128), np.float32)
    for nn in range(16):
        ind[nn, nn, :] = 1.0
    c[0:16, 3200:5248] = ind.reshape(16, 2048)
    return c


def _col8(v):
    return np.ascontiguousarray(np.asarray(v, np.float32).reshape(-1, 128).T)


def _layer_inputs(l, inp):
    p = "l%d_" % l
    d = {}
    d[p + "norm_gT"] = _col8(inp[p + "norm_g"])
    d[p + "w_in"] = np.ascontiguousarray(inp[p + "w_in"], dtype=np.float32)
    d[p + "w_out"] = np.ascontiguousarray(inp[p + "w_out"], dtype=np.float32)
    if l == 1:
        cw = np.asarray(inp[p + "conv_w"], np.float32).reshape(31, 8, 128).transpose(2, 0, 1)
        rest = np.stack([_col8(inp[p + k]) for k in ("conv_b", "ln_g", "ln_b")], axis=1)
        d[p + "vec"] = np.ascontiguousarray(np.concatenate([cw, rest], axis=1))
    if l == 2:
        cw = np.asarray(inp[p + "conv_w"], np.float32).reshape(4, 10, 128).transpose(2, 0, 1)
        rest = np.stack([_col8(inp[p + k]) for k in ("conv_b", "b_rg", "b_ig", "lam")], axis=1)
        d[p + "vec"] = np.ascontiguousarray(np.concatenate([cw, rest], axis=1))
        d[p + "w_rg"] = np.ascontiguousarray(inp[p + "w_rg"], dtype=np.float32)
        d[p + "w_ig"] = np.ascontiguousarray(inp[p + "w_ig"], dtype=np.float32)
    return d


def run_layers(layers, final_norm, hT_list, inp):
    prog = Prog(layers, final_norm, first=True)
    shared = {"consts": _consts()}
    for l in layers:
        shared.update(_layer_inputs(l, inp))
    if final_norm:
        shared["final_gT"] = _col8(inp["final_g"])
    in_maps = []
    for b in range(len(hT_list)):
        m = dict(shared)
        m["xT"] = hT_list[b]
        in_maps.append(m)
    res = run_bass_kernel_spmd(prog.nc, in_maps, core_ids=list(range(len(hT_list))))
    return [r["oT"] for r in res.results]


def kernel(**inputs):
    inp = {k: np.asarray(inputs[k]) for k in ALL_INPUTS}
    x = inp["x"]
    hT = [np.ascontiguousarray(x[b].T, dtype=np.float32) for b in range(NCORES)]
    if MODE == "fused":
        hT = run_layers([0, 1, 2, 3], True, hT, inp)
    else:
        for l in range(4):
            hT = run_layers([l], l == 3, hT, inp)
    return np.ascontiguousarray(np.stack([h.T for h in hT], axis=0)).astype(np.float32)
```

```python
import contextlib
import numpy as np
import concourse.bass as bass
import concourse.mybir as mybir
from concourse.bass_utils import run_bass_kernel_spmd

F32 = mybir.dt.float32
BF16 = mybir.dt.bfloat16
AF = mybir.ActivationFunctionType
ALU = mybir.AluOpType

S = 4096
D = 1024
EPS = 1e-6
NCORES = 8
NBLK = 16
BLK = 256
BIG = 30000.0
LW = 1280
NCONST = 128 + 2048 + 512 + 512 + 2048

ALL_INPUTS = (
    "x", "l0_norm_g", "l0_w_in", "l0_w_out",
    "l1_norm_g", "l1_w_in", "l1_conv_w", "l1_conv_b", "l1_ln_g", "l1_ln_b", "l1_w_out",
    "l2_norm_g", "l2_w_in", "l2_conv_w", "l2_conv_b", "l2_w_rg", "l2_b_rg", "l2_w_ig", "l2_b_ig", "l2_lam", "l2_w_out",
    "l3_norm_g", "l3_w_in", "l3_w_out", "final_g",
)

HALF_DIAG = False
MODE = "fused"


class SemRec:
    def __init__(self, sem):
        self.sem = sem
        self.cnt = 0


class Buf:
    def __init__(self, name):
        self.name = name
        self.w = {}
        self.r = {}
        self.ds = None


class Eng:
    def __init__(self, nc, name, h, same_sync):
        self.name = name
        self.h = h
        self.sem = nc.alloc_semaphore(name="e_" + name)
        self.cnt = 0
        self.seen = {}
        self.same_sync = same_sync


def _add(d, tok):
    k = id(tok[0])
    if k not in d or d[k][1] < tok[1]:
        d[k] = tok


class Trk:
    def __init__(self, nc):
        self.nc = nc
        self.pe = Eng(nc, "pe", nc.tensor, False)
        self.act = Eng(nc, "act", nc.scalar, True)
        self.dve = Eng(nc, "dve", nc.vector, True)
        self.pool = Eng(nc, "pool", nc.gpsimd, True)
        self.sp = Eng(nc, "sp", nc.sync, False)
        self.engs = [self.pe, self.act, self.dve, self.pool, self.sp]
        self.free_ds = []
        self.all_ds = []
        self.scope_bufs = []
        self.uid = 0

    def buf(self, name):
        b = Buf(name)
        if self.scope_bufs:
            self.scope_bufs[-1].append(b)
        return b

    def _wait(self, eng, deps):
        for (s, v) in deps.values():
            k = id(s)
            if eng.seen.get(k, 0) >= v:
                continue
            if s is eng.sem and v > eng.cnt:
                continue
            eng.h.wait_ge(s, v)
            eng.seen[k] = v

    def op(self, eng, fn, reads=(), writes=(), inc=True):
        deps = {}
        for b in reads:
            for t in b.w.values():
                if t[0] is eng.sem and not eng.same_sync:
                    continue
                _add(deps, t)
        for b in writes:
            for t in b.w.values():
                if t[0] is not eng.sem or eng.same_sync:
                    _add(deps, t)
            for t in b.r.values():
                if t[0] is not eng.sem or eng.same_sync:
                    _add(deps, t)
        self._wait(eng, deps)
        ins = fn(eng.h)
        if inc:
            eng.cnt += 1
            ins.then_inc(eng.sem, 1)
            tok = (eng.sem, eng.cnt)
        else:
            tok = (eng.sem, eng.cnt + 1)
        for b in reads:
            _add(b.r, tok)
        for b in writes:
            _add(b.w, tok)
        return ins

    def dma(self, q, out_ap, in_ap, src, dst, sembuf, **kw):
        srcs = list(src) if isinstance(src, (list, tuple)) else [src]
        deps = {}
        for sb_ in srcs:
            for t in sb_.w.values():
                _add(deps, t)
        for t in dst.w.values():
            _add(deps, t)
        for t in dst.r.values():
            _add(deps, t)
        self._wait(q, deps)
        if sembuf.ds is None:
            if self.free_ds:
                sembuf.ds = self.free_ds.pop()
            else:
                sembuf.ds = SemRec(self.nc.alloc_semaphore(name="d%d" % len(self.all_ds)))
                self.all_ds.append(sembuf.ds)
        ds = sembuf.ds
        ins = q.h.dma_start(out=out_ap, in_=in_ap, **kw)
        ds.cnt += 16
        ins.then_inc(ds.sem, 16)
        tok = (ds.sem, ds.cnt)
        for sb_ in srcs:
            _add(sb_.r, tok)
        _add(dst.w, tok)
        return ins

    def barrier(self):
        for e in self.engs:
            deps = {}
            for f in self.engs:
                if f is not e and f.cnt > 0:
                    _add(deps, (f.sem, f.cnt))
            for ds in self.all_ds:
                if ds.cnt > 0:
                    _add(deps, (ds.sem, ds.cnt))
            self._wait(e, deps)

    @contextlib.contextmanager
    def scope(self):
        es = contextlib.ExitStack()
        self.scope_bufs.append([])
        nc = self.nc

        def alloc(name, shape, dt):
            self.uid += 1
            return es.enter_context(nc.sbuf_tensor("%s_u%d" % (name, self.uid), shape, dt))

        try:
            yield alloc
            self.barrier()
            for b in self.scope_bufs[-1]:
                if b.ds is not None:
                    self.free_ds.append(b.ds)
                    b.ds = None
        finally:
            self.scope_bufs.pop()
            es.close()


class Prog:
    def __init__(self, layers, final_norm, first, name_in="xT", name_out="oT"):
        nc = bass.Bass("TRN2", target_bir_lowering=False)
        self.nc = nc
        self.K = Trk(nc)
        K = self.K
        self.inputs = {}
        self.pending = []
        self.bg_pieces = []
        self.bg_n = 0

        def ext(name, shape, dt=F32):
            self.inputs[name] = shape
            return nc.dram_tensor(name, list(shape), dt, kind="ExternalInput").ap()

        self.ext = ext
        self.h_in = ext(name_in, [D, S])
        self.h_out = nc.dram_tensor(name_out, [D, S], F32, kind="ExternalOutput").ap()
        self.b_hin = Buf("h_in")
        self.b_hout = Buf("h_out")
        self.h_scr = None
        consts = ext("consts", [128, NCONST])

        self.ps = [nc.alloc_psum_tensor("ps%d" % i, [128, 512], F32) for i in range(7)]
        self.b_ps = [Buf("ps%d" % i) for i in range(7)]
        self.pg = nc.alloc_psum_tensor("pg", [128, 512], F32)
        self.b_pg = Buf("pg")

        self.ones = nc.alloc_sbuf_tensor("ones", [128, 128], BF16)
        self.ident = nc.alloc_sbuf_tensor("ident", [128, 128], BF16)
        self.identf = nc.alloc_sbuf_tensor("identf", [128, 128], F32)
        self.b_const = Buf("const")
        self.consts_d = consts
        with K.scope() as alloc:
            cst = alloc("cst", [128, 128], F32)
            b_cst = K.buf("cst")
            b_cd = Buf("consts_d")
            K.dma(K.sp, cst[:], consts[:, 0:128], b_cd, b_cst, b_cst)
            K.op(K.dve, lambda e: e.memset(self.ones[:], 1.0), writes=[self.b_const])
            K.op(K.dve, lambda e: e.tensor_copy(out=self.ident[:], in_=cst[:]), reads=[b_cst], writes=[self.b_const])
            K.op(K.dve, lambda e: e.tensor_copy(out=self.identf[:], in_=cst[:]), reads=[b_cst], writes=[self.b_const])

        cur_in, b_in = self.h_in, self.b_hin
        fuse_fn = final_norm and len(layers) > 0 and layers[-1] in (0, 3)
        if fuse_fn:
            final_norm = False
        carry = None
        for li, l in enumerate(layers):
            last = (li == len(layers) - 1) and not final_norm
            if last:
                cur_out, b_out = self.h_out, self.b_hout
            else:
                if self.h_scr is None:
                    self.h_scr = nc.dram_tensor("h_scr", [D, S], F32, kind="Internal").ap()
                    self.b_hscr = Buf("h_scr")
                cur_out, b_out = self.h_scr, self.b_hscr
            if l in (0, 3):
                nxt = layers[li + 1] if li + 1 < len(layers) else None
                carry = self.attn_layer(l, cur_in, b_in, cur_out, b_out, fuse_final=(fuse_fn and li == len(layers) - 1),
                                        prefetch_conv=(nxt == 1))
            elif l == 1:
                self.conv_layer(l, cur_in, b_in, cur_out, b_out, pre=carry)
                if carry is not None:
                    carry["cm"].__exit__(None, None, None)
                carry = None
            else:
                self.lru_layer(l, cur_in, b_in, cur_out, b_out)
            cur_in, b_in = cur_out, b_out
        if final_norm:
            self.final_norm(cur_in, b_in, self.h_out, self.b_hout)
        deps = {}
        for t in self.b_hout.w.values():
            _add(deps, t)
        K._wait(K.sp, deps)

    def load_w_bf16(self, alloc, name, shape_in, kchunks, ncols):
        K = self.K
        w_d = self.ext(name, shape_in)
        wb = alloc(name + "_sb", [128, kchunks, ncols], BF16)
        b = K.buf(name)
        for c in range(kchunks):
            for n0 in range(0, ncols, 2048):
                n1 = min(ncols, n0 + 2048)
                self.pending.append((wb[:, c, n0:n1], w_d[c * 128:(c + 1) * 128, n0:n1], b, 128, n1 - n0))
        return wb, b

    def stage_all(self):
        K = self.K
        if not self.pending:
            return
        with K.scope() as alloc:
            NS = 8
            stg = [alloc("stg%d" % k, [128, 2048], F32) for k in range(NS)]
            b_stg = [K.buf("stg%d" % k) for k in range(NS)]
            bd = Buf("wdram")
            engs = [K.dve, K.act]
            for n, (dst, src, b, npart, ncol) in enumerate(self.pending):
                k = n % NS
                sv = stg[k][0:npart, 0:ncol]
                if len(dst.shape) == 3:
                    sv = sv.rearrange("p (a b) -> p a b", b=dst.shape[2])
                K.dma(K.sp, sv, src, bd, b_stg[k], b_stg[k])
                eng = engs[n % 2]
                if eng is K.act:
                    K.op(eng, lambda e: e.activation(out=dst, in_=sv, func=AF.Copy), reads=[b_stg[k]], writes=[b])
                else:
                    K.op(eng, lambda e: e.tensor_copy(out=dst, in_=sv), reads=[b_stg[k]], writes=[b])
        self.pending = []

    def bg_register(self, name, shape_in, kchunks, ncols, alloc):
        K = self.K
        w_d = self.ext(name, shape_in)
        wb = alloc(name + "_sb", [128, kchunks, ncols], BF16)
        b = K.buf(name)
        for c in range(kchunks):
            for n0 in range(0, ncols, 1024):
                n1 = min(ncols, n0 + 1024)
                self.bg_pieces.append((wb[:, c, n0:n1], w_d[c * 128:(c + 1) * 128, n0:n1], b, 128, n1 - n0))
        return wb, b

    def bg_step(self):
        K = self.K
        if not self.bg_pieces:
            return
        dst, src, b, npart, ncol = self.bg_pieces.pop(0)
        k = self.bg_n % 2
        self.bg_n += 1
        stg, b_stg = self.bg_stg[k], self.b_bg_stg[k]
        K.dma(K.sp, stg[0:npart, 0:ncol], src, Buf("wdram"), b_stg, b_stg)
        K.op(K.pool, lambda e: e.tensor_copy(out=dst, in_=stg[0:npart, 0:ncol]), reads=[b_stg], writes=[b])

    def load_f32(self, alloc, name, shape):
        K = self.K
        d = self.ext(name, shape)
        t = alloc(name + "_sb", list(shape), F32)
        b = K.buf(name)
        K.dma(K.sp, t[:], d, Buf(name + "_d"), b, b)
        return t, b

    def norm_stage(self, TT, ht, b_ht, sq, b_sq, pstat, b_pstat, std, b_std, rstd, b_rstd, ub, b_ub, g_sb, b_g):
        K = self.K
        K.op(K.act, lambda e: e.activation(out=sq[:], in_=ht[:], func=AF.Square), reads=[b_ht], writes=[b_sq])
        for c in range(8):
            K.op(K.pe, lambda e: e.matmul(pstat[:, 0:TT], lhsT=self.ones[:], rhs=sq[:, c, :], start=(c == 0), stop=(c == 7)),
                 reads=[self.b_const, b_sq], writes=[b_pstat], inc=(c == 7))
        K.op(K.act, lambda e: e.activation(out=std[:], in_=pstat[:, 0:TT], func=AF.Sqrt, bias=EPS, scale=1.0 / D),
             reads=[b_pstat], writes=[b_std])
        K.op(K.dve, lambda e: e.reciprocal(out=rstd[:], in_=std[:]), reads=[b_std], writes=[b_rstd])
        for c in range(8):
            K.op(K.dve, lambda e: e.scalar_tensor_tensor(out=ub[:, c, :], in0=ht[:, c, :], scalar=g_sb[:, c:c + 1],
                                                          in1=rstd[:], op0=ALU.mult, op1=ALU.mult),
                 reads=[b_ht, b_g, b_rstd], writes=[b_ub])

    def final_norm(self, h_in, b_in, h_out, b_out):
        K = self.K
        TT = 512
        NT = S // TT
        hin_v = h_in.rearrange("(c p) t -> p c t", p=128)
        hout_v = h_out.rearrange("(c p) t -> p c t", p=128)
        with K.scope() as alloc:
            g_sb, b_g = self.load_f32(alloc, "final_gT", [128, 8])
            ht = [alloc("fn_ht%d" % i, [128, 8, TT], F32) for i in range(2)]
            b_ht = [K.buf("fn_ht%d" % i) for i in range(2)]
            sq = alloc("fn_sq", [128, 8, TT], BF16)
            b_sq = K.buf("fn_sq")
            std = alloc("fn_std", [128, TT], F32)
            b_std = K.buf("fn_std")
            rstd = alloc("fn_rstd", [128, TT], F32)
            b_rstd = K.buf("fn_rstd")
            K.dma(K.sp, ht[0][:], hin_v[:, :, 0:TT], b_in, b_ht[0], b_ht[0])
            for T in range(NT):
                i = T % 2
                sl = slice(T * TT, (T + 1) * TT)
                if T + 1 < NT:
                    K.dma(K.sp, ht[1 - i][:], hin_v[:, :, (T + 1) * TT:(T + 2) * TT], b_in, b_ht[1 - i], b_ht[1 - i])
                self.norm_stage(TT, ht[i], b_ht[i], sq, b_sq, self.ps[6], self.b_ps[6], std, b_std, rstd, b_rstd,
                                ht[i], b_ht[i], g_sb, b_g)
                K.dma(K.sp, hout_v[:, :, sl], ht[i][:], b_ht[i], b_out, b_ht[i])

    def lru_layer(self, l, h_in, b_in, h_out, b_out):
        K = self.K
        nc = self.nc
        TT = 256
        NT = S // TT
        NC = LW // 128
        p = "l%d_" % l
        hin_v = h_in.rearrange("(c p) t -> p c t", p=128)
        hout_v = h_out.rearrange("(c p) t -> p c t", p=128)
        with K.scope() as alloc:
            win, b_win = self.load_w_bf16(alloc, p + "w_in", [D, 2 * LW], 8, 2 * LW)
            wout, b_wout = self.load_w_bf16(alloc, p + "w_out", [LW, D], NC, D)
            g_sb, b_g = self.load_f32(alloc, p + "norm_gT", [128, 8])
            vec, b_vec = self.load_f32(alloc, p + "vec", [128, 8, NC])
            wrg_d = self.ext(p + "w_rg", [NC, 128, 128])
            wig_d = self.ext(p + "w_ig", [NC, 128, 128])
            wrg = alloc("wrg", [128, NC, 128], BF16)
            wig = alloc("wig", [128, NC, 128], BF16)
            b_wg = K.buf("wg")
            self.pending.append((wrg[:], wrg_d.rearrange("h i j -> i h j"), b_wg, 128, NC * 128))
            self.pending.append((wig[:], wig_d.rearrange("h i j -> i h j"), b_wg, 128, NC * 128))
            self.stage_all()
            cl = alloc("cl", [128, 2, NC], F32)
            tmpv = alloc("tmpv", [128, 2, NC], F32)
            b_cl = K.buf("cl")
            K.op(K.act, lambda e: e.activation(out=tmpv[:, 0, :], in_=vec[:, 7, :], func=AF.Exp, scale=-1.0), reads=[b_vec], writes=[b_cl])
            K.op(K.act, lambda e: e.activation(out=tmpv[:, 1, :], in_=tmpv[:, 0, :], func=AF.Ln, bias=1.0, scale=1.0), reads=[b_cl], writes=[b_cl])
            K.op(K.dve, lambda e: e.tensor_scalar(out=cl[:, 0, :], in0=tmpv[:, 1, :], scalar1=-8.0, scalar2=None, op0=ALU.mult),
                 reads=[b_cl], writes=[b_cl])
            K.op(K.dve, lambda e: e.tensor_scalar(out=cl[:, 1, :], in0=tmpv[:, 1, :], scalar1=-16.0, scalar2=None, op0=ALU.mult),
                 reads=[b_cl], writes=[b_cl])
            dg = alloc("dg", [128, 4, NC, 128], BF16)
            b_dg = K.buf("dg")
            for j in range(4):
                for c in range(NC):
                    K.op(K.dve, lambda e: e.tensor_scalar(out=dg[:, j, c, :], in0=self.identf[:], scalar1=vec[:, j, c:c + 1],
                                                           scalar2=None, op0=ALU.mult),
                         reads=[b_vec, self.b_const], writes=[b_dg])
            ht = [alloc("ht%d" % i, [128, 8, TT], F32) for i in range(3)]
            b_ht = [K.buf("ht%d" % i) for i in range(3)]
            sq = alloc("sq", [128, 8, TT], BF16)
            b_sq = K.buf("sq")
            std = alloc("std", [128, TT], F32)
            b_std = K.buf("std")
            rstd = alloc("rstd", [128, TT], F32)
            b_rstd = K.buf("rstd")
            ub = [alloc("ub%d" % i, [128, 8, TT], BF16) for i in range(2)]
            b_ub = [K.buf("ub%d" % i) for i in range(2)]
            xbuf = [alloc("xbuf%d" % i, [128, NC, TT + 3], BF16) for i in range(2)]
            b_xbuf = [K.buf("xbuf%d" % i) for i in range(2)]
            sg = [alloc("sg%d" % i, [128, NC, TT], BF16) for i in range(2)]
            b_sg = [K.buf("sg%d" % i) for i in range(2)]
            xc = alloc("xc", [128, NC, TT], F32)
            xcb = alloc("xcb", [128, NC, TT], BF16)
            b_xc = [K.buf("xc%d" % c) for c in range(NC)]
            b_xcb = [K.buf("xcb%d" % c) for c in range(NC)]
            hs = [alloc("hs%d" % i, [128, NC, TT], F32) for i in range(2)]
            b_hs = [[K.buf("hs%d_%d" % (i, c)) for c in range(NC)] for i in range(2)]
            zt = alloc("zt", [128, NC, TT], BF16)
            b_zt = K.buf("zt")
            HC = 5
            NR = 3
            r5 = alloc("r5", [128, HC, TT], F32)
            i5 = alloc("i5", [128, HC, TT], F32)
            a5 = alloc("a5", [128, HC, TT], F32)
            s5 = alloc("s5", [128, HC, TT], F32)
            b_r5 = [K.buf("r5_%d" % k) for k in range(HC)]
            b_i5 = [K.buf("i5_%d" % k) for k in range(HC)]
            b_a5 = [K.buf("a5_%d" % k) for k in range(HC)]
            b_s5 = [K.buf("s5_%d" % k) for k in range(HC)]
            gx5 = alloc("gx5", [128, HC, TT], F32)
            b_gx5 = K.buf("gx5")
            bt5 = alloc("bt5", [128, HC, TT], F32)
            b_bt5 = K.buf("bt5")
            K.op(K.pool, lambda e: e.memset(xbuf[0][:, :, 0:3], 0.0), writes=[b_xbuf[0]])
            psrot = [0]

            def nextps():
                k = psrot[0] % 6
                psrot[0] += 1
                return self.ps[k], self.b_ps[k]

            def loadA(T):
                K.dma(K.sp, ht[T % 3][:], hin_v[:, :, T * TT:(T + 1) * TT], b_in, b_ht[T % 3], b_ht[T % 3])

            def stageA(T):
                i = T % 2
                self.norm_stage(TT, ht[T % 3], b_ht[T % 3], sq, b_sq, self.ps[6], self.b_ps[6], std, b_std, rstd, b_rstd,
                                ub[i], b_ub[i], g_sb, b_g)

            def stageB(T, part):
                i = T % 2
                for f in range(part * NC, (part + 1) * NC):
                    pst, b_pst = nextps()
                    for c in range(8):
                        K.op(K.pe, lambda e: e.matmul(pst[:, 0:TT], lhsT=win[:, c, f * 128:(f + 1) * 128], rhs=ub[i][:, c, :],
                                                      start=(c == 0), stop=(c == 7)),
                             reads=[b_win, b_ub[i]], writes=[b_pst], inc=(c == 7))
                    if f < NC:
                        K.op(K.act, lambda e: e.activation(out=xbuf[i][:, f, 3:3 + TT], in_=pst[:, 0:TT], func=AF.Copy),
                             reads=[b_pst], writes=[b_xbuf[i]])
                    else:
                        K.op(K.act, lambda e: e.activation(out=sg[i][:, f - NC, :], in_=pst[:, 0:TT], func=AF.Silu),
                             reads=[b_pst], writes=[b_sg[i]])
                if part == 0 and T + 1 < NT:
                    K.op(K.pool, lambda e: e.tensor_copy(out=xbuf[1 - i][:, :, 0:3], in_=xbuf[i][:, :, TT:TT + 3]),
                         reads=[b_xbuf[i]], writes=[b_xbuf[1 - i]])

            def stageC(T):
                i = T % 2
                for c in range(NC):
                    pst, b_pst = nextps()
                    for j in range(4):
                        K.op(K.pe, lambda e: e.matmul(pst[:, 0:TT], lhsT=dg[:, j, c, :], rhs=xbuf[i][:, c, j:j + TT],
                                                      start=(j == 0), stop=(j == 3)),
                             reads=[b_dg, b_xbuf[i]], writes=[b_pst], inc=(j == 3))
                    K.op(K.act, lambda e: e.activation(out=xc[:, c, :], in_=pst[:, 0:TT], func=AF.Identity, bias=vec[:, 4, c:c + 1], scale=1.0),
                         reads=[b_pst, b_vec], writes=[b_xc[c]])
                    K.op(K.dve, lambda e: e.tensor_copy(out=xcb[:, c, :], in_=xc[:, c, :]), reads=[b_xc[c]], writes=[b_xcb[c]])

            def stageD(T, h):
                i = T % 2
                if True:
                    cs = list(range(h * HC, (h + 1) * HC))
                    for k, c in enumerate(cs):
                        psr, b_psr = nextps()
                        K.op(K.pe, lambda e: e.matmul(psr[:, 0:TT], lhsT=wrg[:, c, :], rhs=xcb[:, c, :], start=True, stop=True),
                             reads=[b_wg, b_xcb[c]], writes=[b_psr])
                        psi, b_psi = nextps()
                        K.op(K.pe, lambda e: e.matmul(psi[:, 0:TT], lhsT=wig[:, c, :], rhs=xcb[:, c, :], start=True, stop=True),
                             reads=[b_wg, b_xcb[c]], writes=[b_psi])
                        K.op(K.act, lambda e: e.activation(out=r5[:, k, :], in_=psr[:, 0:TT], func=AF.Sigmoid, bias=vec[:, 5, c:c + 1], scale=1.0),
                             reads=[b_psr, b_vec], writes=[b_r5[k]])
                        K.op(K.act, lambda e: e.activation(out=i5[:, k, :], in_=psi[:, 0:TT], func=AF.Sigmoid, bias=vec[:, 6, c:c + 1], scale=1.0),
                             reads=[b_psi, b_vec], writes=[b_i5[k]])
                    for k, c in enumerate(cs):
                        K.op(K.act, lambda e: e.activation(out=a5[:, k, :], in_=r5[:, k, :], func=AF.Exp, scale=cl[:, 0, c:c + 1]),
                             reads=[b_r5[k], b_cl], writes=[b_a5[k]])
                    cs0, cs1 = cs[0], cs[-1] + 1
                    K.op(K.pool, lambda e: e.tensor_tensor(out=s5[:], in0=a5[:], in1=a5[:], op=ALU.mult),
                         reads=b_a5, writes=b_s5)
                    for k, c in enumerate(cs):
                        K.op(K.act, lambda e: e.activation(out=s5[:, k, :], in_=s5[:, k, :], func=AF.Sqrt, bias=1.0, scale=-1.0),
                             reads=[b_s5[k]], writes=[b_s5[k]])
                    K.op(K.pool, lambda e: e.tensor_tensor(out=gx5[:], in0=i5[:], in1=xc[:, cs0:cs1, :], op=ALU.mult),
                         reads=b_i5 + [b_xc[c] for c in cs], writes=[b_gx5])
                    K.op(K.dve, lambda e: e.tensor_tensor(out=bt5[:], in0=s5[:], in1=gx5[:], op=ALU.mult),
                         reads=b_s5 + [b_gx5], writes=[b_bt5])
                    for k, c in enumerate(cs):
                        init = 0.0 if T == 0 else hs[1 - i][:, c, TT - 1:TT]
                        rd = [b_a5[k], b_bt5] + ([] if T == 0 else [b_hs[1 - i][c]])
                        K.op(K.dve, lambda e: e.tensor_tensor_scan(out=hs[i][:, c, :], data0=a5[:, k, :], data1=bt5[:, k, :], initial=init,
                                                                    op0=ALU.mult, op1=ALU.add),
                             reads=rd, writes=[b_hs[i][c]])
                    K.op(K.pool, lambda e: e.tensor_tensor(out=zt[:, cs0:cs1, :], in0=hs[i][:, cs0:cs1, :], in1=sg[i][:, cs0:cs1, :], op=ALU.mult),
                         reads=[b_hs[i][c] for c in cs] + [b_sg[i]], writes=[b_zt])

            def stageE(T):
                i = T % 2
                for f in range(8):
                    pst, b_pst = nextps()
                    for c in range(NC):
                        K.op(K.pe, lambda e: e.matmul(pst[:, 0:TT], lhsT=wout[:, c, f * 128:(f + 1) * 128], rhs=zt[:, c, :],
                                                      start=(c == 0), stop=(c == NC - 1)),
                             reads=[b_wout, b_zt], writes=[b_pst], inc=(c == NC - 1))
                    K.op(K.dve, lambda e: e.tensor_tensor(out=ht[T % 3][:, f, :], in0=pst[:, 0:TT], in1=ht[T % 3][:, f, :], op=ALU.add),
                         reads=[b_pst, b_ht[T % 3]], writes=[b_ht[T % 3]])
                K.dma(K.sp, hout_v[:, :, T * TT:(T + 1) * TT], ht[T % 3][:], b_ht[T % 3], b_out, b_ht[T % 3])

            loadA(0)
            loadA(1)
            stageA(0)
            stageB(0, 0)
            stageB(0, 1)
            stageC(0)
            for T in range(NT):
                if T + 2 < NT:
                    loadA(T + 2)
                if T + 1 < NT:
                    stageA(T + 1)
                for h in range(2):
                    stageD(T, h)
                    if T + 1 < NT:
                        stageB(T + 1, h)
                if T + 1 < NT:
                    stageC(T + 1)
                stageE(T)

    def conv_layer(self, l, h_in, b_in, h_out, b_out, pre=None):
        K = self.K
        TT = 256
        NT = S // TT
        CW = 31
        p = "l%d_" % l
        hin_v = h_in.rearrange("(c p) t -> p c t", p=128)
        hout_v = h_out.rearrange("(c p) t -> p c t", p=128)
        with K.scope() as alloc:
            if pre is not None:
                win, b_win, wout, b_wout = pre["win"], pre["b_win"], pre["wout"], pre["b_wout"]
            else:
                win, b_win = self.load_w_bf16(alloc, p + "w_in", [D, 3 * D], 8, 3 * D)
                wout, b_wout = self.load_w_bf16(alloc, p + "w_out", [D, D], 8, D)
                self.stage_all()
            g_sb, b_g = self.load_f32(alloc, p + "norm_gT", [128, 8])
            vec, b_vec = self.load_f32(alloc, p + "vec", [128, CW + 3, 8])
            dg = alloc("dg", [128, CW, 8, 128], BF16)
            b_dgd = K.buf("dgd")
            b_dga = K.buf("dga")
            for j in range(CW):
                for c in range(8):
                    if (j * 8 + c) % 3 != 0:
                        K.op(K.dve, lambda e: e.tensor_scalar(out=dg[:, j, c, :], in0=self.ident[:], scalar1=vec[:, j, c:c + 1],
                                                               scalar2=None, op0=ALU.mult),
                             reads=[b_vec, self.b_const], writes=[b_dgd])
                    else:
                        K.op(K.act, lambda e: e.activation(out=dg[:, j, c, :], in_=self.identf[:], func=AF.Copy, scale=vec[:, j, c:c + 1]),
                             reads=[b_vec, self.b_const], writes=[b_dga])
            ht = [alloc("ht%d" % i, [128, 8, TT], F32) for i in range(2)]
            b_ht = [K.buf("ht%d" % i) for i in range(2)]
            sq = alloc("sq", [128, 8, TT], BF16)
            b_sq = K.buf("sq")
            std = alloc("std", [128, TT], F32)
            b_std = K.buf("std")
            rstd = alloc("rstd", [128, TT], F32)
            b_rstd = K.buf("rstd")
            ub = [alloc("ub%d" % i, [128, 8, TT], BF16) for i in range(2)]
            b_ub = [K.buf("ub%d" % i) for i in range(2)]
            H = CW - 1
            ybuf = [alloc("ybuf%d" % i, [128, 8, TT + H], BF16) for i in range(2)]
            b_ybuf = [K.buf("ybuf%d" % i) for i in range(2)]
            sgt = [alloc("sgt%d" % k, [128, TT], F32) for k in range(2)]
            b_sgt = [K.buf("sgt%d" % k) for k in range(2)]
            sg = [alloc("sg%d" % i, [128, 8, TT], BF16) for i in range(2)]
            b_sg = [K.buf("sg%d" % i) for i in range(2)]
            y2 = alloc("y2", [128, 8, TT], F32)
            y2b = alloc("y2b", [128, 8, TT], BF16)
            sq2 = alloc("sq2", [128, 8, TT], BF16)
            b_y2 = [K.buf("y2_%d" % c) for c in range(8)]
            st = {n: alloc("st_" + n, [128, TT], F32) for n in ["mean", "var", "rstd2"]}
            st["msq"] = st["var"]
            st["std2"] = st["var"]
            st["mr"] = st["mean"]
            b_st = {n: K.buf("st_" + n) for n in ["mean", "var", "rstd2"]}
            b_st["msq"] = b_st["var"]
            b_st["std2"] = b_st["var"]
            b_st["mr"] = b_st["mean"]
            ost = [alloc("ost%d" % k, [128, TT], F32) for k in range(2)]
            b_ost = [K.buf("ost%d" % k) for k in range(2)]
            tn = [alloc("tn%d" % k, [128, TT], F32) for k in range(2)]
            b_tn = [K.buf("tn%d" % k) for k in range(2)]
            sn = [alloc("sn%d" % k, [128, TT], F32) for k in range(2)]
            b_sn = [K.buf("sn%d" % k) for k in range(2)]
            zt = alloc("zt", [128, 8, TT], BF16)
            b_zt = K.buf("zt")
            K.op(K.pool, lambda e: e.memset(ybuf[0][:, :, 0:H], 0.0), writes=[b_ybuf[0]])
            psrot = [0]

            def nextps():
                k = psrot[0] % 5
                psrot[0] += 1
                return self.ps[k], self.b_ps[k]

            def loadA(T):
                i = T % 2
                K.dma(K.sp, ht[i][:], hin_v[:, :, T * TT:(T + 1) * TT], b_in, b_ht[i], b_ht[i])

            def stageA(T):
                i = T % 2
                self.norm_stage(TT, ht[i], b_ht[i], sq, b_sq, self.ps[6], self.b_ps[6], std, b_std, rstd, b_rstd,
                                ub[i], b_ub[i], g_sb, b_g)

            def proj(i, f):
                pst, b_pst = nextps()
                for c in range(8):
                    K.op(K.pe, lambda e: e.matmul(pst[:, 0:TT], lhsT=win[:, c, f * 128:(f + 1) * 128], rhs=ub[i][:, c, :],
                                                  start=(c == 0), stop=(c == 7)),
                         reads=[b_win, b_ub[i]], writes=[b_pst], inc=(c == 7))
                return pst, b_pst

            def stageB(T, f):
                i = T % 2
                if True:
                    k = f % 2
                    psb_, b_psb_ = proj(i, 8 + f)
                    K.op(K.act, lambda e: e.activation(out=sgt[k][:], in_=psb_[:, 0:TT], func=AF.Sigmoid),
                         reads=[b_psb_], writes=[b_sgt[k]])
                    psa, b_psa = proj(i, f)
                    K.op(K.dve, lambda e: e.tensor_tensor(out=ybuf[i][:, f, H:H + TT], in0=psa[:, 0:TT], in1=sgt[k][:], op=ALU.mult),
                         reads=[b_psa, b_sgt[k]], writes=[b_ybuf[i]])
                    psg, b_psg = proj(i, 16 + f)
                    K.op(K.act, lambda e: e.activation(out=sgt[1 - k][:], in_=psg[:, 0:TT], func=AF.Sigmoid),
                         reads=[b_psg], writes=[b_sgt[1 - k]])
                    K.op(K.dve, lambda e: e.tensor_tensor(out=sg[i][:, f, :], in0=psg[:, 0:TT], in1=sgt[1 - k][:], op=ALU.mult),
                         reads=[b_psg, b_sgt[1 - k]], writes=[b_sg[i]])
                if f == 7 and T + 1 < NT:
                    K.op(K.pool, lambda e: e.tensor_copy(out=ybuf[1 - i][:, :, 0:H], in_=ybuf[i][:, :, TT:TT + H]),
                         reads=[b_ybuf[i]], writes=[b_ybuf[1 - i]])

            def stageC(T, chunks):
                i = T % 2
                for c in chunks:
                    pst, b_pst = nextps()
                    for j in range(CW):
                        K.op(K.pe, lambda e: e.matmul(pst[:, 0:TT], lhsT=dg[:, j, c, :], rhs=ybuf[i][:, c, j:j + TT],
                                                      start=(j == 0), stop=(j == CW - 1)),
                             reads=[b_dgd, b_dga, b_ybuf[i]], writes=[b_pst], inc=(j == CW - 1))
                    K.op(K.act, lambda e: e.activation(out=y2[:, c, :], in_=pst[:, 0:TT], func=AF.Identity, bias=vec[:, CW, c:c + 1], scale=1.0),
                         reads=[b_pst, b_vec], writes=[b_y2[c]])
                    K.op(K.act, lambda e: e.activation(out=sq2[:, c, :], in_=pst[:, 0:TT], func=AF.Square, bias=vec[:, CW, c:c + 1], scale=1.0),
                         reads=[b_pst, b_vec], writes=[b_y2[c]])
                    K.op(K.pool, lambda e: e.tensor_copy(out=y2b[:, c, :], in_=y2[:, c, :]), reads=[b_y2[c]], writes=[b_y2[c]])

            def stageD(T):
                i = T % 2
                ps1, b_ps1 = self.ps[5], self.b_ps[5]
                ps2, b_ps2 = self.ps[6], self.b_ps[6]
                for c in range(8):
                    K.op(K.pe, lambda e: e.matmul(ps1[:, 0:TT], lhsT=self.ones[:], rhs=y2b[:, c, :], start=(c == 0), stop=(c == 7)),
                         reads=[self.b_const, b_y2[c]], writes=[b_ps1], inc=(c == 7))
                for c in range(8):
                    K.op(K.pe, lambda e: e.matmul(ps2[:, 0:TT], lhsT=self.ones[:], rhs=sq2[:, c, :], start=(c == 0), stop=(c == 7)),
                         reads=[self.b_const, b_y2[c]], writes=[b_ps2], inc=(c == 7))
                K.op(K.dve, lambda e: e.tensor_scalar(out=st["mean"][:], in0=ps1[:, 0:TT], scalar1=1.0 / D, scalar2=None, op0=ALU.mult),
                     reads=[b_ps1], writes=[b_st["mean"]])
                K.op(K.dve, lambda e: e.tensor_tensor(out=st["msq"][:], in0=st["mean"][:], in1=st["mean"][:], op=ALU.mult),
                     reads=[b_st["mean"]], writes=[b_st["msq"]])
                K.op(K.dve, lambda e: e.scalar_tensor_tensor(out=st["var"][:], in0=ps2[:, 0:TT], scalar=1.0 / D, in1=st["msq"][:],
                                                              op0=ALU.mult, op1=ALU.subtract),
                     reads=[b_ps2, b_st["msq"]], writes=[b_st["var"]])
                K.op(K.dve, lambda e: e.tensor_scalar(out=st["var"][:], in0=st["var"][:], scalar1=0.0, scalar2=None, op0=ALU.max),
                     reads=[b_st["var"]], writes=[b_st["var"]])
                K.op(K.act, lambda e: e.activation(out=st["std2"][:], in_=st["var"][:], func=AF.Sqrt, bias=EPS, scale=1.0),
                     reads=[b_st["var"]], writes=[b_st["std2"]])
                K.op(K.dve, lambda e: e.reciprocal(out=st["rstd2"][:], in_=st["std2"][:]), reads=[b_st["std2"]], writes=[b_st["rstd2"]])
                K.op(K.dve, lambda e: e.tensor_tensor(out=st["mr"][:], in0=st["mean"][:], in1=st["rstd2"][:], op=ALU.mult),
                     reads=[b_st["mean"], b_st["rstd2"]], writes=[b_st["mr"]])

            def stageDn(T, c):
                i = T % 2
                if True:
                    k = c % 2
                    K.op(K.dve, lambda e: e.tensor_tensor(out=tn[k][:], in0=y2[:, c, :], in1=st["rstd2"][:], op=ALU.mult),
                         reads=[b_y2[c], b_st["rstd2"]], writes=[b_tn[k]])
                    K.op(K.dve, lambda e: e.tensor_tensor(out=tn[k][:], in0=tn[k][:], in1=st["mr"][:], op=ALU.subtract),
                         reads=[b_tn[k], b_st["mr"]], writes=[b_tn[k]])
                    K.op(K.act, lambda e: e.activation(out=sn[k][:], in_=tn[k][:], func=AF.Sigmoid, bias=vec[:, CW + 2, c:c + 1],
                                                       scale=vec[:, CW + 1, c:c + 1]),
                         reads=[b_tn[k], b_vec], writes=[b_sn[k]])
                    K.op(K.dve, lambda e: e.tensor_scalar(out=tn[k][:], in0=tn[k][:], scalar1=vec[:, CW + 1, c:c + 1],
                                                           scalar2=vec[:, CW + 2, c:c + 1], op0=ALU.mult, op1=ALU.add),
                         reads=[b_tn[k], b_vec], writes=[b_tn[k]])
                    K.op(K.pool, lambda e: e.tensor_tensor(out=sn[k][:], in0=sn[k][:], in1=tn[k][:], op=ALU.mult),
                         reads=[b_sn[k], b_tn[k]], writes=[b_sn[k]])
                    K.op(K.pool, lambda e: e.tensor_tensor(out=zt[:, c, :], in0=sn[k][:], in1=sg[i][:, c, :], op=ALU.mult),
                         reads=[b_sn[k], b_sg[i]], writes=[b_zt])

            def stageE(T):
                i = T % 2
                for f in range(8):
                    pst, b_pst = nextps()
                    for c in range(8):
                        K.op(K.pe, lambda e: e.matmul(pst[:, 0:TT], lhsT=wout[:, c, f * 128:(f + 1) * 128], rhs=zt[:, c, :],
                                                      start=(c == 0), stop=(c == 7)),
                             reads=[b_wout, b_zt], writes=[b_pst], inc=(c == 7))
                    k2 = f % 2
                    K.op(K.dve, lambda e: e.tensor_tensor(out=ost[k2][:], in0=pst[:, 0:TT], in1=ht[i][:, f, :], op=ALU.add),
                         reads=[b_pst, b_ht[i]], writes=[b_ost[k2]])
                    K.dma(K.sp, hout_v[:, f, T * TT:(T + 1) * TT], ost[k2][:], b_ost[k2], b_out, b_ost[k2])

            loadA(0)
            stageA(0)
            for f in range(8):
                stageB(0, f)
            for T in range(NT):
                if T + 1 < NT:
                    loadA(T + 1)
                stageC(T, range(0, 4) if T == 0 else range(2, 4))
                if T + 1 < NT:
                    stageA(T + 1)
                stageC(T, range(4, 8))
                stageD(T)
                for f in range(8):
                    if T + 1 < NT:
                        stageB(T + 1, f)
                    stageDn(T, f)
                if T + 1 < NT:
                    stageC(T + 1, range(0, 2))
                stageE(T)

    def attn_layer(self, l, h_in, b_in, h_out, b_out, fuse_final=False, prefetch_conv=False):
        K = self.K
        nc = self.nc
        TT = 512
        NT = S // TT
        p = "l%d_" % l
        hin_v = h_in.rearrange("(c p) t -> p c t", p=128)
        hout_v = h_out.rearrange("(c p) t -> p c t", p=128)
        if not hasattr(self, "qT_d"):
            self.qT_d = nc.dram_tensor("qT_scr", [D, S], BF16, kind="Internal").ap()
            self.kT_d = nc.dram_tensor("kT_scr", [D, S], BF16, kind="Internal").ap()
            self.sg_d = nc.dram_tensor("sg_scr", [D, S], BF16, kind="Internal").ap()
            self.v_d = nc.dram_tensor("v_scr", [S, D], BF16, kind="Internal").ap()
            self.z_d = nc.dram_tensor("z_scr", [D, S], BF16, kind="Internal").ap()
            self.b_scr = Buf("qkv_scr")
            self.b_zd = Buf("z_scr")
        qT_d, kT_d, sg_d, v_d, z_d = self.qT_d, self.kT_d, self.sg_d, self.v_d, self.z_d
        b_scr, b_zd = self.b_scr, self.b_zd
        ksum = nc.alloc_sbuf_tensor(p + "ksum", [128, 8, NBLK], F32)
        b_ksum = Buf(p + "ksum")

        b_mc = Buf("maskc")

        def alloc_masks(malloc):
            self.cm = malloc("cm", [128, 4, 512], BF16)
            self.cneg = malloc("cneg", [128, 512], F32)
            self.ownb = malloc("ownb", [128, 512], F32)
            self.ind = malloc("ind", [128, 16, 128], BF16)
            K.op(K.pool, lambda e: e.memset(self.ind[:].rearrange("p a b -> p (a b)"), 0.0), writes=[b_mc])
            b_cd = Buf("consts_d")
            b_ms = K.buf("masksem")
            self.pending.append((self.cm[:].rearrange("p a b -> p (a b)"), self.consts_d[:, 128:2176], b_mc, 128, 2048))
            self.pending.append((self.ind[0:16].rearrange("p a b -> p (a b)"), self.consts_d[0:16, 3200:5248], b_mc, 16, 2048))
            K.dma(K.sp, self.cneg[:], self.consts_d[:, 2176:2688], b_cd, b_mc, b_ms)
            K.dma(K.sp, self.ownb[:], self.consts_d[:, 2688:3200], b_cd, b_mc, b_ms)

        mcm = None
        if not prefetch_conv:
            mcm = K.scope()
            alloc_masks(mcm.__enter__())

        with K.scope() as alloc:
            win, b_win = self.load_w_bf16(alloc, p + "w_in", [D, 4 * D], 8, 4 * D)
            self.stage_all()
            g_sb, b_g = self.load_f32(alloc, p + "norm_gT", [128, 8])
            ht = [alloc("ht%d" % i, [128, 8, TT], F32) for i in range(2)]
            b_ht = [K.buf("ht%d" % i) for i in range(2)]
            sq = alloc("sq", [128, 8, TT], BF16)
            b_sq = K.buf("sq")
            std = alloc("std", [128, TT], F32)
            b_std = K.buf("std")
            rstd = alloc("rstd", [128, TT], F32)
            b_rstd = K.buf("rstd")
            ub = [alloc("ub%d" % i, [128, 8, TT], BF16) for i in range(2)]
            b_ub = [K.buf("ub%d" % i) for i in range(2)]
            qo = alloc("qo", [128, 8, TT], BF16)
            ko = alloc("ko", [128, 8, TT], BF16)
            go = alloc("go", [128, 8, TT], BF16)
            vo = alloc("vo", [128, 4, D], BF16)
            b_qo = [K.buf("qo%d" % k) for k in range(8)]
            b_ko = [K.buf("ko%d" % k) for k in range(8)]
            b_go = [K.buf("go%d" % k) for k in range(8)]
            b_vo = [K.buf("vo%d" % k) for k in range(8)]
            b_qs, b_ks, b_gs_, b_vs = K.buf("qs"), K.buf("ks"), K.buf("gs"), K.buf("vs")
            psrot = [0]

            def nextps():
                k = psrot[0] % 6
                psrot[0] += 1
                return self.ps[k], self.b_ps[k]

            def loadA(T):
                i = T % 2
                K.dma(K.sp, ht[i][:], hin_v[:, :, T * TT:(T + 1) * TT], b_in, b_ht[i], b_ht[i])

            def stageA(T):
                i = T % 2
                self.norm_stage(TT, ht[i], b_ht[i], sq, b_sq, self.ps[6], self.b_ps[6], std, b_std, rstd, b_rstd,
                                ub[i], b_ub[i], g_sb, b_g)

            def proj(i, col0):
                pst, b_pst = nextps()
                for c in range(8):
                    K.op(K.pe, lambda e: e.matmul(pst[:], lhsT=win[:, c, col0:col0 + 128], rhs=ub[i][:, c, :],
                                                  start=(c == 0), stop=(c == 7)),
                         reads=[b_win, b_ub[i]], writes=[b_pst], inc=(c == 7))
                return pst, b_pst

            def stageB(T):
                i = T % 2
                tsl = slice(T * TT, (T + 1) * TT)
                if T + 1 < NT:
                    loadA(T + 1)
                for hd in range(8):
                    pst, b_pst = proj(i, hd * 128)
                    K.op(K.act, lambda e: e.activation(out=qo[:, hd, :], in_=pst[:], func=AF.Copy, scale=128.0 ** -0.5),
                         reads=[b_pst], writes=[b_qo[hd]])
                K.dma(K.sp, qT_d.rearrange("(c p) t -> p c t", p=128)[:, :, tsl], qo[:], b_qo, b_scr, b_qs)
                for hd in range(8):
                    pst, b_pst = proj(i, D + hd * 128)
                    for hf in range(2):
                        K.op(K.act, lambda e: e.activation(out=ko[:, hd, hf * BLK:(hf + 1) * BLK], in_=pst[:, hf * BLK:(hf + 1) * BLK],
                                                           func=AF.Identity, accum_out=ksum[:, hd, 2 * T + hf:2 * T + hf + 1]),
                             reads=[b_pst], writes=[b_ko[hd], b_ksum])
                K.dma(K.sp, kT_d.rearrange("(c p) t -> p c t", p=128)[:, :, tsl], ko[:], b_ko, b_scr, b_ks)
                if T + 1 < NT:
                    stageA(T + 1)
                for s4 in range(4):
                    for hf in range(2):
                        pst, b_pst = nextps()
                        for c in range(8):
                            K.op(K.pe, lambda e: e.matmul(pst[:], lhsT=ub[i][:, c, s4 * 128:(s4 + 1) * 128],
                                                          rhs=win[:, c, 2 * D + hf * 512:2 * D + (hf + 1) * 512],
                                                          start=(c == 0), stop=(c == 7)),
                                 reads=[b_win, b_ub[i]], writes=[b_pst], inc=(c == 7))
                        K.op(K.dve, lambda e: e.tensor_copy(out=vo[:, s4, hf * 512:(hf + 1) * 512], in_=pst[:]),
                             reads=[b_pst], writes=[b_vo[s4 * 2 + hf]])
                K.dma(K.sp, v_d[tsl, :].rearrange("(s p) n -> p s n", p=128), vo[:], b_vo, b_scr, b_vs)
                for hd in range(8):
                    pst, b_pst = proj(i, 3 * D + hd * 128)
                    K.op(K.act, lambda e: e.activation(out=go[:, hd, :], in_=pst[:], func=AF.Silu),
                         reads=[b_pst], writes=[b_go[hd]])
                K.dma(K.sp, sg_d.rearrange("(c p) t -> p c t", p=128)[:, :, tsl], go[:], b_go, b_scr, b_gs_)

            loadA(0)
            stageA(0)
            for T in range(NT):
                stageB(T)

        carry = None
        if prefetch_conv:
            ccm = K.scope()
            calloc = ccm.__enter__()
            cwin, cb_win = self.bg_register("l1_w_in", [D, 3 * D], 8, 3 * D, calloc)
            cwout, cb_wout = self.bg_register("l1_w_out", [D, D], 8, D, calloc)
            carry = {"cm": ccm, "win": cwin, "b_win": cb_win, "wout": cwout, "b_wout": cb_wout}
        pcm = K.scope()
        palloc = pcm.__enter__()
        if not prefetch_conv:
            wout, b_wout = self.bg_register(p + "w_out", [D, D], 8, D, palloc)
        if prefetch_conv:
            alloc_masks(palloc)
            self.stage_all()
        self.bg_stg = [palloc("bgstg%d" % k, [128, 1024], F32) for k in range(2)]
        self.b_bg_stg = [K.buf("bgstg%d" % k) for k in range(2)]
        n_bg = len(self.bg_pieces)
        bg_every = max(1, 1000 // max(1, n_bg))

        with K.scope() as alloc:
            qh = [alloc("qh%d" % i, [128, S], BF16) for i in range(2)]
            kh = [alloc("kh%d" % i, [128, S], BF16) for i in range(2)]
            sgh = [alloc("sgh%d" % i, [128, S], BF16) for i in range(2)]
            vh = [alloc("vh%d" % i, [128, 32, 128], BF16) for i in range(2)]
            b_hd = [K.buf("hd%d" % i) for i in range(2)]
            b_hq = [K.buf("hq%d" % i) for i in range(2)]
            kmT = alloc("kmT", [128, 8, NBLK], BF16)
            b_kmT = K.buf("kmT")
            K.op(K.dve, lambda e: e.tensor_scalar(out=kmT[:], in0=ksum[:], scalar1=1.0 / BLK, scalar2=None, op0=ALU.mult),
                 reads=[b_ksum], writes=[b_kmT])
            ownbm = alloc("ownbm", [128, 512], F32)
            b_ownbm = K.buf("ownbm")
            K.op(K.dve, lambda e: e.tensor_scalar(out=ownbm[:], in0=self.ownb[:], scalar1=-BIG, scalar2=None, op0=ALU.add),
                 reads=[b_mc], writes=[b_ownbm])
            gsm = alloc("gsm", [128, 32, NBLK], F32)
            top8 = alloc("top8", [128, 32, 8], F32)
            thr = alloc("thr", [128, 32], F32)
            tmpm = alloc("tmpm", [128, 32, NBLK], F32)
            negsel = alloc("negsel", [128, 32, 128], F32)
            b_gate = K.buf("gate")
            b_negsel = K.buf("negsel")
            K.op(K.pool, lambda e: e.memset(negsel[:].rearrange("p a b -> p (a b)"), 0.0), writes=[b_negsel])
            nselT = [alloc("nselT%d" % i, [128, S], BF16) for i in range(2)]
            b_nselT = [K.buf("nselT%d" % i) for i in range(2)]
            NP = 8
            pT = [alloc("pT%d" % k, [128, TT], BF16) for k in range(NP)]
            b_pT = [K.buf("pT%d" % k) for k in range(NP)]
            rden = alloc("rden", [128, TT], F32)
            b_rden = K.buf("rden")
            acc = [alloc("acc%d" % a, [128, TT], F32) for a in range(2)]
            b_acc = [K.buf("acc%d" % a) for a in range(2)]
            onesf = alloc("onesf", [128, 128], F32)
            b_onesf = K.buf("onesf")
            K.op(K.pool, lambda e: e.memset(onesf[:], 1.0), writes=[b_onesf])
            ot = alloc("ot", [128, TT], F32)
            b_ot = K.buf("ot")
            zo = [alloc("zo%d" % k, [128, TT], BF16) for k in range(2)]
            b_zo = [K.buf("zo%d" % k) for k in range(2)]
            z_v = z_d.rearrange("(c p) t -> c p t", p=128)
            pg, b_pg = self.pg, self.b_pg

            def load_head(hd):
                i = hd % 2
                hs_ = slice(hd * 128, (hd + 1) * 128)
                K.dma(K.sp, qh[i][:], qT_d[hs_, :], b_scr, b_hq[i], b_hq[i])
                K.dma(K.sp, kh[i][:], kT_d[hs_, :], b_scr, b_hd[i], b_hd[i])
                K.dma(K.sp, sgh[i][:], sg_d[hs_, :], b_scr, b_hd[i], b_hd[i])
                vv = v_d[:, hs_].rearrange("(s p) n -> p s n", p=128)
                for s8 in range(4):
                    K.dma(K.sp, vh[i][:, s8 * 8:(s8 + 1) * 8, :], vv[:, s8 * 8:(s8 + 1) * 8, :], b_scr, b_hd[i], b_hd[i])

            def gating1(hd):
                i = hd % 2
                for j in range(32):
                    K.op(K.pe, lambda e: e.matmul(pg[:, j * NBLK:(j + 1) * NBLK], lhsT=qh[i][:, j * 128:(j + 1) * 128], rhs=kmT[:, hd, :],
                                                  start=True, stop=True),
                         reads=[b_hq[i], b_kmT], writes=[b_pg], inc=(j == 31))
                gsm2 = gsm[:].rearrange("p a b -> p (a b)")
                K.op(K.dve, lambda e: e.tensor_tensor(out=gsm2, in0=pg[:], in1=self.cneg[:], op=ALU.add),
                     reads=[b_pg, b_mc], writes=[b_gate])
                for j in range(32):
                    K.op(K.dve, lambda e: e.max(out=top8[:, j, :], in_=gsm[:, j, :]), reads=[b_gate], writes=[b_gate], inc=(j == 31))
                K.op(K.dve, lambda e: e.tensor_scalar(out=thr[:], in0=top8[:, :, 2], scalar1=-1e29, scalar2=None, op0=ALU.max),
                     reads=[b_gate], writes=[b_gate])
                K.op(K.dve, lambda e: e.tensor_tensor(out=tmpm[:], in0=gsm[:], in1=thr[:, :, None].broadcast_to([128, 32, NBLK]),
                                                      op=ALU.is_ge),
                     reads=[b_gate], writes=[b_gate])
                K.op(K.dve, lambda e: e.scalar_tensor_tensor(out=negsel[:, :, 0:NBLK], in0=tmpm[:], scalar=BIG,
                                                              in1=ownbm[:].rearrange("p (a b) -> p a b", b=NBLK),
                                                              op0=ALU.mult, op1=ALU.add),
                     reads=[b_gate, b_ownbm], writes=[b_negsel])

            def gating2(hd, groups=range(8)):
                i = hd % 2
                for g8 in groups:
                    for jj in range(4):
                        j = g8 * 4 + jj
                        K.op(K.pe, lambda e: e.transpose(out=pg[:, jj * 128:(jj + 1) * 128], in_=negsel[:, j, :], identity=self.identf[:]),
                             reads=[b_negsel, self.b_const], writes=[b_pg], inc=(jj == 3))
                    K.op(K.dve, lambda e: e.tensor_copy(out=nselT[i][:, g8 * 512:(g8 + 1) * 512], in_=pg[:]),
                         reads=[b_pg], writes=[b_nselT[i]])

            steps = [(hd, T, kt) for hd in range(8) for T in range(NT) for kt in range(4 * (T + 1))]
            NS = len(steps)

            def obanks(T):
                return (self.ps[3 + (T % 2) * 2], self.b_ps[3 + (T % 2) * 2], self.ps[4 + (T % 2) * 2], self.b_ps[4 + (T % 2) * 2])

            def cols(T, kt):
                return 256 if (HALF_DIAG and kt >= 4 * T + 2) else 0

            def s_mm(g):
                hd, T, kt = steps[g]
                i = hd % 2
                c0 = cols(T, kt)
                qsl = slice(T * TT + c0, (T + 1) * TT)
                k3 = g % 3
                sp_, b_sp = self.ps[k3], self.b_ps[k3]
                diag = kt >= 4 * T
                K.op(K.pe, lambda e: e.matmul(sp_[:, c0:TT], lhsT=kh[i][:, kt * 128:(kt + 1) * 128], rhs=qh[i][:, qsl], start=True, stop=False),
                     reads=[b_hd[i], b_hq[i]], writes=[b_sp], inc=False)
                K.op(K.pe, lambda e: e.matmul(sp_[:, c0:TT], lhsT=self.ind[:, kt // 2, :], rhs=nselT[i][:, qsl], start=False, stop=not diag),
                     reads=[b_mc, b_nselT[i]], writes=[b_sp], inc=not diag)
                if diag:
                    K.op(K.pe, lambda e: e.matmul(sp_[:, c0:TT], lhsT=self.ident[:], rhs=self.cm[:, kt - 4 * T, c0:TT], start=False, stop=True),
                         reads=[self.b_const, b_mc], writes=[b_sp], inc=True)

            def pv_mm(g):
                hd, T, kt = steps[g]
                i = hd % 2
                nk = 4 * (T + 1)
                c0 = cols(T, kt)
                k3 = g % 3
                kp = g % NP
                sp_, b_sp = self.ps[k3], self.b_ps[k3]
                ops_, b_ops, dps, b_dps = obanks(T)
                K.op(K.act, lambda e: e.activation(out=pT[kp][:, c0:TT], in_=sp_[:, c0:TT], func=AF.Exp), reads=[b_sp], writes=[b_pT[kp]])
                K.op(K.pe, lambda e: e.matmul(ops_[:, c0:TT], lhsT=vh[i][:, kt, :], rhs=pT[kp][:, c0:TT], start=(kt == 0), stop=(kt == nk - 1)),
                     reads=[b_hd[i], b_pT[kp]], writes=[b_ops], inc=True)
                if kt % 2 == 1:
                    K.op(K.pe, lambda e: e.matmul(dps[:, c0:TT], lhsT=self.ones[:], rhs=pT[kp][:, c0:TT], start=(kt == 1), stop=False),
                         reads=[self.b_const, b_pT[kp]], writes=[b_dps], inc=True)
                else:
                    ac, b_ac = acc[T % 2], b_acc[T % 2]
                    if kt == 0:
                        K.op(K.dve, lambda e: e.tensor_copy(out=ac[:], in_=pT[kp][:]), reads=[b_pT[kp]], writes=[b_ac])
                    else:
                        K.op(K.dve, lambda e: e.tensor_tensor(out=ac[:, c0:TT], in0=ac[:, c0:TT], in1=pT[kp][:, c0:TT], op=ALU.add),
                             reads=[b_pT[kp], b_ac], writes=[b_ac])

            deferred = []

            def finalize(hd, T, g):
                i = hd % 2
                qsl = slice(T * TT, (T + 1) * TT)
                ops_, b_ops, dps, b_dps = obanks(T)
                zi = T % 2
                K.op(K.pe, lambda e: e.matmul(dps[:], lhsT=onesf[:], rhs=acc[T % 2][:], start=False, stop=True),
                     reads=[b_onesf, b_acc[T % 2]], writes=[b_dps], inc=True)

                def quarter(q):
                    cs = slice(q * 128, (q + 1) * 128)
                    K.op(K.dve, lambda e: e.reciprocal(out=rden[:, cs], in_=dps[:, cs]), reads=[b_dps], writes=[b_rden])
                    K.op(K.dve, lambda e: e.tensor_tensor(out=ot[:, cs], in0=ops_[:, cs], in1=rden[:, cs], op=ALU.mult),
                         reads=[b_ops, b_rden], writes=[b_ot])
                    if q == 3:
                        K.op(K.pool, lambda e: e.tensor_tensor(out=zo[zi][:], in0=ot[:], in1=sgh[i][:, qsl], op=ALU.mult),
                             reads=[b_ot, b_hd[i]], writes=[b_zo[zi]])
                        K.dma(K.sp, z_v[hd, :, qsl], zo[zi][:], b_zo[zi], b_zd, b_zo[zi])

                for q in range(4):
                    deferred.append((g + 1 + q, lambda q=q: quarter(q)))
                deferred.sort(key=lambda d: d[0])

            def run_deferred(g):
                while deferred and deferred[0][0] <= g:
                    deferred.pop(0)[1]()

            load_head(0)
            gating1(0)
            gating2(0)
            s_mm(0)
            s_mm(1)
            for g in range(NS):
                hd, T, kt = steps[g]
                if T == 1 and kt == 0 and hd + 1 < 8:
                    load_head(hd + 1)
                if g + 2 < NS:
                    s_mm(g + 2)
                pv_mm(g)
                run_deferred(g)
                if g >= 10 and (g - 10) % bg_every == 0:
                    self.bg_step()
                if kt == 4 * (T + 1) - 1:
                    finalize(hd, T, g)
                    if hd + 1 < 8 and T == 3:
                        gating1(hd + 1)
                    if hd + 1 < 8 and T == 5:
                        for g8 in range(8):
                            deferred.append((g + 1 + 2 * g8, lambda g8=g8, hd=hd: gating2(hd + 1, [g8])))
                        deferred.sort(key=lambda d: d[0])
            run_deferred(NS + 10)
            while self.bg_pieces:
                self.bg_step()

        with K.scope() as alloc:
            if prefetch_conv:
                wout, b_wout = self.load_w_bf16(alloc, p + "w_out", [D, D], 8, D)
                self.stage_all()
            ht = [alloc("ht%d" % i, [128, 8, TT], F32) for i in range(2)]
            b_ht = [K.buf("ht%d" % i) for i in range(2)]
            zt = [alloc("zt%d" % i, [128, 8, TT], BF16) for i in range(2)]
            b_zt = [K.buf("zt%d" % i) for i in range(2)]
            zd_v = z_d.rearrange("(c p) t -> p c t", p=128)
            psrot = [0]
            if fuse_final:
                fg_sb, b_fg = self.load_f32(alloc, "final_gT", [128, 8])
                fsq = alloc("fsq", [128, 8, TT], BF16)
                b_fsq = K.buf("fsq")
                fstd = alloc("fstd", [128, TT], F32)
                b_fstd = K.buf("fstd")
                frstd = alloc("frstd", [128, TT], F32)
                b_frstd = K.buf("frstd")

            def load3(T):
                i = T % 2
                K.dma(K.sp, ht[i][:], hin_v[:, :, T * TT:(T + 1) * TT], b_in, b_ht[i], b_ht[i])
                K.dma(K.sp, zt[i][:], zd_v[:, :, T * TT:(T + 1) * TT], b_zd, b_zt[i], b_zt[i])

            load3(0)
            for T in range(NT):
                i = T % 2
                if T + 1 < NT:
                    load3(T + 1)
                for f in range(8):
                    k = psrot[0] % 6
                    psrot[0] += 1
                    pst, b_pst = self.ps[k], self.b_ps[k]
                    for c in range(8):
                        K.op(K.pe, lambda e: e.matmul(pst[:], lhsT=wout[:, c, f * 128:(f + 1) * 128], rhs=zt[i][:, c, :],
                                                      start=(c == 0), stop=(c == 7)),
                             reads=[b_wout, b_zt[i]], writes=[b_pst], inc=(c == 7))
                    K.op(K.dve, lambda e: e.tensor_tensor(out=ht[i][:, f, :], in0=pst[:], in1=ht[i][:, f, :], op=ALU.add),
                         reads=[b_pst, b_ht[i]], writes=[b_ht[i]])
                if fuse_final:
                    self.norm_stage(TT, ht[i], b_ht[i], fsq, b_fsq, self.ps[6], self.b_ps[6], fstd, b_fstd, frstd, b_frstd,
                                    ht[i], b_ht[i], fg_sb, b_fg)
                K.dma(K.sp, hout_v[:, :, T * TT:(T + 1) * TT], ht[i][:], b_ht[i], b_out, b_ht[i])

        pcm.__exit__(None, None, None)
        if mcm is not None:
            mcm.__exit__(None, None, None)
        return carry


def _consts():
    c = np.zeros((128, NCONST), np.float32)
    c[:, 0:128] = np.eye(128, dtype=np.float32)
    r = np.arange(128)[:, None]
    q = np.arange(512)[None, :]
    cm = np.zeros((128, 4, 512), np.float32)
    for a in range(4):
        kpos = 128 * a + r
        blk_k = kpos // BLK
        blk_q = q // BLK
        same = blk_k == blk_q
        cm[:, a, :] = np.where(same & (kpos > q), -BIG, 0.0)
    c[:, 128:2176] = cm.reshape(128, 2048)
    j = np.arange(32)[:, None]
    n = np.arange(NBLK)[None, :]
    cneg = np.where(n < (j // 2), 0.0, -1e30).astype(np.float32)
    ownb = np.where(n == (j // 2), BIG, 0.0).astype(np.float32)
    c[:, 2176:2688] = np.broadcast_to(cneg.reshape(1, 512), (128, 512))
    c[:, 2688:3200] = np.broadcast_to(ownb.reshape(1, 512), (128, 512))
    ind = np.zeros((16, 16, 128), np.float32)
    for nn in range(16):
        ind[nn, nn, :] = 1.0
    c[0:16, 3200:5248] = ind.reshape(16, 2048)
    return c


def _col8(v):
    return np.ascontiguousarray(np.asarray(v, np.float32).reshape(-1, 128).T)


def _layer_inputs(l, inp):
    p = "l%d_" % l
    d = {}
    d[p + "norm_gT"] = _col8(inp[p + "norm_g"])
    d[p + "w_in"] = np.ascontiguousarray(inp[p + "w_in"], dtype=np.float32)
    d[p + "w_out"] = np.ascontiguousarray(inp[p + "w_out"], dtype=np.float32)
    if l == 1:
        cw = np.asarray(inp[p + "conv_w"], np.float32).reshape(31, 8, 128).transpose(2, 0, 1)
        rest = np.stack([_col8(inp[p + k]) for k in ("conv_b", "ln_g", "ln_b")], axis=1)
        d[p + "vec"] = np.ascontiguousarray(np.concatenate([cw, rest], axis=1))
    if l == 2:
        cw = np.asarray(inp[p + "conv_w"], np.float32).reshape(4, 10, 128).transpose(2, 0, 1)
        rest = np.stack([_col8(inp[p + k]) for k in ("conv_b", "b_rg", "b_ig", "lam")], axis=1)
        d[p + "vec"] = np.ascontiguousarray(np.concatenate([cw, rest], axis=1))
        d[p + "w_rg"] = np.ascontiguousarray(inp[p + "w_rg"], dtype=np.float32)
        d[p + "w_ig"] = np.ascontiguousarray(inp[p + "w_ig"], dtype=np.float32)
    return d


def run_layers(layers, final_norm, hT_list, inp):
    prog = Prog(layers, final_norm, first=True)
    shared = {"consts": _consts()}
    for l in layers:
        shared.update(_layer_inputs(l, inp))
    if final_norm:
        shared["final_gT"] = _col8(inp["final_g"])
    in_maps = []
    for b in range(len(hT_list)):
        m = dict(shared)
        m["xT"] = hT_list[b]
        in_maps.append(m)
    res = run_bass_kernel_spmd(prog.nc, in_maps, core_ids=list(range(len(hT_list))))
    return [r["oT"] for r in res.results]


def kernel(**inputs):
    inp = {k: np.asarray(inputs[k]) for k in ALL_INPUTS}
    x = inp["x"]
    hT = [np.ascontiguousarray(x[b].T, dtype=np.float32) for b in range(NCORES)]
    if MODE == "fused":
        hT = run_layers([0, 1, 2, 3], True, hT, inp)
    else:
        for l in range(4):
            hT = run_layers([l], l == 3, hT, inp)
    return np.ascontiguousarray(np.stack([h.T for h in hT], axis=0)).astype(np.float32)
```
